# Optimizing a Trainium2 kernel written in Bass

```python
import math
import jax, jax.numpy as jnp
from jax import lax
import numpy as np

D_MODEL = 1024
BATCH = 2
SEQ = 8192
DEPTH = 2

N_EVEN = (DEPTH + 1) // 2
N_ODD = DEPTH // 2
NORM_EPS = 1e-6
Q_BLOCK = 128
D_FF = 2816
S5_WIDTH = D_MODEL // 2
S5_GROUP_CH = 16
S5_GROUPS = S5_WIDTH // S5_GROUP_CH
S5_STATE = 64
S5_DT_MIN = 1e-3
S5_DT_MAX = 1e-1
MLA_HEADS = 8
MLA_NOPE = 64
MLA_ROPE = 32
MLA_V = 64
MLA_Q_LORA = 384
MLA_KV_LORA = 256
MLA_ROPE_THETA = 10000.0
EVEN_IN = S5_WIDTH + MLA_Q_LORA + MLA_KV_LORA + MLA_ROPE
EVEN_MIX = S5_WIDTH + MLA_HEADS * MLA_V
DIFF_HEADS = 8
DIFF_HEAD_DIM = D_MODEL // DIFF_HEADS // 2
DIFF_ROT = DIFF_HEAD_DIM // 4
ROPE_THETA = 500000.0
ODD_IN = 3 * DIFF_HEADS * 2 * DIFF_HEAD_DIM
ODD_MIX = DIFF_HEADS * 2 * DIFF_HEAD_DIM

kernel_name = "hybrid_s5_mla_diffattn_macaron"

F32 = jnp.float32


def _rmsnorm(x, g):
    xf = x.astype(F32)
    y = xf * lax.rsqrt(jnp.mean(xf * xf, axis=-1, keepdims=True) + NORM_EPS)
    return (y * g.astype(F32)).astype(x.dtype)


def _swiglu(h, w_gu, w_down):
    g, u = jnp.split(h @ w_gu, 2, axis=-1)
    return (jax.nn.silu(g) * u) @ w_down


def _rope_tables(positions, rot_dim, theta):
    half = rot_dim // 2
    inv = theta ** (-jnp.arange(half, dtype=F32) * 2.0 / rot_dim)
    ang = positions.astype(F32)[..., None] * inv
    return jnp.cos(ang), jnp.sin(ang)


def _apply_rope(x, cos, sin):
    half = cos.shape[-1]
    shape = (cos.shape[0],) + (1,) * (x.ndim - 3) + cos.shape[1:]
    c = cos.reshape(shape)
    s = sin.reshape(shape)
    xf = x.astype(F32)
    x1 = xf[..., :half]
    x2 = xf[..., half:2 * half]
    out = jnp.concatenate([x1 * c - x2 * s, x2 * c + x1 * s, xf[..., 2 * half:]], axis=-1)
    return out.astype(x.dtype)


def _causal_mask(i, s):
    qpos = i * Q_BLOCK + jnp.arange(Q_BLOCK)
    return jnp.arange(s)[None, :] <= qpos[:, None]


def _causal_attention(q, k, v):
    bsz, nh, s, dk = q.shape
    dv = v.shape[-1]
    nb = s // Q_BLOCK
    scale = dk ** -0.5
    qb = q.reshape(bsz, nh, nb, Q_BLOCK, dk).transpose(2, 0, 1, 3, 4)

    def block(args):
        qi, i = args
        sc = jnp.einsum('bhqd,bhkd->bhqk', qi, k).astype(F32) * scale
        sc = jnp.where(_causal_mask(i, s), sc, -jnp.inf)
        p = jax.nn.softmax(sc, axis=-1)
        return jnp.einsum('bhqk,bhkd->bhqd', p.astype(v.dtype), v)

    o = lax.map(block, (qb, jnp.arange(nb)))
    return o.transpose(1, 2, 0, 3, 4).reshape(bsz, nh, s, dv)


def _diff_attention(q, k, v, lam):
    bsz, nh, _, s, d = q.shape
    dv = v.shape[-1]
    nb = s // Q_BLOCK
    scale = d ** -0.5
    qb = q.reshape(bsz, nh, 2, nb, Q_BLOCK, d).transpose(3, 0, 1, 2, 4, 5)

    def block(args):
        qi, i = args
        sc = jnp.einsum('bhmqd,bhmkd->bhmqk', qi, k).astype(F32) * scale
        sc = jnp.where(_causal_mask(i, s), sc, -jnp.inf)
        p = jax.nn.softmax(sc, axis=-1)
        w = p[:, :, 0] - lam * p[:, :, 1]
        return jnp.einsum('bhqk,bhkd->bhqd', w.astype(v.dtype), v)

    o = lax.map(block, (qb, jnp.arange(nb)))
    return o.transpose(1, 2, 0, 3, 4).reshape(bsz, nh, s, dv)


def _s5(u, lam_re, lam_im, log_dt, b_re, b_im, c_re, c_im, d_skip, w_glu, b_glu):
    bsz, s, _ = u.shape
    uf = u.astype(F32).reshape(bsz, s, S5_GROUPS, S5_GROUP_CH)
    dt = jnp.exp(log_dt.astype(F32))[:, None]
    lr = lam_re.astype(F32)
    li = lam_im.astype(F32)
    mag = jnp.exp(lr * dt)
    ang = li * dt
    ab_re = mag * jnp.cos(ang)
    ab_im = mag * jnp.sin(ang)
    den = lr * lr + li * li
    nr = ab_re - 1.0
    f_re = (nr * lr + ab_im * li) / den
    f_im = (ab_im * lr - nr * li) / den
    br = b_re.astype(F32)
    bi = b_im.astype(F32)
    bb_re = f_re[..., None] * br - f_im[..., None] * bi
    bb_im = f_re[..., None] * bi + f_im[..., None] * br
    bu_re = jnp.einsum('bsgh,gph->sbgp', uf, bb_re)
    bu_im = jnp.einsum('bsgh,gph->sbgp', uf, bb_im)
    a_re = jnp.broadcast_to(ab_re[None, None], (s, 1, S5_GROUPS, S5_STATE))
    a_im = jnp.broadcast_to(ab_im[None, None], (s, 1, S5_GROUPS, S5_STATE))

    def combine(e_i, e_j):
        ar_i, ai_i, xr_i, xi_i = e_i
        ar_j, ai_j, xr_j, xi_j = e_j
        return (ar_j * ar_i - ai_j * ai_i,
                ar_j * ai_i + ai_j * ar_i,
                ar_j * xr_i - ai_j * xi_i + xr_j,
                ar_j * xi_i + ai_j * xr_i + xi_j)

    _, _, x_re, x_im = lax.associative_scan(combine, (a_re, a_im, bu_re, bu_im), axis=0)
    y = (jnp.einsum('sbgp,ghp->bsgh', x_re, c_re.astype(F32))
         - jnp.einsum('sbgp,ghp->bsgh', x_im, c_im.astype(F32))
         + d_skip.astype(F32) * uf)
    g = jax.nn.gelu(y.reshape(bsz, s, S5_WIDTH))
    out = g * jax.nn.sigmoid(g @ w_glu.astype(F32) + b_glu.astype(F32))
    return out.astype(u.dtype)


def _even_mixer(h, cos_m, sin_m, w_in, lam_re, lam_im, log_dt, b_re, b_im, c_re, c_im,
                d_skip, w_glu, b_glu, q_norm, w_uq, kv_norm, w_ukv, w_out):
    bsz, s, _ = h.shape
    proj = h @ w_in
    u, c_q, c_kv, k_rope = jnp.split(
        proj, [S5_WIDTH, S5_WIDTH + MLA_Q_LORA, S5_WIDTH + MLA_Q_LORA + MLA_KV_LORA], axis=-1)
    s5_out = _s5(u, lam_re, lam_im, log_dt, b_re, b_im, c_re, c_im, d_skip, w_glu, b_glu)
    q = (_rmsnorm(c_q, q_norm) @ w_uq).reshape(bsz, s, MLA_HEADS, MLA_NOPE + MLA_ROPE).transpose(0, 2, 1, 3)
    kv = (_rmsnorm(c_kv, kv_norm) @ w_ukv).reshape(bsz, s, MLA_HEADS, MLA_NOPE + MLA_V).transpose(0, 2, 1, 3)
    k_nope, v = kv[..., :MLA_NOPE], kv[..., MLA_NOPE:]
    q_pe = _apply_rope(q[..., MLA_NOPE:], cos_m, sin_m)
    k_pe = _apply_rope(k_rope, cos_m, sin_m)[:, None]
    q_full = jnp.concatenate([q[..., :MLA_NOPE], q_pe], axis=-1)
    k_full = jnp.concatenate([k_nope, jnp.broadcast_to(k_pe, (bsz, MLA_HEADS, s, MLA_ROPE))], axis=-1)
    o = _causal_attention(q_full, k_full, v)
    o = o.transpose(0, 2, 1, 3).reshape(bsz, s, MLA_HEADS * MLA_V)
    return jnp.concatenate([s5_out, o], axis=-1) @ w_out


def _odd_mixer(h, cos_t, sin_t, layer_idx, w_in, lq1, lk1, lq2, lk2, subln, w_out):
    bsz, s, _ = h.shape
    q, k, v = jnp.split(h @ w_in, 3, axis=-1)
    q = q.reshape(bsz, s, DIFF_HEADS, 2, DIFF_HEAD_DIM).transpose(0, 2, 3, 1, 4)
    k = k.reshape(bsz, s, DIFF_HEADS, 2, DIFF_HEAD_DIM).transpose(0, 2, 3, 1, 4)
    v = v.reshape(bsz, s, DIFF_HEADS, 2 * DIFF_HEAD_DIM).transpose(0, 2, 1, 3)
    q = _apply_rope(q, cos_t, sin_t)
    k = _apply_rope(k, cos_t, sin_t)
    lam_init = 0.8 - 0.6 * math.exp(-0.3 * layer_idx)
    lam = (jnp.exp(jnp.sum(lq1.astype(F32) * lk1.astype(F32)))
           - jnp.exp(jnp.sum(lq2.astype(F32) * lk2.astype(F32))) + lam_init)
    o = _diff_attention(q, k, v, lam)
    o = _rmsnorm(o, subln) * (1.0 - lam_init)
    o = o.transpose(0, 2, 1, 3).reshape(bsz, s, ODD_MIX)
    return o @ w_out


def setup_inputs(seed: int = 0) -> dict:
    key = jax.random.key(seed)
    ks = iter(jax.random.split(key, 32))

    def nrm(shape, scale):
        return jax.random.normal(next(ks), shape, F32) * scale

    def gain(shape):
        return 1.0 + nrm(shape, 0.02)

    G, P, H = S5_GROUPS, S5_STATE, S5_GROUP_CH
    x = nrm((BATCH, SEQ, D_MODEL), 1.0)
    positions = jnp.broadcast_to(jnp.arange(SEQ, dtype=jnp.int32)[None, :], (BATCH, SEQ))
    ffn_norm = gain((DEPTH, 2, D_MODEL))
    ffn_w_gu = nrm((DEPTH, 2, D_MODEL, 2 * D_FF), D_MODEL ** -0.5)
    ffn_w_down = nrm((DEPTH, 2, D_FF, D_MODEL), D_FF ** -0.5)
    ev_norm = gain((N_EVEN, D_MODEL))
    ev_w_in = nrm((N_EVEN, D_MODEL, EVEN_IN), D_MODEL ** -0.5)
    s5_lambda_re = -0.5 + nrm((N_EVEN, G, P), 0.01)
    s5_lambda_im = jnp.pi * jnp.arange(P, dtype=F32)[None, None, :] + nrm((N_EVEN, G, P), 0.01)
    s5_log_dt = jax.random.uniform(next(ks), (N_EVEN, G), F32,
                                   minval=math.log(S5_DT_MIN), maxval=math.log(S5_DT_MAX))
    s5_b_re = nrm((N_EVEN, G, P, H), (2 * H) ** -0.5)
    s5_b_im = nrm((N_EVEN, G, P, H), (2 * H) ** -0.5)
    s5_c_re = nrm((N_EVEN, G, H, P), P ** -0.5)
    s5_c_im = nrm((N_EVEN, G, H, P), P ** -0.5)
    s5_d = nrm((N_EVEN, G, H), 1.0)
    s5_w_glu = nrm((N_EVEN, S5_WIDTH, S5_WIDTH), S5_WIDTH ** -0.5)
    s5_b_glu = nrm((N_EVEN, S5_WIDTH), 0.01)
    mla_q_norm = gain((N_EVEN, MLA_Q_LORA))
    mla_w_uq = nrm((N_EVEN, MLA_Q_LORA, MLA_HEADS * (MLA_NOPE + MLA_ROPE)), MLA_Q_LORA ** -0.5)
    mla_kv_norm = gain((N_EVEN, MLA_KV_LORA))
    mla_w_ukv = nrm((N_EVEN, MLA_KV_LORA, MLA_HEADS * (MLA_NOPE + MLA_V)), MLA_KV_LORA ** -0.5)
    ev_w_out = nrm((N_EVEN, EVEN_MIX, D_MODEL), EVEN_MIX ** -0.5)
    od_norm = gain((N_ODD, D_MODEL))
    od_w_in = nrm((N_ODD, D_MODEL, ODD_IN), D_MODEL ** -0.5)
    diff_lq1 = nrm((N_ODD, DIFF_HEAD_DIM), 0.1)
    diff_lk1 = nrm((N_ODD, DIFF_HEAD_DIM), 0.1)
    diff_lq2 = nrm((N_ODD, DIFF_HEAD_DIM), 0.1)
    diff_lk2 = nrm((N_ODD, DIFF_HEAD_DIM), 0.1)
    diff_subln = gain((N_ODD, 2 * DIFF_HEAD_DIM))
    od_w_out = nrm((N_ODD, ODD_MIX, D_MODEL), ODD_MIX ** -0.5)
    final_norm = gain((D_MODEL,))
    return {"x": x, "positions": positions, "ffn_norm": ffn_norm, "ffn_w_gu": ffn_w_gu,
            "ffn_w_down": ffn_w_down, "ev_norm": ev_norm, "ev_w_in": ev_w_in,
            "s5_lambda_re": s5_lambda_re, "s5_lambda_im": s5_lambda_im, "s5_log_dt": s5_log_dt,
            "s5_b_re": s5_b_re, "s5_b_im": s5_b_im, "s5_c_re": s5_c_re, "s5_c_im": s5_c_im,
            "s5_d": s5_d, "s5_w_glu": s5_w_glu, "s5_b_glu": s5_b_glu, "mla_q_norm": mla_q_norm,
            "mla_w_uq": mla_w_uq, "mla_kv_norm": mla_kv_norm, "mla_w_ukv": mla_w_ukv,
            "ev_w_out": ev_w_out, "od_norm": od_norm, "od_w_in": od_w_in, "diff_lq1": diff_lq1,
            "diff_lk1": diff_lk1, "diff_lq2": diff_lq2, "diff_lk2": diff_lk2,
            "diff_subln": diff_subln, "od_w_out": od_w_out, "final_norm": final_norm}


def reference(x, positions, ffn_norm, ffn_w_gu, ffn_w_down, ev_norm, ev_w_in, s5_lambda_re,
              s5_lambda_im, s5_log_dt, s5_b_re, s5_b_im, s5_c_re, s5_c_im, s5_d, s5_w_glu,
              s5_b_glu, mla_q_norm, mla_w_uq, mla_kv_norm, mla_w_ukv, ev_w_out, od_norm, od_w_in,
              diff_lq1, diff_lk1, diff_lq2, diff_lk2, diff_subln, od_w_out, final_norm):
    cos_t, sin_t = _rope_tables(positions, DIFF_ROT, ROPE_THETA)
    cos_m, sin_m = _rope_tables(positions, MLA_ROPE, MLA_ROPE_THETA)
    for l in range(DEPTH):
        x = x + 0.5 * _swiglu(_rmsnorm(x, ffn_norm[l, 0]), ffn_w_gu[l, 0], ffn_w_down[l, 0])
        i = l // 2
        if l % 2 == 0:
            h = _rmsnorm(x, ev_norm[i])
            x = x + _even_mixer(h, cos_m, sin_m, ev_w_in[i], s5_lambda_re[i], s5_lambda_im[i],
                                s5_log_dt[i], s5_b_re[i], s5_b_im[i], s5_c_re[i], s5_c_im[i],
                                s5_d[i], s5_w_glu[i], s5_b_glu[i], mla_q_norm[i], mla_w_uq[i],
                                mla_kv_norm[i], mla_w_ukv[i], ev_w_out[i])
        else:
            h = _rmsnorm(x, od_norm[i])
            x = x + _odd_mixer(h, cos_t, sin_t, l, od_w_in[i], diff_lq1[i], diff_lk1[i],
                               diff_lq2[i], diff_lk2[i], diff_subln[i], od_w_out[i])
        x = x + 0.5 * _swiglu(_rmsnorm(x, ffn_norm[l, 1]), ffn_w_gu[l, 1], ffn_w_down[l, 1])
    return _rmsnorm(x, final_norm)
```

```python
import math
import numpy as np
from contextlib import ExitStack
import concourse.bass as bass
import concourse.mybir as mybir
from concourse.bass_utils import run_bass_kernel_spmd


F32 = mybir.dt.float32
BF16 = mybir.dt.bfloat16
I32 = mybir.dt.int32
AF = mybir.ActivationFunctionType
ALU = mybir.AluOpType
AX = mybir.AxisListType


class Stream:
    def __init__(self, kb, name, h, in_order_safe=False):
        self.kb = kb
        self.name = name
        self.h = h
        self.sem = kb.new_sem("e_" + name)
        self.count = 0
        self.seen = {}
        self.in_order_safe = in_order_safe

    def wait(self, ticket):
        if ticket is None:
            return
        sem, val = ticket
        key = id(sem)
        if self.seen.get(key, 0) >= val:
            return
        self.h.wait_ge(sem, val)
        self.seen[key] = val


class Buf:
    __slots__ = ("name", "lw", "rd", "dsem", "dcount", "excl")

    def __init__(self, name, excl=False):
        self.name = name
        self.excl = excl
        self.lw = None
        self.rd = []
        self.dsem = None
        self.dcount = 0


class KB:
    def __init__(self):
        self.nc = bass.Bass("TRN2", target_bir_lowering=False)
        self.es = ExitStack()
        self.nsem = 0
        self.scopes = []
        self.scope_bufs = []
        self.sem_pool = []
        self.no_recycle = False
        nc = self.nc
        self.pe = Stream(self, "pe", nc.tensor, in_order_safe=True)
        self.act = Stream(self, "act", nc.scalar)
        self.dve = Stream(self, "dve", nc.vector)
        self.pool = Stream(self, "pool", nc.gpsimd)
        self.sp = Stream(self, "sp", nc.sync)
        self.streams = [self.pe, self.act, self.dve, self.pool, self.sp]
        self.dma_tickets = []
        self.bufs = []

    def new_sem(self, name):
        self.nsem += 1
        return self.es.enter_context(self.nc.semaphore(f"{name}_{self.nsem}"))

    def buf(self, name, excl=False):
        b = Buf(name, excl)
        self.bufs.append(b)
        if self.scope_bufs:
            self.scope_bufs[-1].append(b)
        return b

    def new_dsem(self, name):
        if self.sem_pool:
            return self.sem_pool.pop()
        return (self.new_sem(name), 0)

    def sb(self, name, shape, dt):
        self.nname = getattr(self, "nname", 0) + 1
        return self.cur.enter_context(self.nc.sbuf_tensor(f"{name}_{self.nname}", shape, dt))

    def ps(self, name, shape, dt=F32):
        self.nname = getattr(self, "nname", 0) + 1
        return self.cur.enter_context(self.nc.psum_tensor(f"{name}_{self.nname}", shape, dt))

    @property
    def cur(self):
        return self.scopes[-1] if self.scopes else self.es

    def scope(self):
        kb = self

        class _S:
            def __enter__(s):
                st = ExitStack()
                kb.scopes.append(st)
                kb.scope_bufs.append([])
                return st

            def __exit__(s, *a):
                if a[0] is None:
                    kb.barrier()
                    for b in kb.scope_bufs[-1]:
                        if b.dsem is not None and not kb.no_recycle:
                            kb.sem_pool.append((b.dsem, b.dcount))
                            b.dsem = None
                kb.scope_bufs.pop()
                st = kb.scopes.pop()
                st.close()
                return False

        return _S()

    def _deps(self, r, w):
        deps = []
        for b in r:
            if b.lw is not None:
                deps.append(b.lw)
            if b.excl:
                deps.extend(b.rd)
        for b in w:
            if b.lw is not None:
                deps.append(b.lw)
            deps.extend(b.rd)
        return deps

    def _commit(self, ticket, r, w):
        for b in w:
            b.lw = ticket
            b.rd = []
        for b in r:
            if b in w:
                continue
            b.rd.append(ticket)
            if len(b.rd) > 24:
                best = {}
                for (s, v) in b.rd:
                    k = id(s)
                    if k not in best or best[k][1] < v:
                        best[k] = (s, v)
                b.rd = list(best.values())

    def op(self, st, fn, r=(), w=(), sig=True):
        r = list(r)
        w = list(w)
        for (sem, val) in self._deps(r, w):
            if sem is st.sem and st.in_order_safe:
                continue
            st.wait((sem, val))
        ins = fn()
        if sig:
            st.count += 1
            ins.then_inc(st.sem, 1)
            ticket = (st.sem, st.count)
        else:
            ticket = (st.sem, st.count + 1)
        self._commit(ticket, r, w)
        return ins

    def dma(self, st, out, in_, r=(), w=(), sembuf=None, **kw):
        r = list(r)
        w = list(w)
        for t in self._deps(r, w):
            st.wait(t)
        if sembuf is None:
            sembuf = w[0] if w else r[0]
        if sembuf.dsem is None:
            sembuf.dsem, sembuf.dcount = self.new_dsem("d_" + sembuf.name)
        ins = st.h.dma_start(out=out, in_=in_, **kw)
        sembuf.dcount += 16
        ins.then_inc(sembuf.dsem, 16)
        ticket = (sembuf.dsem, sembuf.dcount)
        self._commit(ticket, r, w)
        self.dma_tickets.append(ticket)
        if len(self.dma_tickets) > 64:
            best = {}
            for (s, v) in self.dma_tickets:
                k = id(s)
                if k not in best or best[k][1] < v:
                    best[k] = (s, v)
            self.dma_tickets = list(best.values())
        return ins

    def allgather(self, src, src_b, dst, dst_b, groups):
        st = self.pool
        for t in self._deps([src_b], [dst_b]):
            st.wait(t)
        ins = st.h.collective_compute("AllGather", ALU.bypass, replica_groups=groups, ins=[src], outs=[dst])
        sem = self.new_sem("cc")
        ins.then_inc(sem, 1)
        ticket = (sem, 1)
        self._commit(ticket, [src_b], [dst_b])
        self.dma_tickets.append(ticket)

    def gather_rows(self, out, out_b, in_, in_b, idx_ap, idx_b):
        st = self.pool
        for t in self._deps([in_b, idx_b], [out_b]):
            st.wait(t)
        if out_b.dsem is None:
            out_b.dsem, out_b.dcount = self.new_dsem("d_" + out_b.name)
        ins = st.h.indirect_dma_start(out=out, out_offset=None, in_=in_,
                                      in_offset=bass.IndirectOffsetOnAxis(ap=idx_ap, axis=0))
        out_b.dcount += 16
        ins.then_inc(out_b.dsem, 16)
        ticket = (out_b.dsem, out_b.dcount)
        self._commit(ticket, [in_b, idx_b], [out_b])
        self.dma_tickets.append(ticket)

    def barrier(self, streams=None):
        tickets = [(s.sem, s.count) for s in self.streams if s.count > 0]
        tickets += self.dma_tickets
        for s in (streams or self.streams):
            for t in tickets:
                if t[0] is s.sem and s.in_order_safe:
                    continue
                s.wait(t)
        if streams is None:
            self.dma_tickets = []
            for b in self.bufs:
                b.lw = None
                b.rd = []

    def finish(self, out_bufs):
        self.barrier(streams=[self.sp])
        self.es.close()


NORM_EPS = 1e-6
D = 1024
DFF = 2816
NFT = 22


def emit_consts(kb):
    c = {}
    c["ones_bf"] = kb.sb("ones_bf", [128, 128], BF16)
    c["ones_b"] = kb.buf("ones_bf")
    kb.op(kb.dve, lambda: kb.nc.vector.memset(c["ones_bf"][:], 1.0), w=[c["ones_b"]])
    c["one"] = kb.sb("one", [128, 1], F32)
    c["one_b"] = kb.buf("one")
    kb.op(kb.dve, lambda: kb.nc.vector.memset(c["one"][:], 1.0), w=[c["one_b"]])
    return c


def emit_norm(kb, c, x_d, x_db, g_d, hT, hT_b, NT, ncols_per=512):
    nc = kb.nc
    ntt = NT // 512
    with kb.scope():
        gT = kb.sb("gT", [128, 8], F32)
        gT_b = kb.buf("gT")
        kb.dma(kb.sp, gT[:], g_d, w=[gT_b])
        xin = [kb.sb("xin", [128, 8, 512], F32) for _ in range(2)]
        xin_b = [kb.buf("xin") for _ in range(2)]
        sq = [kb.sb("sq", [128, 8, 512], BF16) for _ in range(2)]
        sq_b = [kb.buf("sq") for _ in range(2)]
        rs = [kb.sb("rs", [128, 512], F32) for _ in range(2)]
        rs_b = [kb.buf("rs") for _ in range(2)]
        pss = [kb.ps("pss", [128, 512]) for _ in range(2)]
        pss_b = [kb.buf("pss", excl=True) for _ in range(2)]
        xv = x_d.rearrange("(k p) t -> p k t", p=128)
        for tt in range(ntt):
            i = tt % 2
            kb.dma(kb.sp, xin[i][:], xv[:, :, tt * 512:(tt + 1) * 512], r=[x_db], w=[xin_b[i]])
            kb.op(kb.act, lambda: nc.scalar.activation(out=sq[i][:], in_=xin[i][:], func=AF.Square),
                  r=[xin_b[i]], w=[sq_b[i]])
            for k in range(8):
                kb.op(kb.pe, lambda: nc.tensor.matmul(pss[i][:], lhsT=c["ones_bf"][:], rhs=sq[i][:, k, :],
                                                      start=(k == 0), stop=(k == 7)),
                      r=[sq_b[i], c["ones_b"]], w=[pss_b[i]], sig=(k == 7))
            kb.op(kb.act, lambda: nc.scalar.activation(out=rs[i][:], in_=pss[i][:], func=AF.Sqrt,
                                                       bias=c["eps"][:], scale=1.0 / D),
                  r=[pss_b[i], c["eps_b"]], w=[rs_b[i]])
            kb.op(kb.dve, lambda: nc.vector.reciprocal(out=rs[i][:], in_=rs[i][:]), r=[rs_b[i]], w=[rs_b[i]])
            for k in range(8):
                kb.op(kb.dve, lambda: nc.vector.scalar_tensor_tensor(
                    out=hT[:, k, tt * 512:(tt + 1) * 512], in0=xin[i][:, k, :], scalar=gT[:, k:k + 1],
                    in1=rs[i][:], op0=ALU.mult, op1=ALU.mult),
                    r=[xin_b[i], rs_b[i], gT_b], w=[hT_b[tt]])
        kb.barrier()


def emit_ffn(kb, c, x_d, x_db, xo_d, xo_db, g_d, wgu_d, wd_d, NT):
    nc = kb.nc
    ntt = NT // 512
    with kb.scope():
        hT = kb.sb("hT", [128, 8, NT], BF16)
        hT_b = [kb.buf("hT") for _ in range(ntt)]
        emit_norm(kb, c, x_d, x_db, g_d, hT, hT_b, NT)
        actT = kb.sb("actT", [128, NFT, NT], BF16)
        actT_b = [[kb.buf("actT") for _ in range(ntt)] for _ in range(NFT)]
        with kb.scope():
            wg = [kb.sb("wgu", [128, 8, 512], BF16) for _ in range(2)]
            wg_b = [kb.buf("wgu") for _ in range(2)]
            psG = [kb.ps("psG", [128, 512]) for _ in range(2)]
            psU = [kb.ps("psU", [128, 512]) for _ in range(2)]
            psG_b = [kb.buf("psG", excl=True) for _ in range(2)]
            psU_b = [kb.buf("psU", excl=True) for _ in range(2)]
            sg = [kb.sb("sg", [128, 512], F32) for _ in range(2)]
            sg_b = [kb.buf("sg") for _ in range(2)]
            it = 0
            for j in range(11):
                s = j % 2
                kb.dma(kb.pool, wg[s][:], wgu_d[j], w=[wg_b[s]])
                for h in range(2):
                    ft = 2 * j + h
                    for tt in range(ntt):
                        i = it % 2
                        it += 1
                        tok = slice(tt * 512, (tt + 1) * 512)
                        for k in range(8):
                            kb.op(kb.pe, lambda: nc.tensor.matmul(
                                psG[i][:], lhsT=wg[s][:, k, h * 128:(h + 1) * 128], rhs=hT[:, k, tok],
                                start=(k == 0), stop=(k == 7)),
                                r=[wg_b[s], hT_b[tt]], w=[psG_b[i]], sig=(k == 7))
                        for k in range(8):
                            kb.op(kb.pe, lambda: nc.tensor.matmul(
                                psU[i][:], lhsT=wg[s][:, k, 256 + h * 128:256 + (h + 1) * 128], rhs=hT[:, k, tok],
                                start=(k == 0), stop=(k == 7)),
                                r=[wg_b[s], hT_b[tt]], w=[psU_b[i]], sig=(k == 7))
                        kb.op(kb.act, lambda: nc.scalar.activation(out=sg[i][:], in_=psG[i][:], func=AF.Silu),
                              r=[psG_b[i]], w=[sg_b[i]])
                        kb.op(kb.dve, lambda: nc.vector.tensor_tensor(
                            out=actT[:, ft, tok], in0=psU[i][:], in1=sg[i][:], op=ALU.mult),
                            r=[psU_b[i], sg_b[i]], w=[actT_b[ft][tt]])
            kb.barrier()
        with kb.scope():
            wd = [kb.sb("wd", [128, NFT, 256], BF16) for _ in range(2)]
            wd_b = [kb.buf("wd") for _ in range(2)]
            psD = [kb.ps("psD", [128, 512]) for _ in range(2)]
            psD_b = [kb.buf("psD", excl=True) for _ in range(2)]
            xr = [kb.sb("xr", [128, NT], F32) for _ in range(2)]
            xr_b = [kb.buf("xr") for _ in range(2)]
            it = 0
            for q in range(4):
                s = q % 2
                kb.dma(kb.pool, wd[s][:], wd_d[q], w=[wd_b[s]])
                for m in range(2):
                    dt = 2 * q + m
                    xi = dt % 2
                    kb.dma(kb.sp, xr[xi][:], x_d[dt * 128:(dt + 1) * 128, :], r=[x_db], w=[xr_b[xi]])
                    for tt in range(ntt):
                        i = it % 2
                        it += 1
                        tok = slice(tt * 512, (tt + 1) * 512)
                        for cc in range(NFT):
                            kb.op(kb.pe, lambda: nc.tensor.matmul(
                                psD[i][:], lhsT=wd[s][:, cc, m * 128:(m + 1) * 128], rhs=actT[:, cc, tok],
                                start=(cc == 0), stop=(cc == NFT - 1)),
                                r=[wd_b[s], actT_b[cc][tt]], w=[psD_b[i]], sig=(cc == NFT - 1))
                        kb.op(kb.dve, lambda: nc.vector.scalar_tensor_tensor(
                            out=xr[xi][:, tok], in0=psD[i][:], scalar=0.5, in1=xr[xi][:, tok],
                            op0=ALU.mult, op1=ALU.add),
                            r=[psD_b[i]], w=[xr_b[xi]])
                    kb.dma(kb.sp, xo_d[dt * 128:(dt + 1) * 128, :], xr[xi][:], r=[xr_b[xi]], w=[xo_db], sembuf=xr_b[xi])
            kb.barrier()


PI = math.pi
TWO_PI = 2.0 * math.pi
NORM_EPS = 1e-6
S = 8192
NQT = 16


def emit_rope_tables(kb, c, pos_d, pos_db, tt, invc, invc_b, sgnc, sgnc_b, COS, SINS, tb, scr, scr_b):
    nc = kb.nc
    pi_t, pf, kf, r, t2 = scr["pi"], scr["pf"], scr["kf"], scr["r"], scr["t2"]
    kb.dma(kb.sp, pi_t[:], pos_d[tt * 512:(tt + 1) * 512].partition_broadcast(128), r=[pos_db], w=[scr_b["pi"]])
    kb.op(kb.dve, lambda: nc.vector.tensor_copy(out=pf[:], in_=pi_t[:]), r=[scr_b["pi"]], w=[scr_b["pf"]])
    kb.op(kb.dve, lambda: nc.vector.tensor_scalar(out=pf[:], in0=pf[:], scalar1=invc[:, 0:1], scalar2=None,
                                                  op0=ALU.mult), r=[scr_b["pf"], invc_b], w=[scr_b["pf"]])
    kb.op(kb.dve, lambda: nc.vector.tensor_scalar(out=scr["ki"][:], in0=pf[:], scalar1=1.0 / TWO_PI, scalar2=None,
                                                  op0=ALU.mult), r=[scr_b["pf"]], w=[scr_b["ki"]])
    kb.op(kb.dve, lambda: nc.vector.tensor_copy(out=kf[:], in_=scr["ki"][:]), r=[scr_b["ki"]], w=[scr_b["kf"]])
    C1 = 6.28125
    C2 = TWO_PI - C1
    kb.op(kb.dve, lambda: nc.vector.scalar_tensor_tensor(out=r[:], in0=kf[:], scalar=-C1, in1=pf[:],
                                                         op0=ALU.mult, op1=ALU.add),
          r=[scr_b["kf"], scr_b["pf"]], w=[scr_b["r"]])
    kb.op(kb.dve, lambda: nc.vector.scalar_tensor_tensor(out=r[:], in0=kf[:], scalar=-C2, in1=r[:],
                                                         op0=ALU.mult, op1=ALU.add),
          r=[scr_b["kf"]], w=[scr_b["r"]])

    def wrap(x, xb):
        kb.op(kb.dve, lambda: nc.vector.tensor_scalar(out=t2[:], in0=x[:], scalar1=PI, scalar2=-TWO_PI,
                                                      op0=ALU.is_gt, op1=ALU.mult), r=[xb], w=[scr_b["t2"]])
        kb.op(kb.dve, lambda: nc.vector.tensor_tensor(out=x[:], in0=x[:], in1=t2[:], op=ALU.add),
              r=[scr_b["t2"]], w=[xb])
        kb.op(kb.dve, lambda: nc.vector.tensor_scalar(out=t2[:], in0=x[:], scalar1=-PI, scalar2=TWO_PI,
                                                      op0=ALU.is_lt, op1=ALU.mult), r=[xb], w=[scr_b["t2"]])
        kb.op(kb.dve, lambda: nc.vector.tensor_tensor(out=x[:], in0=x[:], in1=t2[:], op=ALU.add),
              r=[scr_b["t2"]], w=[xb])

    wrap(r, scr_b["r"])
    kb.op(kb.act, lambda: nc.scalar.activation(out=SINS[:], in_=r[:], func=AF.Sin), r=[scr_b["r"]], w=[tb["sin"]])
    kb.op(kb.dve, lambda: nc.vector.tensor_scalar(out=SINS[:], in0=SINS[:], scalar1=sgnc[:, 0:1], scalar2=None,
                                                  op0=ALU.mult), r=[sgnc_b], w=[tb["sin"]])
    kb.op(kb.dve, lambda: nc.vector.tensor_scalar(out=pf[:], in0=r[:], scalar1=PI / 2, scalar2=None, op0=ALU.add),
          r=[scr_b["r"]], w=[scr_b["pf"]])
    wrap(pf, scr_b["pf"])
    kb.op(kb.act, lambda: nc.scalar.activation(out=COS[:], in_=pf[:], func=AF.Sin), r=[scr_b["pf"]], w=[tb["cos"]])


def rope_scratch(kb):
    scr = {}
    scr_b = {}
    for n, dt in [("pi", I32), ("pf", F32), ("ki", I32), ("kf", F32), ("r", F32), ("t2", F32)]:
        scr[n] = kb.sb("rs_" + n, [128, 512], dt)
        scr_b[n] = kb.buf("rs_" + n)
    return scr, scr_b


def emit_attention(kb, c, nheads, dk, nmaps, dv, QT, QT_b, KT, KT_b, V, V_b, scale, finalize, QTILE=512,
                   nob=2, nst=2, npt=3):
    nc = kb.nc
    W = dv + 1
    nsub = QTILE // 128
    nqt = S // QTILE
    assert nmaps * QTILE == 512
    with kb.scope():
        ST = [kb.ps("ST", [128, 512]) for _ in range(nst)]
        ST_b = [kb.buf("ST", excl=True) for _ in range(nst)]
        per_bank = 512 // W
        nbk = (nsub * nmaps + per_bank - 1) // per_bank
        OB = [[kb.ps("OB", [128, 512]) for _ in range(nbk)] for _ in range(nob)]
        OB_b = [[kb.buf("OB", excl=True) for _ in range(nbk)] for _ in range(nob)]
        PT = [kb.sb("PT", [128, 512], BF16) for _ in range(npt)]
        PT_b = [kb.buf("PT") for _ in range(npt)]

        def oslot(m, s):
            idx = m * nsub + s
            return idx // per_bank, (idx % per_bank) * W

        blocks = []
        oi = 0
        for qt in range(nqt):
            for h in range(nheads):
                ob = oi % nob
                oi += 1
                nkb = nsub * qt + nsub
                for kbi in range(nkb):
                    blocks.append((qt, h, kbi, ob, kbi == nkb - 1))
        bank_started = {}

        def emit_qk_exp(i):
            qt, h, kbi, ob, _ = blocks[i]
            o = max(0, (kbi - nsub * qt) * 128)
            n = QTILE - o
            si = i % nst
            pi_ = i % npt
            q0 = qt * QTILE + o
            if nmaps > 1 and o == 0:
                kb.op(kb.pe, lambda: nc.tensor.matmul(
                    ST[si][:, :], lhsT=KT[h][0:dk, kbi * 128:(kbi + 1) * 128],
                    rhs=QT[h][0:dk, qt * nmaps * QTILE:(qt + 1) * nmaps * QTILE], start=True, stop=True, skip_group_check=True),
                    r=[KT_b[h][kbi // 4], QT_b[h][q0 // 512]], w=[ST_b[si]], sig=True)
            else:
                for m in range(nmaps):
                    rows = slice(0, dk)
                    if nmaps > 1:
                        cbase = qt * nmaps * QTILE + m * QTILE + o
                        qsrc = QT[h][rows, cbase:cbase + n]
                    else:
                        qsrc = QT[h][rows, q0:q0 + n]
                    kb.op(kb.pe, lambda: nc.tensor.matmul(
                        ST[si][:, m * QTILE:m * QTILE + n], lhsT=KT[h][rows, kbi * 128:(kbi + 1) * 128],
                        rhs=qsrc, start=(m == 0), stop=(m == nmaps - 1), skip_group_check=True),
                        r=[KT_b[h][kbi // 4], QT_b[h][q0 // 512]], w=[ST_b[si]], sig=(m == nmaps - 1))
            if nmaps == 1:
                src = ST[si][:, 0:n]
                dst = PT[pi_][:, 0:n]
            else:
                src = ST[si][:].rearrange("p (m q) -> p m q", m=nmaps)[:, :, 0:n]
                dst = PT[pi_][:].rearrange("p (m q) -> p m q", m=nmaps)[:, :, 0:n]
            kb.op(kb.act, lambda: nc.scalar.activation(out=dst, in_=src, func=AF.Exp, scale=scale),
                  r=[ST_b[si]], w=[PT_b[pi_]])
            if kbi >= nsub * qt:
                for m in range(nmaps):
                    kb.op(kb.pool, lambda: nc.gpsimd.tensor_tensor(
                        out=PT[pi_][:, m * QTILE:m * QTILE + 128], in0=PT[pi_][:, m * QTILE:m * QTILE + 128],
                        in1=c["tri"][:], op=ALU.mult),
                        r=[c["tri_b"]], w=[PT_b[pi_]])

        def emit_pv(i):
            qt, h, kbi, ob, is_last = blocks[i]
            o = max(0, (kbi - nsub * qt) * 128)
            pi_ = i % npt
            for m in range(nmaps):
                for s in range(o // 128, nsub):
                    bk, col = oslot(m, s)
                    first = (kbi == 0)
                    last = (kbi == nsub * qt + s)
                    key = (qt, h, bk)
                    st_flag = first and key not in bank_started
                    if first:
                        bank_started[key] = True
                    c0 = m * QTILE + s * 128 - o
                    kb.op(kb.pe, lambda: nc.tensor.matmul(
                        OB[ob][bk][:, col:col + W], lhsT=PT[pi_][:, c0:c0 + 128],
                        rhs=V[:, kbi, h, :], start=st_flag, stop=last, skip_group_check=True),
                        r=[PT_b[pi_], V_b[kbi // 4]], w=[OB_b[ob][bk]],
                        sig=(m == nmaps - 1 and s == nsub - 1))
            if is_last:
                def getO(m, s, ob=ob):
                    bk, col = oslot(m, s)
                    return OB[ob][bk][:, col:col + W], OB_b[ob][bk]

                finalize(h, qt, getO)

        nb = len(blocks)
        ahead = nst - 1
        for i0 in range(min(ahead, nb)):
            emit_qk_exp(i0)
        for i in range(nb):
            if i + ahead < nb:
                emit_qk_exp(i + ahead)
            emit_pv(i)
        kb.barrier()


def load_hin(kb, d, hin_t, hin_b, tt):
    rr, cc0 = tt // 4, (tt % 4) * 512
    if "hTg" in d:
        for q in range(4):
            kb.dma(kb.sp, hin_t[:, 2 * q:2 * q + 2, :],
                   d["hTg"][q][rr * 256:(rr + 1) * 256, cc0:cc0 + 512].rearrange("(kk p) t -> p kk t", p=128),
                   r=[d["hTg_b"][q]], w=[hin_b])
    else:
        kb.dma(kb.sp, hin_t[:], d["hTf"][rr].rearrange("(k p) t -> p k t", p=128)[:, :, cc0:cc0 + 512],
               r=[d["hTf_b"]], w=[hin_b])


def mix_ap(d, key, r0, r1, t0, n):
    if key + "_chunks" in d:
        j, lo = t0 // 2048, t0 % 2048
        return d[key + "_chunks"][j][r0:r1, lo:lo + n]
    return d[key][r0:r1, t0:t0 + n]


def load_cast(kb, name, shape, src_d, dt=BF16):
    t = kb.sb(name, shape, dt)
    b = kb.buf(name)
    kb.dma(kb.pool, t[:], src_d, w=[b])
    return t, b


def load_plain(kb, name, shape, src_d, dt=F32):
    t = kb.sb(name, shape, dt)
    b = kb.buf(name)
    kb.dma(kb.sp, t[:], src_d, w=[b])
    return t, b


def emit_launchB_mla(kb, c, d):
    nc = kb.nc
    scale = (64 + 32) ** -0.5
    with kb.scope():
        QT = [kb.sb("QT", [128, S], BF16) for _ in range(2)]
        KT = [kb.sb("KT", [128, S], BF16) for _ in range(2)]
        QT_b = [[kb.buf("QT") for _ in range(NQT)] for _ in range(2)]
        KT_b = [[kb.buf("KT") for _ in range(NQT)] for _ in range(2)]
        V = kb.sb("V", [128, S // 128, 2, 65], BF16)
        V_b = [kb.buf("V") for _ in range(NQT)]
        kb.op(kb.pool, lambda: nc.gpsimd.memset(V[:], 1.0), w=V_b)
        with kb.scope():
          if True:
              w_inA, w_inA_b = load_cast(kb, "w_inA", [128, 8, 800], d["w_inA"])
              wkr, wkr_b = kb.sb("wkr", [128, 8, 96], BF16), kb.buf("wkr")
              wkrs, wkrs_b = kb.sb("wkrs", [128, 8, 96], BF16), kb.buf("wkrs")
              kb.op(kb.pool, lambda: nc.gpsimd.memset(wkr[:], 0.0), w=[wkr_b])
              kb.op(kb.pool, lambda: nc.gpsimd.memset(wkrs[:], 0.0), w=[wkrs_b])
              kb.dma(kb.pool, wkr[:, :, 64:96], d["w_kr"], w=[wkr_b])
              kb.dma(kb.pool, wkrs[:, :, 64:96], d["w_krsw"], w=[wkrs_b])
              w_uq, w_uq_b = load_cast(kb, "w_uq", [128, 3, 192], d["w_uq"])
              w_uqs, w_uqs_b = load_cast(kb, "w_uqs", [128, 3, 192], d["w_uq_sw"])
              w_kk, w_kk_b = load_cast(kb, "w_kk", [128, 2, 128], d["w_ukv_k"])
              w_kv, w_kv_b = load_cast(kb, "w_kv", [128, 2, 128], d["w_ukv_v"])
              gq, gq_b = load_plain(kb, "gq", [128, 5], d["lat_g"])
              invc, invc_b = load_plain(kb, "invc", [128, 1], d["rope_inv"])
              sgnc, sgnc_b = load_plain(kb, "sgnc", [128, 1], d["rope_sgn"])
              scr, scr_b = rope_scratch(kb)
              COS = kb.sb("COS", [128, 512], F32)
              SINS = kb.sb("SINS", [128, 512], F32)
              tb = {"cos": kb.buf("COS"), "sin": kb.buf("SINS")}
              hin = [kb.sb("hin", [128, 8, 512], BF16) for _ in range(2)]
              hin_b = [kb.buf("hin") for _ in range(2)]
              csb = kb.sb("csb", [128, 5, 512], F32)
              csb_b = [kb.buf("csb") for _ in range(5)]
              sq = kb.sb("sq", [128, 5, 512], BF16)
              sq_b = [kb.buf("sq") for _ in range(5)]
              cn = kb.sb("cn", [128, 5, 512], BF16)
              cn_b = [kb.buf("cn") for _ in range(5)]
              rs = [kb.sb("rs", [128, 512], F32) for _ in range(2)]
              rs_b = [kb.buf("rs") for _ in range(2)]
              m1 = [kb.sb("m1", [128, 512], F32) for _ in range(2)]
              m1_b = [kb.buf("m1") for _ in range(2)]
              m2 = [kb.sb("m2", [128, 512], F32) for _ in range(2)]
              m2_b = [kb.buf("m2") for _ in range(2)]
              PS = [kb.ps("PS", [128, 512]) for _ in range(6)]
              PS_b = [kb.buf("PS", excl=True) for _ in range(6)]
              pctr = [0]
              mctr = [0]

              def nextps():
                  i = pctr[0] % 6
                  pctr[0] += 1
                  return PS[i], PS_b[i]

              for tt in range(NQT):
                  hi = tt % 2
                  tok = slice(tt * 512, (tt + 1) * 512)
                  rr, cc0 = tt // 4, (tt % 4) * 512
                  load_hin(kb, d, hin[hi], hin_b[hi], tt)
                  emit_rope_tables(kb, c, d["pos"], d["pos_b"], tt, invc, invc_b, sgnc, sgnc_b, COS, SINS, tb, scr, scr_b)
                  if d.get("uT") is not None:
                      ps, psb = nextps()
                      for k in range(8):
                          kb.op(kb.pe, lambda: nc.tensor.matmul(ps[:], lhsT=w_inA[:, k, 0:128], rhs=hin[hi][:, k, :],
                                                                start=(k == 0), stop=(k == 7)),
                                r=[w_inA_b, hin_b[hi]], w=[psb], sig=(k == 7))
                      kb.op(kb.act, lambda: nc.scalar.copy(out=d["uT"][:, tok], in_=ps[:]), r=[psb], w=[d["uT_b"][tt]])
                  if d.get('stop') == 1:
                      break
                  for j in range(5):
                      ps, psb = nextps()
                      for k in range(8):
                          kb.op(kb.pe, lambda: nc.tensor.matmul(ps[:], lhsT=w_inA[:, k, 128 + j * 128:256 + j * 128],
                                                                rhs=hin[hi][:, k, :], start=(k == 0), stop=(k == 7)),
                                r=[w_inA_b, hin_b[hi]], w=[psb], sig=(k == 7))
                      kb.op(kb.act, lambda: nc.scalar.activation(out=sq[:, j, :], in_=ps[:], func=AF.Square),
                            r=[psb], w=[sq_b[j]])
                      kb.op(kb.dve, lambda: nc.vector.tensor_scalar(out=csb[:, j, :], in0=ps[:], scalar1=gq[:, j:j + 1],
                                                                    scalar2=None, op0=ALU.mult),
                            r=[psb, gq_b], w=[csb_b[j]])
                  if d.get('stop') == 2:
                      break
                  for (idx, chunks, n) in [(0, [0, 1, 2], 384), (1, [3, 4], 256)]:
                      ps, psb = nextps()
                      for ii, j in enumerate(chunks):
                          kb.op(kb.pe, lambda: nc.tensor.matmul(ps[:], lhsT=c["ones_bf"][:], rhs=sq[:, j, :],
                                                                start=(ii == 0), stop=(ii == len(chunks) - 1)),
                                r=[sq_b[j], c["ones_b"]], w=[psb], sig=(ii == len(chunks) - 1))
                      kb.op(kb.act, lambda: nc.scalar.activation(out=rs[idx][:], in_=ps[:], func=AF.Sqrt,
                                                                 bias=c["eps"][:], scale=1.0 / n),
                            r=[psb, c["eps_b"]], w=[rs_b[idx]])
                      kb.op(kb.dve, lambda: nc.vector.reciprocal(out=rs[idx][:], in_=rs[idx][:]), r=[rs_b[idx]], w=[rs_b[idx]])
                      for j in chunks:
                          kb.op(kb.dve, lambda: nc.vector.tensor_tensor(out=cn[:, j, :], in0=csb[:, j, :], in1=rs[idx][:],
                                                                        op=ALU.mult),
                                r=[csb_b[j], rs_b[idx]], w=[cn_b[j]])

                  def rope_combine(ps_a, psb_a, ps_s, psb_s, out_ap, out_b, rows):
                      mi = mctr[0] % 2
                      mctr[0] += 1
                      kb.op(kb.dve, lambda: nc.vector.tensor_tensor(out=m1[mi][rows, :], in0=ps_a[rows, :], in1=COS[rows, :],
                                                                    op=ALU.mult), r=[psb_a, tb["cos"]], w=[m1_b[mi]])
                      kb.op(kb.dve, lambda: nc.vector.tensor_tensor(out=m2[mi][rows, :], in0=ps_s[rows, :], in1=SINS[rows, :],
                                                                    op=ALU.mult), r=[psb_s, tb["sin"]], w=[m2_b[mi]])
                      kb.op(kb.pool, lambda: nc.gpsimd.tensor_tensor(out=out_ap, in0=m1[mi][rows, :], in1=m2[mi][rows, :],
                                                                     op=ALU.add), r=[m1_b[mi], m2_b[mi]], w=[out_b])

                  if d.get('stop') == 3:
                      break
                  ps_a, psb_a = nextps()
                  for k in range(8):
                      kb.op(kb.pe, lambda: nc.tensor.matmul(ps_a[0:96, :], lhsT=wkr[:, k, :], rhs=hin[hi][:, k, :],
                                                            start=(k == 0), stop=(k == 7)),
                            r=[wkr_b, hin_b[hi]], w=[psb_a], sig=(k == 7))
                  ps_s, psb_s = nextps()
                  for k in range(8):
                      kb.op(kb.pe, lambda: nc.tensor.matmul(ps_s[0:96, :], lhsT=wkrs[:, k, :], rhs=hin[hi][:, k, :],
                                                            start=(k == 0), stop=(k == 7)),
                            r=[wkrs_b, hin_b[hi]], w=[psb_s], sig=(k == 7))
                  rope_combine(ps_a, psb_a, ps_s, psb_s, KT[0][64:96, tok], KT_b[0][tt], slice(64, 96))
                  kb.op(kb.act, lambda: nc.scalar.copy(out=KT[1][64:96, tok], in_=KT[0][64:96, tok]),
                        r=[KT_b[0][tt]], w=[KT_b[1][tt]])
                  if d.get('stop') == 4:
                      break
                  for h in range(2):
                      ps_a, psb_a = nextps()
                      for j in range(3):
                          kb.op(kb.pe, lambda: nc.tensor.matmul(ps_a[0:96, :], lhsT=w_uq[:, j, h * 96:(h + 1) * 96],
                                                                rhs=cn[:, j, :], start=(j == 0), stop=(j == 2)),
                                r=[w_uq_b, cn_b[j]], w=[psb_a], sig=(j == 2))
                      ps_s, psb_s = nextps()
                      for j in range(3):
                          kb.op(kb.pe, lambda: nc.tensor.matmul(ps_s[0:96, :], lhsT=w_uqs[:, j, h * 96:(h + 1) * 96],
                                                                rhs=cn[:, j, :], start=(j == 0), stop=(j == 2)),
                                r=[w_uqs_b, cn_b[j]], w=[psb_s], sig=(j == 2))
                      rope_combine(ps_a, psb_a, ps_s, psb_s, QT[h][0:96, tok], QT_b[h][tt], slice(0, 96))
                  if d.get('stop') == 5:
                      break
                  for h in range(2):
                      ps, psb = nextps()
                      for j in range(2):
                          kb.op(kb.pe, lambda: nc.tensor.matmul(ps[0:64, :], lhsT=w_kk[:, j, h * 64:(h + 1) * 64],
                                                                rhs=cn[:, 3 + j, :], start=(j == 0), stop=(j == 1)),
                                r=[w_kk_b, cn_b[3 + j]], w=[psb], sig=(j == 1))
                      kb.op(kb.act, lambda: nc.scalar.copy(out=KT[h][0:64, tok], in_=ps[0:64, :]),
                            r=[psb], w=[KT_b[h][tt]])
                  if d.get('stop') == 6:
                      break
                  for s4 in range(4):
                      ps, psb = nextps()
                      for j in range(2):
                          kb.op(kb.pe, lambda: nc.tensor.matmul(ps[:, 0:128], lhsT=cn[:, 3 + j, s4 * 128:(s4 + 1) * 128],
                                                                rhs=w_kv[:, j, :], start=(j == 0), stop=(j == 1)),
                                r=[w_kv_b, cn_b[3 + j]], w=[psb], sig=(j == 1))
                      kb.op(kb.act, lambda: nc.scalar.copy(
                          out=V[:, tt * 4 + s4, :, 0:64], in_=ps[:, 0:128].rearrange("p (h e) -> p h e", h=2)),
                          r=[psb], w=[V_b[tt]])
              kb.barrier()
        if d.get('mid_hook') is not None:
            d['mid_hook']()
        if d.get('skip_attn'):
            return
        with kb.scope():
            on = [kb.sb("on", [128, 4, 128], BF16) for _ in range(2)]
            on_b = [kb.buf("on") for _ in range(2)]
            rec = kb.sb("rec", [128, 8], F32)
            rec_b = kb.buf("rec")
            psT = kb.ps("psT", [128, 1024], BF16)
            psT_b = kb.buf("psT", excl=True)
            oT = [kb.sb("oT", [128, 512], BF16) for _ in range(2)]
            oT_b = [kb.buf("oT") for _ in range(2)]

            def finalize(h, qt, getO):
                i = qt % 2
                for s in range(4):
                    O, Ob = getO(0, s)
                    kb.op(kb.dve, lambda: nc.vector.reciprocal(out=rec[:, s:s + 1], in_=O[:, 64:65]), r=[Ob], w=[rec_b])
                    kb.op(kb.dve, lambda: nc.vector.tensor_scalar(out=on[i][:, s, h * 64:(h + 1) * 64], in0=O[:, 0:64],
                                                                  scalar1=rec[:, s:s + 1], scalar2=None, op0=ALU.mult),
                          r=[Ob, rec_b], w=[on_b[i]])
                if h == 1:
                    for s2 in range(4):
                        kb.op(kb.pe, lambda: nc.tensor.transpose(psT[:, s2 * 128:(s2 + 1) * 128], on[i][:, s2, :],
                                                                 c["ident_bf"][:]),
                              r=[on_b[i], c["ident_b"]], w=[psT_b])
                    kb.op(kb.act, lambda: nc.scalar.copy(out=oT[i][:], in_=psT[:, 0:512]), r=[psT_b], w=[oT_b[i]])
                    kb.dma(kb.sp, mix_ap(d, "mixB", 128, 256, qt * 512, 512), oT[i][:], r=[oT_b[i]], w=[d["mixB_b"]],
                           sembuf=oT_b[i])
                    if d.get("chunk_done") is not None and (qt + 1) % 4 == 0:
                        d["chunk_done"](qt // 4)

            emit_attention(kb, c, 2, 96, 1, 64, QT, QT_b, KT, KT_b, V, V_b, scale, finalize, nst=3, npt=4)


PI = math.pi
T = 512


def bc(ap2, n):
    return ap2.rearrange("p (g o) -> p g o", o=1).broadcast_to([128, 8, n])


def emit_s5(kb, c, d, uT, uT_b):
    nc = kb.nc
    V = nc.vector
    with kb.scope():
        COSt = kb.sb("COSt", [128, 8, T], F32)
        SINt = kb.sb("SINt", [128, 8, T], F32)
        tab_b = kb.buf("tabs")
        LB = kb.sb("LB", [128, 8, 128], BF16)
        LBs = kb.sb("LBs", [128, 8, 128], BF16)
        LC1 = kb.sb("LC1", [128, 8, 128], BF16)
        LC2 = kb.sb("LC2", [128, 8, 128], BF16)
        Rm = kb.sb("Rm", [128, 8, 128], F32)
        Rb = kb.sb("Rb", [128, 8], F32)
        par_b = kb.buf("s5par")
        d_t, d_tb = load_plain(kb, "d_t", [128, 1], d["s5_d"])
        with kb.scope():
            lr, lr_b = load_plain(kb, "lr", [128, 8], d["s5_lr"])
            li, li_b = load_plain(kb, "li", [128, 8], d["s5_li"])
            ldt, ldt_b = load_plain(kb, "ldt", [128, 8], d["s5_logdt"])
            br, br_b = load_plain(kb, "br", [128, 8, 16], d["s5_bre"])
            bi, bi_b = load_plain(kb, "bi", [128, 8, 16], d["s5_bim"])
            c1s, c1s_b = load_plain(kb, "c1s", [128, 128], d["s5_c1src"])
            c2s, c2s_b = load_plain(kb, "c2s", [128, 128], d["s5_c2src"])
            rowm, rowm_b = load_plain(kb, "rowm", [128, 8], d["rowmask"])
            Jm, Jm_b = load_plain(kb, "Jm", [128, 128], d["Jmat"])
            idf, idf_b = load_plain(kb, "idf", [128, 128], d["ident"])
            sb = kb.buf("s5scr")
            names = ["dt", "mag", "th", "cs", "sn", "t1", "t2", "t3", "abr", "abi", "den", "nr", "fre", "fim", "wc", "ws"]
            t = {n: kb.sb("s5_" + n, [128, 8], F32) for n in names}
            RW = [sb, lr_b, li_b, ldt_b]

            def dv(fn):
                kb.op(kb.dve, fn, r=RW, w=[sb])

            def ac(fn):
                kb.op(kb.act, fn, r=RW, w=[sb])

            ac(lambda: nc.scalar.activation(out=t["dt"][:], in_=ldt[:], func=AF.Exp))
            dv(lambda: V.tensor_tensor(out=t["t1"][:], in0=lr[:], in1=t["dt"][:], op=ALU.mult))
            ac(lambda: nc.scalar.activation(out=t["mag"][:], in_=t["t1"][:], func=AF.Exp))
            dv(lambda: V.tensor_copy(out=Rb[:], in_=t["mag"][:]))
            dv(lambda: V.tensor_tensor(out=t["th"][:], in0=li[:], in1=t["dt"][:], op=ALU.mult))
            ac(lambda: nc.scalar.activation(out=t["sn"][:], in_=t["th"][:], func=AF.Sin, scale=1.0 / 16))
            dv(lambda: V.tensor_scalar(out=t["t1"][:], in0=t["th"][:], scalar1=1.0 / 16, scalar2=PI / 2,
                                       op0=ALU.mult, op1=ALU.add))
            ac(lambda: nc.scalar.activation(out=t["cs"][:], in_=t["t1"][:], func=AF.Sin))

            def csq(cn, sn_):
                dv(lambda: V.tensor_tensor(out=t["t1"][:], in0=t[cn][:], in1=t[cn][:], op=ALU.mult))
                dv(lambda: V.tensor_tensor(out=t["t2"][:], in0=t[sn_][:], in1=t[sn_][:], op=ALU.mult))
                dv(lambda: V.tensor_tensor(out=t["t3"][:], in0=t[cn][:], in1=t[sn_][:], op=ALU.mult))
                dv(lambda: V.tensor_tensor(out=t[cn][:], in0=t["t1"][:], in1=t["t2"][:], op=ALU.subtract))
                dv(lambda: V.tensor_scalar(out=t[sn_][:], in0=t["t3"][:], scalar1=2.0, scalar2=None, op0=ALU.mult))

            for _ in range(4):
                csq("cs", "sn")
            dv(lambda: V.tensor_tensor(out=t["abr"][:], in0=t["mag"][:], in1=t["cs"][:], op=ALU.mult))
            dv(lambda: V.tensor_tensor(out=t["abi"][:], in0=t["mag"][:], in1=t["sn"][:], op=ALU.mult))
            dv(lambda: V.tensor_tensor(out=t["t1"][:], in0=lr[:], in1=lr[:], op=ALU.mult))
            dv(lambda: V.tensor_tensor(out=t["t2"][:], in0=li[:], in1=li[:], op=ALU.mult))
            dv(lambda: V.tensor_tensor(out=t["den"][:], in0=t["t1"][:], in1=t["t2"][:], op=ALU.add))
            dv(lambda: V.reciprocal(out=t["den"][:], in_=t["den"][:]))
            dv(lambda: V.tensor_scalar(out=t["nr"][:], in0=t["abr"][:], scalar1=-1.0, scalar2=None, op0=ALU.add))
            dv(lambda: V.tensor_tensor(out=t["t1"][:], in0=t["nr"][:], in1=lr[:], op=ALU.mult))
            dv(lambda: V.tensor_tensor(out=t["t2"][:], in0=t["abi"][:], in1=li[:], op=ALU.mult))
            dv(lambda: V.tensor_tensor(out=t["t1"][:], in0=t["t1"][:], in1=t["t2"][:], op=ALU.add))
            dv(lambda: V.tensor_tensor(out=t["fre"][:], in0=t["t1"][:], in1=t["den"][:], op=ALU.mult))
            dv(lambda: V.tensor_tensor(out=t["t1"][:], in0=t["abi"][:], in1=lr[:], op=ALU.mult))
            dv(lambda: V.tensor_tensor(out=t["t2"][:], in0=t["nr"][:], in1=li[:], op=ALU.mult))
            dv(lambda: V.tensor_tensor(out=t["t1"][:], in0=t["t1"][:], in1=t["t2"][:], op=ALU.subtract))
            dv(lambda: V.tensor_tensor(out=t["fim"][:], in0=t["t1"][:], in1=t["den"][:], op=ALU.mult))
            bbr = kb.sb("bbr", [128, 8, 16], F32)
            bbi = kb.sb("bbi", [128, 8, 16], F32)
            tA = kb.sb("tA", [128, 8, 16], F32)
            RWb = RW + [br_b, bi_b]

            def dvb(fn):
                kb.op(kb.dve, fn, r=RWb, w=[sb])

            dvb(lambda: V.tensor_tensor(out=bbr[:], in0=br[:], in1=bc(t["fre"][:], 16), op=ALU.mult))
            dvb(lambda: V.tensor_tensor(out=tA[:], in0=bi[:], in1=bc(t["fim"][:], 16), op=ALU.mult))
            dvb(lambda: V.tensor_tensor(out=bbr[:], in0=bbr[:], in1=tA[:], op=ALU.subtract))
            dvb(lambda: V.tensor_tensor(out=bbi[:], in0=bi[:], in1=bc(t["fre"][:], 16), op=ALU.mult))
            dvb(lambda: V.tensor_tensor(out=tA[:], in0=br[:], in1=bc(t["fim"][:], 16), op=ALU.mult))
            dvb(lambda: V.tensor_tensor(out=bbi[:], in0=bbi[:], in1=tA[:], op=ALU.add))
            BbA = kb.sb("BbA", [128, 128], F32)
            BswA = kb.sb("BswA", [128, 128], F32)
            fl = lambda x: x.rearrange("p g h -> p (g h)")
            dvb(lambda: V.tensor_copy(out=BbA[0:64, :], in_=fl(bbr[0:64])))
            dvb(lambda: V.tensor_copy(out=BbA[64:128, :], in_=fl(bbi[64:128])))
            dvb(lambda: V.tensor_copy(out=BswA[0:64, :], in_=fl(bbi[0:64])))
            dvb(lambda: V.tensor_scalar(out=BswA[64:128, :], in0=fl(bbr[64:128]), scalar1=-1.0, scalar2=None, op0=ALU.mult))
            kb.op(kb.dve, lambda: V.tensor_scalar(out=c1s[:, 64:128], in0=c1s[:, 64:128], scalar1=-1.0, scalar2=None,
                                                  op0=ALU.mult), r=[c1s_b], w=[c1s_b])
            kb.op(kb.dve, lambda: V.tensor_scalar(out=c2s[:], in0=c2s[:], scalar1=-1.0, scalar2=None, op0=ALU.mult),
                  r=[c2s_b], w=[c2s_b])
            pt = [kb.ps("s5pt", [128, 512]) for _ in range(2)]
            pt_b = [kb.buf("s5pt", excl=True) for _ in range(2)]
            for i, (src, srcb) in enumerate([(BbA, sb), (BswA, sb), (c1s, c1s_b), (c2s, c2s_b)]):
                kb.op(kb.pe, lambda: nc.tensor.transpose(pt[i // 2][:, (i % 2) * 128:(i % 2) * 128 + 128], src[:], idf[:]),
                      r=[srcb, idf_b], w=[pt_b[i // 2]])
            for g in range(8):
                kb.op(kb.dve, lambda: V.tensor_scalar(out=LB[:, g, :], in0=pt[0][:, 0:128], scalar1=rowm[:, g:g + 1],
                                                      scalar2=None, op0=ALU.mult), r=[pt_b[0], rowm_b], w=[par_b])
                kb.op(kb.dve, lambda: V.tensor_scalar(out=LBs[:, g, :], in0=pt[0][:, 128:256], scalar1=rowm[:, g:g + 1],
                                                      scalar2=None, op0=ALU.mult), r=[pt_b[0], rowm_b], w=[par_b])
            kb.op(kb.pool, lambda: nc.gpsimd.memset(LC1[:], 0.0), w=[par_b])
            kb.op(kb.pool, lambda: nc.gpsimd.memset(LC2[:], 0.0), w=[par_b])
            for g in range(8):
                kb.op(kb.dve, lambda: V.tensor_copy(out=LC1[:, g, 16 * g:16 * g + 16], in_=pt[1][:, 16 * g:16 * g + 16]),
                      r=[pt_b[1]], w=[par_b])
                kb.op(kb.dve, lambda: V.tensor_copy(out=LC2[:, g, 16 * g:16 * g + 16],
                                                    in_=pt[1][:, 128 + 16 * g:128 + 16 * g + 16]),
                      r=[pt_b[1]], w=[par_b])
            dv(lambda: V.tensor_copy(out=t["wc"][:], in_=t["cs"][:]))
            dv(lambda: V.tensor_copy(out=t["ws"][:], in_=t["sn"][:]))
            kb.op(kb.pool, lambda: nc.gpsimd.memset(COSt[:, :, 0:1], 1.0), w=[tab_b])
            kb.op(kb.pool, lambda: nc.gpsimd.memset(SINt[:, :, 0:1], 0.0), w=[tab_b])
            x1 = kb.sb("x1", [128, 8, T // 2], F32)
            x2 = kb.sb("x2", [128, 8, T // 2], F32)
            RT = RW + [tab_b]

            def dvt(fn):
                kb.op(kb.dve, fn, r=RT, w=[tab_b, sb])

            n = 1
            while n < T:
                wcb = bc(t["wc"][:], n)
                wsb = bc(t["ws"][:], n)
                dvt(lambda: V.tensor_tensor(out=x1[:, :, 0:n], in0=COSt[:, :, 0:n], in1=wcb, op=ALU.mult))
                dvt(lambda: V.tensor_tensor(out=x2[:, :, 0:n], in0=SINt[:, :, 0:n], in1=wsb, op=ALU.mult))
                dvt(lambda: V.tensor_tensor(out=COSt[:, :, n:2 * n], in0=x1[:, :, 0:n], in1=x2[:, :, 0:n], op=ALU.subtract))
                dvt(lambda: V.tensor_tensor(out=x1[:, :, 0:n], in0=SINt[:, :, 0:n], in1=wcb, op=ALU.mult))
                dvt(lambda: V.tensor_tensor(out=x2[:, :, 0:n], in0=COSt[:, :, 0:n], in1=wsb, op=ALU.mult))
                dvt(lambda: V.tensor_tensor(out=SINt[:, :, n:2 * n], in0=x1[:, :, 0:n], in1=x2[:, :, 0:n], op=ALU.add))
                csq("wc", "ws")
                n *= 2
            tmpR = kb.sb("tmpR", [128, 128], F32)
            for g in range(8):
                kb.op(kb.dve, lambda: V.tensor_scalar(out=tmpR[:], in0=Jm[:], scalar1=t["ws"][:, g:g + 1], scalar2=None,
                                                      op0=ALU.mult), r=[Jm_b, sb], w=[sb])
                kb.op(kb.dve, lambda: V.scalar_tensor_tensor(out=Rm[:, g, :], in0=idf[:], scalar=t["wc"][:, g:g + 1],
                                                             in1=tmpR[:], op0=ALU.mult, op1=ALU.add),
                      r=[idf_b, sb], w=[par_b])
            kb.barrier()
        with kb.scope():
            psA = [kb.ps("psA", [128, 512]) for _ in range(2)]
            psB = [kb.ps("psB", [128, 512]) for _ in range(2)]
            psA_b = [kb.buf("psA", excl=True) for _ in range(2)]
            psB_b = [kb.buf("psB", excl=True) for _ in range(2)]
            psY = [kb.ps("psY", [128, 512]) for _ in range(2)]
            psY_b = [kb.buf("psY", excl=True) for _ in range(2)]
            psI = [kb.ps("psI", [128, 512]) for _ in range(2)]
            psI_b = [kb.buf("psI", excl=True) for _ in range(2)]
            NB = 2
            m1 = [kb.sb("m1", [128, T], F32) for _ in range(NB)]
            m2 = [kb.sb("m2", [128, T], F32) for _ in range(NB)]
            Wt = [kb.sb("Wt", [128, T], F32) for _ in range(NB)]
            NZ = 3
            Z = [kb.sb("Z", [128, T], F32) for _ in range(NZ)]
            ZC = [kb.sb("ZC", [128, T], BF16) for _ in range(NZ)]
            ZS = [kb.sb("ZS", [128, T], BF16) for _ in range(NZ)]
            m1_b = [kb.buf("m1") for _ in range(NB)]
            m2_b = [kb.buf("m2") for _ in range(NB)]
            Wt_b = [kb.buf("Wt") for _ in range(NB)]
            Z_b = [kb.buf("Z") for _ in range(3)]
            ZC_b = [kb.buf("ZC") for _ in range(3)]
            ZS_b = [kb.buf("ZS") for _ in range(3)]
            init = kb.sb("init", [128, 8], F32)
            init_b = [kb.buf("init") for _ in range(8)]
            kb.op(kb.dve, lambda: V.memset(init[:], 0.0), w=init_b)
            ysb = [kb.sb("ysb", [128, T], F32) for _ in range(2)]
            ysb_b = [kb.buf("ysb") for _ in range(2)]
            g1 = [kb.sb("g1", [128, T], F32) for _ in range(2)]
            g1_b = [kb.buf("g1") for _ in range(2)]
            gout = [kb.sb("gout", [128, T], BF16) for _ in range(2)]
            gout_b = [kb.buf("gout") for _ in range(2)]
            items = [(tt, g) for tt in range(NQT) for g in range(8)]

            def stage1(j):
                tt, g = items[j]
                i = j % 2
                tok = slice(tt * T, (tt + 1) * T)
                kb.op(kb.pe, lambda: nc.tensor.matmul(psA[i][:], lhsT=LB[:, g, :], rhs=uT[:, tok], start=True, stop=True),
                      r=[par_b, uT_b[tt]], w=[psA_b[i]])
                kb.op(kb.pe, lambda: nc.tensor.matmul(psB[i][:], lhsT=LBs[:, g, :], rhs=uT[:, tok], start=True, stop=True),
                      r=[par_b, uT_b[tt]], w=[psB_b[i]])
                kb.op(kb.dve, lambda: V.tensor_tensor(out=m1[i][:], in0=psA[i][:], in1=COSt[:, g, :], op=ALU.mult),
                      r=[psA_b[i], tab_b], w=[m1_b[i]])
                kb.op(kb.dve, lambda: V.tensor_tensor(out=m2[i][:], in0=psB[i][:], in1=SINt[:, g, :], op=ALU.mult),
                      r=[psB_b[i], tab_b], w=[m2_b[i]])
                kb.op(kb.pool, lambda: nc.gpsimd.tensor_tensor(out=Wt[i][:], in0=m1[i][:], in1=m2[i][:], op=ALU.add),
                      r=[m1_b[i], m2_b[i]], w=[Wt_b[i]])

            def stage2(j):
                tt, g = items[j]
                i = j % 2
                z = j % 3
                yi = tt % 2
                tok = slice(tt * T, (tt + 1) * T)
                kb.op(kb.dve, lambda: V.tensor_tensor_scan(out=Z[z][:], data0=Rb[:, g:g + 1].broadcast_to([128, T]),
                                                          data1=Wt[i][:], initial=init[:, g:g + 1],
                                                          op0=ALU.mult, op1=ALU.add),
                      r=[Wt_b[i], init_b[g], par_b], w=[Z_b[z]])
                kb.op(kb.pool, lambda: nc.gpsimd.tensor_tensor(out=ZC[z][:], in0=Z[z][:], in1=COSt[:, g, :], op=ALU.mult),
                      r=[Z_b[z], tab_b], w=[ZC_b[z]])
                kb.op(kb.dve, lambda: V.tensor_tensor(out=ZS[z][:], in0=Z[z][:], in1=SINt[:, g, :], op=ALU.mult),
                      r=[Z_b[z], tab_b], w=[ZS_b[z]])

            def stage3(j):
                tt, g = items[j]
                z = j % 3
                yi = tt % 2
                tok = slice(tt * T, (tt + 1) * T)
                if tt < NQT - 1:
                    ii = g % 2
                    kb.op(kb.pe, lambda: nc.tensor.matmul(psI[ii][:, g:g + 1], lhsT=Rm[:, g, :], rhs=Z[z][:, T - 1:T],
                                                          start=True, stop=True),
                          r=[par_b, Z_b[z]], w=[psI_b[ii]])
                    kb.op(kb.act, lambda: nc.scalar.copy(out=init[:, g:g + 1], in_=psI[ii][:, g:g + 1]),
                          r=[psI_b[ii]], w=[init_b[g]])
                kb.op(kb.pe, lambda: nc.tensor.matmul(psY[yi][:], lhsT=LC1[:, g, :], rhs=ZC[z][:], start=(g == 0), stop=False),
                      r=[par_b, ZC_b[z]], w=[psY_b[yi]], sig=False)
                kb.op(kb.pe, lambda: nc.tensor.matmul(psY[yi][:], lhsT=LC2[:, g, :], rhs=ZS[z][:], start=False, stop=(g == 7)),
                      r=[par_b, ZS_b[z]], w=[psY_b[yi]], sig=True)
                if g == 7:
                    kb.op(kb.dve, lambda: V.scalar_tensor_tensor(out=ysb[yi][:], in0=uT[:, tok], scalar=d_t[:, 0:1], in1=psY[yi][:],
                                                                 op0=ALU.mult, op1=ALU.add),
                          r=[uT_b[tt], d_tb, psY_b[yi]], w=[ysb_b[yi]])
                    kb.op(kb.act, lambda: nc.scalar.activation(out=g1[yi][:], in_=ysb[yi][:], func=AF.Square),
                          r=[ysb_b[yi]], w=[g1_b[yi]])
                    kb.op(kb.act, lambda: nc.scalar.activation(out=g1[yi][:], in_=g1[yi][:], func=AF.Identity, scale=0.044715,
                                                               bias=c["one"][:]), r=[g1_b[yi], c["one_b"]], w=[g1_b[yi]])
                    kb.op(kb.pool, lambda: nc.gpsimd.tensor_tensor(out=g1[yi][:], in0=g1[yi][:], in1=ysb[yi][:], op=ALU.mult),
                          r=[ysb_b[yi]], w=[g1_b[yi]])
                    kb.op(kb.act, lambda: nc.scalar.activation(out=g1[yi][:], in_=g1[yi][:], func=AF.Sigmoid,
                                                               scale=2.0 * math.sqrt(2.0 / PI)), r=[g1_b[yi]], w=[g1_b[yi]])
                    kb.op(kb.pool, lambda: nc.gpsimd.tensor_tensor(out=gout[yi][:], in0=g1[yi][:], in1=ysb[yi][:], op=ALU.mult),
                          r=[g1_b[yi], ysb_b[yi]], w=[gout_b[yi]])
                    kb.dma(kb.sp, mix_ap(d, "mixB", 0, 128, tt * T, T), gout[yi][:], r=[gout_b[yi]], w=[d["mixB_b"]],
                           sembuf=gout_b[yi])

            n_it = len(items)
            stage1(0)
            if n_it > 1:
                stage1(1)
            stage2(0)
            for j in range(n_it):
                if j + 2 < n_it:
                    stage1(j + 2)
                if j + 1 < n_it:
                    stage2(j + 1)
                stage3(j)
            kb.barrier()


NORM_EPS = 1e-6


def emit_mix_out(kb, c, x_d, x_db, xo_d, xo_db, mixF_d, mixF_db, col0, chunk_rows, wout_d, NT, glu=None, idx_d=None):
    nc = kb.nc
    ntt = NT // 512
    with kb.scope():
        wout, wout_b = load_cast(kb, "wout", [128, 8, 1024], wout_d)
        if glu is not None:
            wglu, wglu_b = load_cast(kb, "wglu", [128, 4, 512], glu["w"])
            bglu, bglu_b = load_plain(kb, "bglu", [128, 4], glu["b"])
        mt = [kb.sb("mt", [128, 8, 512], BF16) for _ in range(2)]
        mt_b = [kb.buf("mt") for _ in range(2)]
        if idx_d is not None:
            U32 = mybir.dt.uint32
            idx_t, idx_b = load_plain(kb, "idx", [128, 8], idx_d, U32)
            mtall = kb.sb("mtall", [128, 8, NT], BF16)
            mtall_b = kb.buf("mtall")
            m4 = mixF_d
            for k in range(8):
                kb.gather_rows(mtall[:, k, :], mtall_b, m4, mixF_db, idx_t[:, k:k + 1], idx_b)
        so = [kb.sb("so", [128, 4, 512], BF16) for _ in range(2)]
        so_b = [kb.buf("so") for _ in range(2)]
        sg = [kb.sb("sgm", [128, 512], F32) for _ in range(2)]
        sg_b = [kb.buf("sgm") for _ in range(2)]
        xt = [kb.sb("xt", [128, 8, 512], F32) for _ in range(2)]
        xt_b = [kb.buf("xt") for _ in range(2)]
        ps = [kb.ps("mo", [128, 512]) for _ in range(4)]
        ps_b = [kb.buf("mo", excl=True) for _ in range(4)]
        pc = 0
        xv = x_d.rearrange("(k p) t -> p k t", p=128)
        xov = xo_d.rearrange("(k p) t -> p k t", p=128)
        for tt in range(ntt):
            i = tt % 2
            c0 = col0 + tt * 512
            if idx_d is not None:
                kb.op(kb.pool, lambda: nc.gpsimd.tensor_copy(out=mt[i][:], in_=mtall[:, :, tt * 512:(tt + 1) * 512]),
                      r=[mtall_b], w=[mt_b[i]])
            else:
                for k in range(8):
                    r0 = chunk_rows[k]
                    kb.dma(kb.sp, mt[i][:, k, :], mixF_d[r0:r0 + 128, c0:c0 + 512], r=[mixF_db], w=[mt_b[i]])
            kb.dma(kb.sp, xt[i][:], xv[:, :, tt * 512:(tt + 1) * 512], r=[x_db], w=[xt_b[i]])
            if glu is not None:
                for j in range(4):
                    p_, pb = ps[pc % 4], ps_b[pc % 4]
                    si = pc % 2
                    pc += 1
                    for k in range(4):
                        kb.op(kb.pe, lambda: nc.tensor.matmul(p_[:], lhsT=wglu[:, k, j * 128:(j + 1) * 128], rhs=mt[i][:, k, :],
                                                              start=(k == 0), stop=(k == 3)),
                              r=[wglu_b, mt_b[i]], w=[pb], sig=(k == 3))
                    kb.op(kb.act, lambda: nc.scalar.activation(out=sg[si][:], in_=p_[:], func=AF.Sigmoid, bias=bglu[:, j:j + 1]),
                          r=[pb, bglu_b], w=[sg_b[si]])
                    kb.op(kb.dve, lambda: nc.vector.tensor_tensor(out=so[i][:, j, :], in0=mt[i][:, j, :], in1=sg[si][:], op=ALU.mult),
                          r=[sg_b[si], mt_b[i]], w=[so_b[i]])
            for m in range(8):
                p_, pb = ps[pc % 4], ps_b[pc % 4]
                pc += 1
                for k in range(8):
                    if glu is not None and k < 4:
                        rhs, rb = so[i][:, k, :], so_b[i]
                    else:
                        rhs, rb = mt[i][:, k, :], mt_b[i]
                    kb.op(kb.pe, lambda: nc.tensor.matmul(p_[:], lhsT=wout[:, k, m * 128:(m + 1) * 128], rhs=rhs,
                                                          start=(k == 0), stop=(k == 7)),
                          r=[wout_b, rb], w=[pb], sig=(k == 7))
                kb.op(kb.dve, lambda: nc.vector.tensor_tensor(out=xt[i][:, m, :], in0=p_[:], in1=xt[i][:, m, :], op=ALU.add),
                      r=[pb], w=[xt_b[i]])
            kb.dma(kb.sp, xov[:, :, tt * 512:(tt + 1) * 512], xt[i][:], r=[xt_b[i]], w=[xo_db], sembuf=xt_b[i])
        kb.barrier()


def emit_launchD(kb, c, d, lam_init):
    nc = kb.nc
    V_ = nc.vector
    scale = 64 ** -0.5
    with kb.scope():
        QT = [kb.sb("QT", [128, 2 * S], BF16) for _ in range(2)]
        KT = [kb.sb("KT", [128, S], BF16) for _ in range(2)]
        QT_b = [[kb.buf("QT") for _ in range(NQT)] for _ in range(2)]
        KT_b = [[kb.buf("KT") for _ in range(NQT)] for _ in range(2)]
        for h_ in range(2):
            kb.op(kb.pool, lambda: nc.gpsimd.memset(QT[h_][:], 0.0), w=QT_b[h_])
        Vt = kb.sb("V", [128, S // 128, 2, 129], BF16)
        V_b = [kb.buf("V") for _ in range(NQT)]
        kb.op(kb.pool, lambda: nc.gpsimd.memset(Vt[:], 1.0), w=V_b)
        neglam = kb.sb("neglam", [128, 1], F32)
        gsub = kb.sb("gsub", [128, 128], F32)
        lam_b = kb.buf("lam")
        with kb.scope():
            lq, lq_b = load_plain(kb, "lq", [128, 4, 64], d["lam_in"])
            pr = kb.sb("pr", [128, 2, 64], F32)
            sm = kb.sb("sm", [128, 2], F32)
            kb.op(kb.dve, lambda: V_.tensor_tensor(out=pr[:, 0, :], in0=lq[:, 0, :], in1=lq[:, 1, :], op=ALU.mult), r=[lq_b], w=[lam_b])
            kb.op(kb.dve, lambda: V_.tensor_tensor(out=pr[:, 1, :], in0=lq[:, 2, :], in1=lq[:, 3, :], op=ALU.mult), r=[lq_b], w=[lam_b])
            kb.op(kb.dve, lambda: V_.tensor_reduce(out=sm[:], in_=pr[:], axis=AX.X, op=ALU.add), r=[lam_b], w=[lam_b])
            kb.op(kb.act, lambda: nc.scalar.activation(out=sm[:], in_=sm[:], func=AF.Exp), r=[lam_b], w=[lam_b])
            kb.op(kb.dve, lambda: V_.tensor_tensor(out=neglam[:], in0=sm[:, 1:2], in1=sm[:, 0:1], op=ALU.subtract), r=[lam_b], w=[lam_b])
            kb.op(kb.dve, lambda: V_.tensor_scalar(out=neglam[:], in0=neglam[:], scalar1=-lam_init, scalar2=None, op0=ALU.add),
                  r=[lam_b], w=[lam_b])
            kb.dma(kb.sp, gsub[:], d["subln"], w=[lam_b])
            kb.op(kb.dve, lambda: V_.tensor_scalar(out=gsub[:], in0=gsub[:], scalar1=1.0 - lam_init, scalar2=None, op0=ALU.mult),
                  r=[lam_b], w=[lam_b])
            kb.barrier()
        with kb.scope():
            wq, wq_b = load_cast(kb, "wq", [128, 8, 256], d["w_q"])
            wqs, wqs_b = load_cast(kb, "wqs", [128, 8, 256], d["w_q_sw"])
            wk, wk_b = load_cast(kb, "wk", [128, 8, 256], d["w_k"])
            wks, wks_b = load_cast(kb, "wks", [128, 8, 256], d["w_k_sw"])
            wv, wv_b = load_cast(kb, "wv", [128, 8, 256], d["w_v"])
            invc, invc_b = load_plain(kb, "invc", [128, 1], d["rope_inv"])
            sgnc, sgnc_b = load_plain(kb, "sgnc", [128, 1], d["rope_sgn"])
            scr, scr_b = rope_scratch(kb)
            COS = kb.sb("COS", [128, 512], F32)
            SINS = kb.sb("SINS", [128, 512], F32)
            tb = {"cos": kb.buf("COS"), "sin": kb.buf("SINS")}
            hin = [kb.sb("hin", [128, 8, 512], BF16) for _ in range(2)]
            hin_b = [kb.buf("hin") for _ in range(2)]
            m1 = [kb.sb("m1", [128, 512], F32) for _ in range(2)]
            m1_b = [kb.buf("m1") for _ in range(2)]
            m2 = [kb.sb("m2", [128, 512], F32) for _ in range(2)]
            m2_b = [kb.buf("m2") for _ in range(2)]
            PS = [kb.ps("PS", [128, 512]) for _ in range(6)]
            PS_b = [kb.buf("PS", excl=True) for _ in range(6)]
            pctr = [0]
            mctr = [0]

            def nextps():
                i = pctr[0] % 6
                pctr[0] += 1
                return PS[i], PS_b[i]

            for tt in range(NQT):
                hi = tt % 2
                tok = slice(tt * 512, (tt + 1) * 512)
                rr, cc0 = tt // 4, (tt % 4) * 512
                load_hin(kb, d, hin[hi], hin_b[hi], tt)
                emit_rope_tables(kb, c, d["pos"], d["pos_b"], tt, invc, invc_b, sgnc, sgnc_b, COS, SINS, tb, scr, scr_b)
                for h in range(2):
                    for (w_, w_b, ws_, ws_b, dst, dst_b) in [(wq, wq_b, wqs, wqs_b, None, QT_b), (wk, wk_b, wks, wks_b, KT, KT_b)]:
                        ps_a, psb_a = nextps()
                        for k in range(8):
                            kb.op(kb.pe, lambda: nc.tensor.matmul(ps_a[:], lhsT=w_[:, k, h * 128:(h + 1) * 128], rhs=hin[hi][:, k, :],
                                                                  start=(k == 0), stop=(k == 7)),
                                  r=[w_b, hin_b[hi]], w=[psb_a], sig=(k == 7))
                        ps_s, psb_s = nextps()
                        for k in range(8):
                            kb.op(kb.pe, lambda: nc.tensor.matmul(ps_s[:], lhsT=ws_[:, k, h * 128:(h + 1) * 128], rhs=hin[hi][:, k, :],
                                                                  start=(k == 0), stop=(k == 7)),
                                  r=[ws_b, hin_b[hi]], w=[psb_s], sig=(k == 7))
                        mi = mctr[0] % 2
                        mctr[0] += 1
                        kb.op(kb.dve, lambda: V_.tensor_tensor(out=m1[mi][:], in0=ps_a[:], in1=COS[:], op=ALU.mult),
                              r=[psb_a, tb["cos"]], w=[m1_b[mi]])
                        kb.op(kb.dve, lambda: V_.tensor_tensor(out=m2[mi][:], in0=ps_s[:], in1=SINS[:], op=ALU.mult),
                              r=[psb_s, tb["sin"]], w=[m2_b[mi]])
                        if dst is None:
                            for mm in range(2):
                                rws = slice(mm * 64, (mm + 1) * 64)
                                qv = QT[h][rws, tt * 1024:(tt + 1) * 1024].rearrange("p (t m q) -> p t m q", t=2, m=2)[:, :, mm, :]
                                kb.op(kb.pool, lambda: nc.gpsimd.tensor_tensor(
                                    out=qv, in0=m1[mi][rws, :].rearrange("p (t q) -> p t q", t=2),
                                    in1=m2[mi][rws, :].rearrange("p (t q) -> p t q", t=2), op=ALU.add),
                                    r=[m1_b[mi], m2_b[mi]], w=[dst_b[h][tt]])
                        else:
                            kb.op(kb.pool, lambda: nc.gpsimd.tensor_tensor(out=dst[h][:, tok], in0=m1[mi][:], in1=m2[mi][:], op=ALU.add),
                                  r=[m1_b[mi], m2_b[mi]], w=[dst_b[h][tt]])
                for s4 in range(4):
                    ps, psb = nextps()
                    for k in range(8):
                        kb.op(kb.pe, lambda: nc.tensor.matmul(ps[:, 0:256], lhsT=hin[hi][:, k, s4 * 128:(s4 + 1) * 128], rhs=wv[:, k, :],
                                                              start=(k == 0), stop=(k == 7)),
                              r=[wv_b, hin_b[hi]], w=[psb], sig=(k == 7))
                    kb.op(kb.act, lambda: nc.scalar.copy(out=Vt[:, tt * 4 + s4, :, 0:128],
                                                         in_=ps[:, 0:256].rearrange("p (h e) -> p h e", h=2)),
                          r=[psb], w=[V_b[tt]])
            kb.barrier()
        with kb.scope():
            on = [kb.sb("on", [128, 2, 256], BF16) for _ in range(2)]
            on_b = [kb.buf("on") for _ in range(2)]
            rec = kb.sb("rec", [128, 4], F32)
            t1 = kb.sb("t1", [128, 128], F32)
            o_ = kb.sb("o_", [128, 128], F32)
            junk = kb.sb("junk", [128, 128], F32)
            ssq = kb.sb("ssq", [128, 1], F32)
            fb = kb.buf("fin")
            psT = kb.ps("psT", [128, 1024], BF16)
            psT_b = kb.buf("psT", excl=True)
            oT = [kb.sb("oT", [128, 2, 256], BF16) for _ in range(2)]
            oT_b = [kb.buf("oT") for _ in range(2)]

            def finalize(h, qt, getO):
                i = qt % 2
                for s in range(2):
                    O1, O1b = getO(0, s)
                    O2, O2b = getO(1, s)
                    kb.op(kb.dve, lambda: V_.reciprocal(out=rec[:, 0:1], in_=O1[:, 128:129]), r=[O1b], w=[fb])
                    kb.op(kb.dve, lambda: V_.reciprocal(out=rec[:, 1:2], in_=O2[:, 128:129]), r=[O2b], w=[fb])
                    kb.op(kb.dve, lambda: V_.tensor_tensor(out=rec[:, 1:2], in0=rec[:, 1:2], in1=neglam[:], op=ALU.mult),
                          r=[fb, lam_b], w=[fb])
                    kb.op(kb.dve, lambda: V_.tensor_scalar(out=t1[:], in0=O1[:, 0:128], scalar1=rec[:, 0:1], scalar2=None,
                                                           op0=ALU.mult), r=[O1b, fb], w=[fb])
                    kb.op(kb.dve, lambda: V_.scalar_tensor_tensor(out=o_[:], in0=O2[:, 0:128], scalar=rec[:, 1:2], in1=t1[:],
                                                                  op0=ALU.mult, op1=ALU.add), r=[O2b, fb], w=[fb])
                    kb.op(kb.dve, lambda: V_.scalar_tensor_tensor(out=junk[:], in0=o_[:], scalar=1.0, in1=o_[:],
                                                                  op0=ALU.mult, op1=ALU.mult, accum_out=ssq[:]),
                          r=[fb], w=[fb])
                    kb.op(kb.act, lambda: nc.scalar.activation(out=ssq[:], in_=ssq[:], func=AF.Ln, bias=c["eps"][:],
                                                               scale=1.0 / 128), r=[fb, c["eps_b"]], w=[fb])
                    kb.op(kb.act, lambda: nc.scalar.activation(out=ssq[:], in_=ssq[:], func=AF.Exp, scale=-0.5),
                          r=[fb], w=[fb])
                    kb.op(kb.dve, lambda: V_.scalar_tensor_tensor(out=on[i][:, s, h * 128:(h + 1) * 128], in0=o_[:],
                                                                  scalar=ssq[:, 0:1], in1=gsub[:], op0=ALU.mult, op1=ALU.mult),
                          r=[fb, lam_b], w=[on_b[i]])
                if h == 1:
                    for h2 in range(2):
                        for s in range(2):
                            kb.op(kb.pe, lambda: nc.tensor.transpose(psT[:, (h2 * 2 + s) * 128:(h2 * 2 + s + 1) * 128],
                                                                     on[i][:, s, h2 * 128:(h2 + 1) * 128], c["ident_bf"][:]),
                                  r=[on_b[i], c["ident_b"]], w=[psT_b])
                    kb.op(kb.act, lambda: nc.scalar.copy(out=oT[i][:].rearrange("p h q -> p (h q)"), in_=psT[:, 0:512]),
                          r=[psT_b], w=[oT_b[i]])
                    for h2 in range(2):
                        kb.dma(kb.sp, mix_ap(d, "mixD", h2 * 128, (h2 + 1) * 128, qt * 256, 256), oT[i][:, h2, :],
                               r=[oT_b[i]], w=[d["mixD_b"]], sembuf=oT_b[i])
                    if d.get("chunk_done") is not None and (qt + 1) % 8 == 0:
                        d["chunk_done"](qt // 8)

            emit_attention(kb, c, 2, 128, 2, 128, QT, QT_b, KT, KT_b, Vt, V_b, scale, finalize, QTILE=256, nob=2, nst=3, npt=4)


def kp(w, k):
    n = w.shape[1]
    return np.ascontiguousarray(w.reshape(k, 128, n).transpose(1, 0, 2))

def swap_halves(w, start, half):
    w = w.copy()
    a = w[:, start:start + half].copy()
    w[:, start:start + half] = w[:, start + half:start + 2 * half]
    w[:, start + half:start + 2 * half] = a
    return w

def mla_rope_consts():
    half = 16
    inv = (10000.0 ** (-np.arange(half, dtype=np.float32) * 2.0 / 32)).astype(np.float32)
    invc = np.zeros((128, 1), np.float32); sgn = np.zeros((128, 1), np.float32)
    invc[64:80, 0] = inv; invc[80:96, 0] = inv
    sgn[64:80, 0] = -1.0; sgn[80:96, 0] = 1.0
    return invc, sgn

def prep_launchB(inp, hp):
    w_in = inp["ev_w_in"][0]
    u = w_in[:, 128 * hp:128 * hp + 128]
    cq = w_in[:, 512:896]; ckv = w_in[:, 896:1152]; kr = w_in[:, 1152:1184]
    d = {}
    d["w_inA"] = kp(np.concatenate([u, cq, ckv, kr], 1), 8)
    d["w_kr"] = kp(kr, 8)
    d["w_krsw"] = kp(swap_halves(kr, 0, 16), 8)
    wuq = inp["mla_w_uq"][0]
    my = wuq[:, 192 * hp:192 * hp + 192]
    mys = swap_halves(swap_halves(my, 64, 16), 96 + 64, 16)
    d["w_uq"] = kp(my, 3); d["w_uq_sw"] = kp(mys, 3)
    wukv = inp["mla_w_ukv"][0]
    h0, h1 = 2 * hp, 2 * hp + 1
    d["w_ukv_k"] = kp(np.concatenate([wukv[:, 128 * h0:128 * h0 + 64], wukv[:, 128 * h1:128 * h1 + 64]], 1), 2)
    d["w_ukv_v"] = kp(np.concatenate([wukv[:, 128 * h0 + 64:128 * h0 + 128], wukv[:, 128 * h1 + 64:128 * h1 + 128]], 1), 2)
    g = np.concatenate([inp["mla_q_norm"][0], inp["mla_kv_norm"][0]])
    d["lat_g"] = np.ascontiguousarray(g.reshape(5, 128).T)
    d["rope_inv"], d["rope_sgn"] = mla_rope_consts()
    return d

def consts():
    tri = (np.arange(128)[:, None] <= np.arange(128)[None, :]).astype(np.float32)
    return {"tri": tri, "ident": np.eye(128, dtype=np.float32)}

def prep_s5(inp, hp):
    gs = slice(8 * hp, 8 * hp + 8)
    d = {}
    def dup(a):
        return np.ascontiguousarray(np.concatenate([a.T, a.T], 0))
    d["s5_lr"] = dup(inp["s5_lambda_re"][0, gs])
    d["s5_li"] = dup(inp["s5_lambda_im"][0, gs])
    d["s5_logdt"] = np.ascontiguousarray(np.broadcast_to(inp["s5_log_dt"][0, gs][None, :], (128, 8))).astype(np.float32)
    def dupb(b):
        x = b.transpose(1, 0, 2)
        return np.ascontiguousarray(np.concatenate([x, x], 0))
    d["s5_bre"] = dupb(inp["s5_b_re"][0, gs]); d["s5_bim"] = dupb(inp["s5_b_im"][0, gs])
    cre = inp["s5_c_re"][0, gs].reshape(128, 64); cim = inp["s5_c_im"][0, gs].reshape(128, 64)
    d["s5_c1src"] = np.ascontiguousarray(np.concatenate([cre, cim], 1))
    d["s5_c2src"] = np.ascontiguousarray(np.concatenate([cim, cre], 1))
    d["s5_d"] = np.ascontiguousarray(inp["s5_d"][0, gs].reshape(128, 1))
    rm = np.zeros((128, 8), np.float32)
    for g in range(8): rm[16 * g:16 * g + 16, g] = 1
    d["rowmask"] = rm
    J = np.zeros((128, 128), np.float32)
    for p in range(64):
        J[p, 64 + p] = 1.0; J[64 + p, p] = -1.0
    d["Jmat"] = J
    return d

def diff_rope_consts():
    half = 8
    inv = (500000.0 ** (-np.arange(half, dtype=np.float32) * 2.0 / 16)).astype(np.float32)
    invc = np.zeros((128, 1), np.float32); sgn = np.zeros((128, 1), np.float32)
    for m in range(2):
        invc[m * 64:m * 64 + 8, 0] = inv; invc[m * 64 + 8:m * 64 + 16, 0] = inv
        sgn[m * 64:m * 64 + 8, 0] = -1.0; sgn[m * 64 + 8:m * 64 + 16, 0] = 1.0
    return invc, sgn

def prep_launchD(inp, hp):
    w = inp["od_w_in"][0]
    q = w[:, 256 * hp:256 * hp + 256]; k = w[:, 1024 + 256 * hp:1024 + 256 * hp + 256]; v = w[:, 2048 + 256 * hp:2048 + 256 * hp + 256]
    def sw(a):
        a = a.copy()
        for b0 in range(0, 256, 64):
            a = swap_halves(a, b0, 8)
        return a
    d = {"w_q": kp(q, 8), "w_q_sw": kp(sw(q), 8), "w_k": kp(k, 8), "w_k_sw": kp(sw(k), 8), "w_v": kp(v, 8)}
    d["rope_inv"], d["rope_sgn"] = diff_rope_consts()
    lam = np.stack([inp["diff_lq1"][0], inp["diff_lk1"][0], inp["diff_lq2"][0], inp["diff_lk2"][0]])
    d["lam_in"] = np.ascontiguousarray(np.broadcast_to(lam[None], (128, 4, 64))).astype(np.float32)
    d["subln"] = np.ascontiguousarray(np.broadcast_to(inp["diff_subln"][0][None], (128, 128))).astype(np.float32)
    return d


def arrange_wgu(w):
    g = w[:, :2816].reshape(8, 128, 11, 256)
    u = w[:, 2816:].reshape(8, 128, 11, 256)
    gu = np.concatenate([g, u], axis=-1)
    return np.ascontiguousarray(gu.transpose(2, 1, 0, 3))

def arrange_wd(w):
    return np.ascontiguousarray(w.reshape(22, 128, 4, 256).transpose(2, 1, 0, 3))


import ml_dtypes as _mld
_BF = _mld.bfloat16
NT = 2048
LAM_INIT = 0.8 - 0.6 * math.exp(-0.3 * 1)


def _common_consts(kb, d):
    nc = kb.nc
    c = emit_consts(kb)
    c["eps"] = kb.sb("eps", [128, 1], F32)
    c["eps_b"] = kb.buf("eps")
    kb.op(kb.dve, lambda: nc.vector.memset(c["eps"][:], NORM_EPS), w=[c["eps_b"]])
    if "tri" in d:
        c["tri"], c["tri_b"] = load_cast(kb, "tri", [128, 128], d["tri"])
        c["ident_bf"], c["ident_b"] = load_cast(kb, "ident", [128, 128], d["ident"])
    return c


def _din(kb, d, name, shape, dt=F32):
    d[name] = kb.nc.dram_tensor(name, shape, dt, kind="ExternalInput").ap()
    d[name + "_b"] = kb.buf(name)


def _dout(kb, d, name, shape, dt=F32):
    d[name] = kb.nc.dram_tensor(name, shape, dt, kind="ExternalOutput").ap()
    d[name + "_b"] = kb.buf(name)


def _ffn_inputs(kb, d, sfx):
    _din(kb, d, "g" + sfx, [128, 8])
    _din(kb, d, "wgu" + sfx, [11, 128, 8, 512])
    _din(kb, d, "wd" + sfx, [4, 128, 22, 256])


def emit_norm_out(kb, c, x_d, x_db, g_d, out_d, out_db, dt):
    with kb.scope():
        hT = kb.sb("hTo", [128, 8, NT], dt)
        hT_b = [kb.buf("hTo") for _ in range(NT // 512)]
        emit_norm(kb, c, x_d, x_db, g_d, hT, hT_b, NT)
        ov = out_d.rearrange("(k p) t -> p k t", p=128)
        for tt in range(NT // 512):
            kb.dma(kb.sp, ov[:, :, tt * 512:(tt + 1) * 512], hT[:, :, tt * 512:(tt + 1) * 512], r=[hT_b[tt]], w=[out_db],
                   sembuf=hT_b[tt])
        kb.barrier()


_B_IN = [("w_inA", [128, 8, 800]), ("w_kr", [128, 8, 32]), ("w_krsw", [128, 8, 32]), ("w_uq", [128, 3, 192]),
         ("w_uq_sw", [128, 3, 192]), ("w_ukv_k", [128, 2, 128]), ("w_ukv_v", [128, 2, 128]), ("lat_g", [128, 5]),
         ("rope_inv", [128, 1]), ("rope_sgn", [128, 1]), ("tri", [128, 128]), ("ident", [128, 128]),
         ("s5_lr", [128, 8]), ("s5_li", [128, 8]), ("s5_logdt", [128, 8]), ("s5_bre", [128, 8, 16]), ("s5_bim", [128, 8, 16]),
         ("s5_c1src", [128, 128]), ("s5_c2src", [128, 128]), ("s5_d", [128, 1]), ("rowmask", [128, 8]), ("Jmat", [128, 128])]


_D_IN = [("w_q", [128, 8, 256]), ("w_q_sw", [128, 8, 256]), ("w_k", [128, 8, 256]), ("w_k_sw", [128, 8, 256]),
         ("w_v", [128, 8, 256]), ("rope_inv", [128, 1]), ("rope_sgn", [128, 1]), ("lam_in", [128, 4, 64]), ("subln", [128, 128])]
GROUPS = [[0, 1, 2, 3], [4, 5, 6, 7]]


def _dint(kb, d, name, shape, dt=F32):
    d[name] = kb.nc.dram_tensor(name, shape, dt, kind="Internal").ap()
    d[name + "_b"] = kb.buf(name)


def build_fused():
    kb = KB(); d = {}
    U32 = mybir.dt.uint32
    _din(kb, d, "xT", [1024, NT]); _din(kb, d, "pos", [S], I32)
    for sfx in "0123":
        _ffn_inputs(kb, d, sfx)
    for n in ["g_ev", "g_od", "g_fin"]:
        _din(kb, d, n, [128, 8])
    for n, sh in _B_IN:
        _din(kb, d, n, sh)
    dD = {}
    for n, sh in _D_IN:
        _din(kb, d, "D_" + n, sh)
        dD[n] = d["D_" + n]
    _din(kb, d, "wout0", [128, 8, 1024]); _din(kb, d, "wout1", [128, 8, 1024])
    _din(kb, d, "wglu", [128, 4, 512]); _din(kb, d, "bglu", [128, 4])
    _din(kb, d, "idxC", [128, 8], U32); _din(kb, d, "idxE", [128, 8], U32)
    for n in ["x1T", "x2T", "x3T", "x4T", "x5T", "x6T"]:
        _dint(kb, d, n, [1024, NT])
    _dint(kb, d, "hT", [1024, NT], BF16); _dint(kb, d, "h2T", [1024, NT], BF16)
    for q in range(4):
        _dint(kb, d, f"hTg{q}", [1024, NT], BF16); _dint(kb, d, f"h2Tg{q}", [1024, NT], BF16)
        _dint(kb, d, f"mixB{q}", [256, NT], BF16); _dint(kb, d, f"mixD{q}", [256, NT], BF16)
    _dint(kb, d, "mixF", [4096, NT], BF16); _dint(kb, d, "mixF2", [4096, NT], BF16)
    d["mixB_b"] = kb.buf("mixB"); d["mixD_b"] = kb.buf("mixD")
    d["mixB_chunks"] = [d[f"mixB{q}"] for q in range(4)]
    _dout(kb, d, "outT", [1024, NT])
    c = _common_consts(kb, d)
    emit_ffn(kb, c, d["xT"], d["xT_b"], d["x1T"], d["x1T_b"], d["g0"], d["wgu0"], d["wd0"], NT)
    emit_norm_out(kb, c, d["x1T"], d["x1T_b"], d["g_ev"], d["hT"], d["hT_b"], BF16)
    for q in range(4):
        kb.allgather(d["hT"][256 * q:256 * (q + 1), :], d["hT_b"], d[f"hTg{q}"], d[f"hTg{q}_b"], GROUPS)
    d["hTg"] = [d[f"hTg{q}"] for q in range(4)]
    d["hTg_b"] = [d[f"hTg{q}_b"] for q in range(4)]
    with kb.scope():
        uT = kb.sb("uT", [128, S], BF16)
        d["uT"] = uT
        d["uT_b"] = [kb.buf("uT") for _ in range(NQT)]
        d["mid_hook"] = lambda: emit_s5(kb, c, d, uT, d["uT_b"])
        d["chunk_done"] = lambda j: kb.allgather(d[f"mixB{j}"], d["mixB_b"], d["mixF"][1024 * j:1024 * (j + 1), :],
                                                 d["mixF_b"], GROUPS)
        emit_launchB_mla(kb, c, d)
    emit_mix_out(kb, c, d["x1T"], d["x1T_b"], d["x2T"], d["x2T_b"], d["mixF"], d["mixF_b"], 0, None, d["wout0"], NT,
                 glu={"w": d["wglu"], "b": d["bglu"]}, idx_d=d["idxC"])
    emit_ffn(kb, c, d["x2T"], d["x2T_b"], d["x3T"], d["x3T_b"], d["g1"], d["wgu1"], d["wd1"], NT)
    emit_ffn(kb, c, d["x3T"], d["x3T_b"], d["x4T"], d["x4T_b"], d["g2"], d["wgu2"], d["wd2"], NT)
    emit_norm_out(kb, c, d["x4T"], d["x4T_b"], d["g_od"], d["h2T"], d["h2T_b"], BF16)
    for q in range(4):
        kb.allgather(d["h2T"][256 * q:256 * (q + 1), :], d["h2T_b"], d[f"h2Tg{q}"], d[f"h2Tg{q}_b"], GROUPS)
    dD.update({"hTg": [d[f"h2Tg{q}"] for q in range(4)], "hTg_b": [d[f"h2Tg{q}_b"] for q in range(4)],
               "pos": d["pos"], "pos_b": d["pos_b"], "mixD_chunks": [d[f"mixD{q}"] for q in range(4)], "mixD_b": d["mixD_b"]})
    dD["chunk_done"] = lambda j: kb.allgather(d[f"mixD{j}"], d["mixD_b"], d["mixF2"][1024 * j:1024 * (j + 1), :],
                                              d["mixF2_b"], GROUPS)
    emit_launchD(kb, c, dD, LAM_INIT)
    emit_mix_out(kb, c, d["x4T"], d["x4T_b"], d["x5T"], d["x5T_b"], d["mixF2"], d["mixF2_b"], 0, None, d["wout1"], NT,
                 idx_d=d["idxE"])
    emit_ffn(kb, c, d["x5T"], d["x5T_b"], d["x6T"], d["x6T_b"], d["g3"], d["wgu3"], d["wd3"], NT)
    emit_norm_out(kb, c, d["x6T"], d["x6T_b"], d["g_fin"], d["outT"], d["outT_b"], F32)
    kb.finish([])
    return kb.nc


def _gT(g):
    return np.ascontiguousarray(np.asarray(g, np.float32).reshape(8, 128).T)


def _ffn_host(inp, l, j, sfx):
    return {"g" + sfx: _gT(inp["ffn_norm"][l, j]), "wgu" + sfx: arrange_wgu(inp["ffn_w_gu"][l, j]),
            "wd" + sfx: arrange_wd(inp["ffn_w_down"][l, j])}


def _run(nc, in_maps):
    res = run_bass_kernel_spmd(nc, in_maps, core_ids=list(range(8)))
    return [{k: np.asarray(v) for k, v in r.items()} for r in res.results]


def _idx(rows, r):
    a = np.zeros((128, 8), np.uint32)
    for k in range(8):
        a[:, k] = r * 1024 + rows[k] + np.arange(128)
    return a


def kernel(**inputs):
    inp = {k: np.asarray(v) for k, v in inputs.items()}
    x = inp["x"].astype(np.float32)
    pos = inp["positions"].astype(np.int32)
    cs = consts()
    shared = {}
    for (l, j, sfx) in [(0, 0, "0"), (0, 1, "1"), (1, 0, "2"), (1, 1, "3")]:
        shared.update(_ffn_host(inp, l, j, sfx))
    shared["g_ev"] = _gT(inp["ev_norm"][0]); shared["g_od"] = _gT(inp["od_norm"][0]); shared["g_fin"] = _gT(inp["final_norm"])
    shared["wout0"] = kp(inp["ev_w_out"][0], 8); shared["wout1"] = kp(inp["od_w_out"][0], 8)
    shared["wglu"] = kp(inp["s5_w_glu"][0], 4)
    shared["bglu"] = np.ascontiguousarray(inp["s5_b_glu"][0].reshape(4, 128).T)
    shared.update(cs)
    rows0 = [256 * k for k in range(4)] + [256 * k + 128 for k in range(4)]
    rows1 = [128 * k for k in range(8)]
    maps = []
    for ci in range(8):
        b, r = ci // 4, ci % 4
        m = dict(shared)
        m["xT"] = np.ascontiguousarray(x[b, NT * r:NT * (r + 1)].T)
        m["pos"] = np.ascontiguousarray(pos[b])
        m.update(prep_launchB(inp, r))
        m.update(prep_s5(inp, r))
        for k_, v_ in prep_launchD(inp, r).items():
            m["D_" + k_] = v_
        m["idxC"] = _idx(rows0, r); m["idxE"] = _idx(rows1, r)
        maps.append(m)
    res = _run(build_fused(), maps)
    out = np.empty((2, S, 1024), np.float32)
    for ci in range(8):
        b, r = ci // 4, ci % 4
        out[b, NT * r:NT * (r + 1)] = res[ci]["outT"].T
    return out
```

```python
import math
import numpy as np
from contextlib import ExitStack
import concourse.bass as bass
import concourse.mybir as mybir
from concourse.bass_utils import run_bass_kernel_spmd


F32 = mybir.dt.float32
BF16 = mybir.dt.bfloat16
I32 = mybir.dt.int32
AF = mybir.ActivationFunctionType
ALU = mybir.AluOpType
AX = mybir.AxisListType


class Stream:
    def __init__(self, kb, name, h, in_order_safe=False):
        self.kb = kb
        self.name = name
        self.h = h
        self.sem = kb.new_sem("e_" + name)
        self.count = 0
        self.seen = {}
        self.in_order_safe = in_order_safe

    def wait(self, ticket):
        if ticket is None:
            return
        sem, val = ticket
        key = id(sem)
        if self.seen.get(key, 0) >= val:
            return
        self.h.wait_ge(sem, val)
        self.seen[key] = val


class Buf:
    __slots__ = ("name", "lw", "rd", "dsem", "dcount", "excl")

    def __init__(self, name, excl=False):
        self.name = name
        self.excl = excl
        self.lw = None
        self.rd = []
        self.dsem = None
        self.dcount = 0


class KB:
    def __init__(self):
        self.nc = bass.Bass("TRN2", target_bir_lowering=False)
        self.es = ExitStack()
        self.nsem = 0
        self.scopes = []
        self.scope_bufs = []
        self.sem_pool = []
        self.no_recycle = False
        nc = self.nc
        self.pe = Stream(self, "pe", nc.tensor, in_order_safe=True)
        self.act = Stream(self, "act", nc.scalar)
        self.dve = Stream(self, "dve", nc.vector)
        self.pool = Stream(self, "pool", nc.gpsimd)
        self.sp = Stream(self, "sp", nc.sync)
        self.streams = [self.pe, self.act, self.dve, self.pool, self.sp]
        self.dma_tickets = []
        self.bufs = []

    def new_sem(self, name):
        self.nsem += 1
        return self.es.enter_context(self.nc.semaphore(f"{name}_{self.nsem}"))

    def buf(self, name, excl=False):
        b = Buf(name, excl)
        self.bufs.append(b)
        if self.scope_bufs:
            self.scope_bufs[-1].append(b)
        return b

    def new_dsem(self, name):
        if self.sem_pool:
            return self.sem_pool.pop()
        return (self.new_sem(name), 0)

    def sb(self, name, shape, dt):
        self.nname = getattr(self, "nname", 0) + 1
        return self.cur.enter_context(self.nc.sbuf_tensor(f"{name}_{self.nname}", shape, dt))

    def ps(self, name, shape, dt=F32):
        self.nname = getattr(self, "nname", 0) + 1
        return self.cur.enter_context(self.nc.psum_tensor(f"{name}_{self.nname}", shape, dt))

    @property
    def cur(self):
        return self.scopes[-1] if self.scopes else self.es

    def scope(self):
        kb = self

        class _S:
            def __enter__(s):
                st = ExitStack()
                kb.scopes.append(st)
                kb.scope_bufs.append([])
                return st

            def __exit__(s, *a):
                if a[0] is None:
                    kb.barrier()
                    for b in kb.scope_bufs[-1]:
                        if b.dsem is not None and not kb.no_recycle:
                            kb.sem_pool.append((b.dsem, b.dcount))
                            b.dsem = None
                kb.scope_bufs.pop()
                st = kb.scopes.pop()
                st.close()
                return False

        return _S()

    def _deps(self, r, w):
        deps = []
        for b in r:
            if b.lw is not None:
                deps.append(b.lw)
            if b.excl:
                deps.extend(b.rd)
        for b in w:
            if b.lw is not None:
                deps.append(b.lw)
            deps.extend(b.rd)
        return deps

    def _commit(self, ticket, r, w):
        for b in w:
            b.lw = ticket
            b.rd = []
        for b in r:
            if b in w:
                continue
            b.rd.append(ticket)
            if len(b.rd) > 24:
                best = {}
                for (s, v) in b.rd:
                    k = id(s)
                    if k not in best or best[k][1] < v:
                        best[k] = (s, v)
                b.rd = list(best.values())

    def op(self, st, fn, r=(), w=(), sig=True):
        r = list(r)
        w = list(w)
        for (sem, val) in self._deps(r, w):
            if sem is st.sem and st.in_order_safe:
                continue
            st.wait((sem, val))
        ins = fn()
        if sig:
            st.count += 1
            ins.then_inc(st.sem, 1)
            ticket = (st.sem, st.count)
        else:
            ticket = (st.sem, st.count + 1)
        self._commit(ticket, r, w)
        return ins

    def dma(self, st, out, in_, r=(), w=(), sembuf=None, **kw):
        r = list(r)
        w = list(w)
        for t in self._deps(r, w):
            st.wait(t)
        if sembuf is None:
            sembuf = w[0] if w else r[0]
        if sembuf.dsem is None:
            sembuf.dsem, sembuf.dcount = self.new_dsem("d_" + sembuf.name)
        ins = st.h.dma_start(out=out, in_=in_, **kw)
        sembuf.dcount += 16
        ins.then_inc(sembuf.dsem, 16)
        ticket = (sembuf.dsem, sembuf.dcount)
        self._commit(ticket, r, w)
        self.dma_tickets.append(ticket)
        if len(self.dma_tickets) > 64:
            best = {}
            for (s, v) in self.dma_tickets:
                k = id(s)
                if k not in best or best[k][1] < v:
                    best[k] = (s, v)
            self.dma_tickets = list(best.values())
        return ins

    def allgather(self, src, src_b, dst, dst_b, groups):
        st = self.pool
        for t in self._deps([src_b], [dst_b]):
            st.wait(t)
        ins = st.h.collective_compute("AllGather", ALU.bypass, replica_groups=groups, ins=[src], outs=[dst])
        sem = self.new_sem("cc")
        ins.then_inc(sem, 1)
        ticket = (sem, 1)
        self._commit(ticket, [src_b], [dst_b])
        self.dma_tickets.append(ticket)

    def gather_rows(self, out, out_b, in_, in_b, idx_ap, idx_b):
        st = self.pool
        for t in self._deps([in_b, idx_b], [out_b]):
            st.wait(t)
        if out_b.dsem is None:
            out_b.dsem, out_b.dcount = self.new_dsem("d_" + out_b.name)
        ins = st.h.indirect_dma_start(out=out, out_offset=None, in_=in_,
                                      in_offset=bass.IndirectOffsetOnAxis(ap=idx_ap, axis=0))
        out_b.dcount += 16
        ins.then_inc(out_b.dsem, 16)
        ticket = (out_b.dsem, out_b.dcount)
        self._commit(ticket, [in_b, idx_b], [out_b])
        self.dma_tickets.append(ticket)

    def barrier(self, streams=None):
        tickets = [(s.sem, s.count) for s in self.streams if s.count > 0]
        tickets += self.dma_tickets
        for s in (streams or self.streams):
            for t in tickets:
                if t[0] is s.sem and s.in_order_safe:
                    continue
                s.wait(t)
        if streams is None:
            self.dma_tickets = []
            for b in self.bufs:
                b.lw = None
                b.rd = []

    def finish(self, out_bufs):
        self.barrier(streams=[self.sp])
        self.es.close()


NORM_EPS = 1e-6
D = 1024
DFF = 2816
NFT = 22


def emit_consts(kb):
    c = {}
    c["ones_bf"] = kb.sb("ones_bf", [128, 128], BF16)
    c["ones_b"] = kb.buf("ones_bf")
    kb.op(kb.dve, lambda: kb.nc.vector.memset(c["ones_bf"][:], 1.0), w=[c["ones_b"]])
    c["one"] = kb.sb("one", [128, 1], F32)
    c["one_b"] = kb.buf("one")
    kb.op(kb.dve, lambda: kb.nc.vector.memset(c["one"][:], 1.0), w=[c["one_b"]])
    return c


def emit_norm(kb, c, x_d, x_db, g_d, hT, hT_b, NT, ncols_per=512):
    nc = kb.nc
    ntt = NT // 512
    with kb.scope():
        gT = kb.sb("gT", [128, 8], F32)
        gT_b = kb.buf("gT")
        kb.dma(kb.sp, gT[:], g_d, w=[gT_b])
        xin = [kb.sb("xin", [128, 8, 512], F32) for _ in range(2)]
        xin_b = [kb.buf("xin") for _ in range(2)]
        sq = [kb.sb("sq", [128, 8, 512], BF16) for _ in range(2)]
        sq_b = [kb.buf("sq") for _ in range(2)]
        rs = [kb.sb("rs", [128, 512], F32) for _ in range(2)]
        rs_b = [kb.buf("rs") for _ in range(2)]
        pss = [kb.ps("pss", [128, 512]) for _ in range(2)]
        pss_b = [kb.buf("pss", excl=True) for _ in range(2)]
        xv = x_d.rearrange("(k p) t -> p k t", p=128)
        for tt in range(ntt):
            i = tt % 2
            kb.dma(kb.sp, xin[i][:], xv[:, :, tt * 512:(tt + 1) * 512], r=[x_db], w=[xin_b[i]])
            kb.op(kb.act, lambda: nc.scalar.activation(out=sq[i][:], in_=xin[i][:], func=AF.Square),
                  r=[xin_b[i]], w=[sq_b[i]])
            for k in range(8):
                kb.op(kb.pe, lambda: nc.tensor.matmul(pss[i][:], lhsT=c["ones_bf"][:], rhs=sq[i][:, k, :],
                                                      start=(k == 0), stop=(k == 7)),
                      r=[sq_b[i], c["ones_b"]], w=[pss_b[i]], sig=(k == 7))
            kb.op(kb.act, lambda: nc.scalar.activation(out=rs[i][:], in_=pss[i][:], func=AF.Sqrt,
                                                       bias=c["eps"][:], scale=1.0 / D),
                  r=[pss_b[i], c["eps_b"]], w=[rs_b[i]])
            kb.op(kb.dve, lambda: nc.vector.reciprocal(out=rs[i][:], in_=rs[i][:]), r=[rs_b[i]], w=[rs_b[i]])
            for k in range(8):
                kb.op(kb.dve, lambda: nc.vector.scalar_tensor_tensor(
                    out=hT[:, k, tt * 512:(tt + 1) * 512], in0=xin[i][:, k, :], scalar=gT[:, k:k + 1],
                    in1=rs[i][:], op0=ALU.mult, op1=ALU.mult),
                    r=[xin_b[i], rs_b[i], gT_b], w=[hT_b[tt]])
        kb.barrier()


def emit_ffn(kb, c, x_d, x_db, xo_d, xo_db, g_d, wgu_d, wd_d, NT):
    nc = kb.nc
    ntt = NT // 512
    with kb.scope():
        hT = kb.sb("hT", [128, 8, NT], BF16)
        hT_b = [kb.buf("hT") for _ in range(ntt)]
        emit_norm(kb, c, x_d, x_db, g_d, hT, hT_b, NT)
        actT = kb.sb("actT", [128, NFT, NT], BF16)
        actT_b = [[kb.buf("actT") for _ in range(ntt)] for _ in range(NFT)]
        with kb.scope():
            wg = [kb.sb("wgu", [128, 8, 512], BF16) for _ in range(2)]
            wg_b = [kb.buf("wgu") for _ in range(2)]
            psG = [kb.ps("psG", [128, 512]) for _ in range(2)]
            psU = [kb.ps("psU", [128, 512]) for _ in range(2)]
            psG_b = [kb.buf("psG", excl=True) for _ in range(2)]
            psU_b = [kb.buf("psU", excl=True) for _ in range(2)]
            sg = [kb.sb("sg", [128, 512], F32) for _ in range(2)]
            sg_b = [kb.buf("sg") for _ in range(2)]
            it = 0
            for j in range(11):
                s = j % 2
                kb.dma(kb.pool, wg[s][:], wgu_d[j], w=[wg_b[s]])
                for h in range(2):
                    ft = 2 * j + h
                    for tt in range(ntt):
                        i = it % 2
                        it += 1
                        tok = slice(tt * 512, (tt + 1) * 512)
                        for k in range(8):
                            kb.op(kb.pe, lambda: nc.tensor.matmul(
                                psG[i][:], lhsT=wg[s][:, k, h * 128:(h + 1) * 128], rhs=hT[:, k, tok],
                                start=(k == 0), stop=(k == 7)),
                                r=[wg_b[s], hT_b[tt]], w=[psG_b[i]], sig=(k == 7))
                        for k in range(8):
                            kb.op(kb.pe, lambda: nc.tensor.matmul(
                                psU[i][:], lhsT=wg[s][:, k, 256 + h * 128:256 + (h + 1) * 128], rhs=hT[:, k, tok],
                                start=(k == 0), stop=(k == 7)),
                                r=[wg_b[s], hT_b[tt]], w=[psU_b[i]], sig=(k == 7))
                        kb.op(kb.act, lambda: nc.scalar.activation(out=sg[i][:], in_=psG[i][:], func=AF.Silu),
                              r=[psG_b[i]], w=[sg_b[i]])
                        kb.op(kb.dve, lambda: nc.vector.tensor_tensor(
                            out=actT[:, ft, tok], in0=psU[i][:], in1=sg[i][:], op=ALU.mult),
                            r=[psU_b[i], sg_b[i]], w=[actT_b[ft][tt]])
            kb.barrier()
        with kb.scope():
            wd = [kb.sb("wd", [128, NFT, 256], BF16) for _ in range(2)]
            wd_b = [kb.buf("wd") for _ in range(2)]
            psD = [kb.ps("psD", [128, 512]) for _ in range(2)]
            psD_b = [kb.buf("psD", excl=True) for _ in range(2)]
            xr = [kb.sb("xr", [128, NT], F32) for _ in range(2)]
            xr_b = [kb.buf("xr") for _ in range(2)]
            it = 0
            for q in range(4):
                s = q % 2
                kb.dma(kb.pool, wd[s][:], wd_d[q], w=[wd_b[s]])
                for m in range(2):
                    dt = 2 * q + m
                    xi = dt % 2
                    kb.dma(kb.sp, xr[xi][:], x_d[dt * 128:(dt + 1) * 128, :], r=[x_db], w=[xr_b[xi]])
                    for tt in range(ntt):
                        i = it % 2
                        it += 1
                        tok = slice(tt * 512, (tt + 1) * 512)
                        for cc in range(NFT):
                            kb.op(kb.pe, lambda: nc.tensor.matmul(
                                psD[i][:], lhsT=wd[s][:, cc, m * 128:(m + 1) * 128], rhs=actT[:, cc, tok],
                                start=(cc == 0), stop=(cc == NFT - 1)),
                                r=[wd_b[s], actT_b[cc][tt]], w=[psD_b[i]], sig=(cc == NFT - 1))
                        kb.op(kb.dve, lambda: nc.vector.scalar_tensor_tensor(
                            out=xr[xi][:, tok], in0=psD[i][:], scalar=0.5, in1=xr[xi][:, tok],
                            op0=ALU.mult, op1=ALU.add),
                            r=[psD_b[i]], w=[xr_b[xi]])
                    kb.dma(kb.sp, xo_d[dt * 128:(dt + 1) * 128, :], xr[xi][:], r=[xr_b[xi]], w=[xo_db], sembuf=xr_b[xi])
            kb.barrier()


PI = math.pi
TWO_PI = 2.0 * math.pi
NORM_EPS = 1e-6
S = 8192
NQT = 16


def emit_rope_tables(kb, c, pos_d, pos_db, tt, invc, invc_b, sgnc, sgnc_b, COS, SINS, tb, scr, scr_b):
    nc = kb.nc
    pi_t, pf, kf, r, t2 = scr["pi"], scr["pf"], scr["kf"], scr["r"], scr["t2"]
    kb.dma(kb.sp, pi_t[:], pos_d[tt * 512:(tt + 1) * 512].partition_broadcast(128), r=[pos_db], w=[scr_b["pi"]])
    kb.op(kb.dve, lambda: nc.vector.tensor_copy(out=pf[:], in_=pi_t[:]), r=[scr_b["pi"]], w=[scr_b["pf"]])
    kb.op(kb.dve, lambda: nc.vector.tensor_scalar(out=pf[:], in0=pf[:], scalar1=invc[:, 0:1], scalar2=None,
                                                  op0=ALU.mult), r=[scr_b["pf"], invc_b], w=[scr_b["pf"]])
    kb.op(kb.dve, lambda: nc.vector.tensor_scalar(out=scr["ki"][:], in0=pf[:], scalar1=1.0 / TWO_PI, scalar2=None,
                                                  op0=ALU.mult), r=[scr_b["pf"]], w=[scr_b["ki"]])
    kb.op(kb.dve, lambda: nc.vector.tensor_copy(out=kf[:], in_=scr["ki"][:]), r=[scr_b["ki"]], w=[scr_b["kf"]])
    C1 = 6.28125
    C2 = TWO_PI - C1
    kb.op(kb.dve, lambda: nc.vector.scalar_tensor_tensor(out=r[:], in0=kf[:], scalar=-C1, in1=pf[:],
                                                         op0=ALU.mult, op1=ALU.add),
          r=[scr_b["kf"], scr_b["pf"]], w=[scr_b["r"]])
    kb.op(kb.dve, lambda: nc.vector.scalar_tensor_tensor(out=r[:], in0=kf[:], scalar=-C2, in1=r[:],
                                                         op0=ALU.mult, op1=ALU.add),
          r=[scr_b["kf"]], w=[scr_b["r"]])

    def wrap(x, xb):
        kb.op(kb.dve, lambda: nc.vector.tensor_scalar(out=t2[:], in0=x[:], scalar1=PI, scalar2=-TWO_PI,
                                                      op0=ALU.is_gt, op1=ALU.mult), r=[xb], w=[scr_b["t2"]])
        kb.op(kb.dve, lambda: nc.vector.tensor_tensor(out=x[:], in0=x[:], in1=t2[:], op=ALU.add),
              r=[scr_b["t2"]], w=[xb])
        kb.op(kb.dve, lambda: nc.vector.tensor_scalar(out=t2[:], in0=x[:], scalar1=-PI, scalar2=TWO_PI,
                                                      op0=ALU.is_lt, op1=ALU.mult), r=[xb], w=[scr_b["t2"]])
        kb.op(kb.dve, lambda: nc.vector.tensor_tensor(out=x[:], in0=x[:], in1=t2[:], op=ALU.add),
              r=[scr_b["t2"]], w=[xb])

    wrap(r, scr_b["r"])
    kb.op(kb.act, lambda: nc.scalar.activation(out=SINS[:], in_=r[:], func=AF.Sin), r=[scr_b["r"]], w=[tb["sin"]])
    kb.op(kb.dve, lambda: nc.vector.tensor_scalar(out=SINS[:], in0=SINS[:], scalar1=sgnc[:, 0:1], scalar2=None,
                                                  op0=ALU.mult), r=[sgnc_b], w=[tb["sin"]])
    kb.op(kb.dve, lambda: nc.vector.tensor_scalar(out=pf[:], in0=r[:], scalar1=PI / 2, scalar2=None, op0=ALU.add),
          r=[scr_b["r"]], w=[scr_b["pf"]])
    wrap(pf, scr_b["pf"])
    kb.op(kb.act, lambda: nc.scalar.activation(out=COS[:], in_=pf[:], func=AF.Sin), r=[scr_b["pf"]], w=[tb["cos"]])


def rope_scratch(kb):
    scr = {}
    scr_b = {}
    for n, dt in [("pi", I32), ("pf", F32), ("ki", I32), ("kf", F32), ("r", F32), ("t2", F32)]:
        scr[n] = kb.sb("rs_" + n, [128, 512], dt)
        scr_b[n] = kb.buf("rs_" + n)
    return scr, scr_b


def emit_attention(kb, c, nheads, dk, nmaps, dv, QT, QT_b, KT, KT_b, V, V_b, scale, finalize, QTILE=512,
                   nob=2, nst=2, npt=3):
    nc = kb.nc
    W = dv + 1
    nsub = QTILE // 128
    nqt = S // QTILE
    assert nmaps * QTILE == 512
    with kb.scope():
        ST = [kb.ps("ST", [128, 512]) for _ in range(nst)]
        ST_b = [kb.buf("ST", excl=True) for _ in range(nst)]
        per_bank = 512 // W
        nbk = (nsub * nmaps + per_bank - 1) // per_bank
        OB = [[kb.ps("OB", [128, 512]) for _ in range(nbk)] for _ in range(nob)]
        OB_b = [[kb.buf("OB", excl=True) for _ in range(nbk)] for _ in range(nob)]
        PT = [kb.sb("PT", [128, 512], BF16) for _ in range(npt)]
        PT_b = [kb.buf("PT") for _ in range(npt)]

        def oslot(m, s):
            idx = m * nsub + s
            return idx // per_bank, (idx % per_bank) * W

        blocks = []
        oi = 0
        for qt in range(nqt):
            for h in range(nheads):
                ob = oi % nob
                oi += 1
                nkb = nsub * qt + nsub
                for kbi in range(nkb):
                    blocks.append((qt, h, kbi, ob, kbi == nkb - 1))
        bank_started = {}

        def emit_qk_exp(i):
            qt, h, kbi, ob, _ = blocks[i]
            o = max(0, (kbi - nsub * qt) * 128)
            n = QTILE - o
            si = i % nst
            pi_ = i % npt
            q0 = qt * QTILE + o
            if nmaps > 1 and o == 0:
                kb.op(kb.pe, lambda: nc.tensor.matmul(
                    ST[si][:, :], lhsT=KT[h][0:dk, kbi * 128:(kbi + 1) * 128],
                    rhs=QT[h][0:dk, qt * nmaps * QTILE:(qt + 1) * nmaps * QTILE], start=True, stop=True, skip_group_check=True),
                    r=[KT_b[h][kbi // 4], QT_b[h][q0 // 512]], w=[ST_b[si]], sig=True)
            else:
                for m in range(nmaps):
                    rows = slice(0, dk)
                    if nmaps > 1:
                        cbase = qt * nmaps * QTILE + m * QTILE + o
                        qsrc = QT[h][rows, cbase:cbase + n]
                    else:
                        qsrc = QT[h][rows, q0:q0 + n]
                    kb.op(kb.pe, lambda: nc.tensor.matmul(
                        ST[si][:, m * QTILE:m * QTILE + n], lhsT=KT[h][rows, kbi * 128:(kbi + 1) * 128],
                        rhs=qsrc, start=(m == 0), stop=(m == nmaps - 1), skip_group_check=True),
                        r=[KT_b[h][kbi // 4], QT_b[h][q0 // 512]], w=[ST_b[si]], sig=(m == nmaps - 1))
            if nmaps == 1:
                src = ST[si][:, 0:n]
                dst = PT[pi_][:, 0:n]
            else:
                src = ST[si][:].rearrange("p (m q) -> p m q", m=nmaps)[:, :, 0:n]
                dst = PT[pi_][:].rearrange("p (m q) -> p m q", m=nmaps)[:, :, 0:n]
            kb.op(kb.act, lambda: nc.scalar.activation(out=dst, in_=src, func=AF.Exp, scale=scale),
                  r=[ST_b[si]], w=[PT_b[pi_]])
            if kbi >= nsub * qt:
                for m in range(nmaps):
                    kb.op(kb.pool, lambda: nc.gpsimd.tensor_tensor(
                        out=PT[pi_][:, m * QTILE:m * QTILE + 128], in0=PT[pi_][:, m * QTILE:m * QTILE + 128],
                        in1=c["tri"][:], op=ALU.mult),
                        r=[c["tri_b"]], w=[PT_b[pi_]])

        def emit_pv(i):
            qt, h, kbi, ob, is_last = blocks[i]
            o = max(0, (kbi - nsub * qt) * 128)
            pi_ = i % npt
            for m in range(nmaps):
                for s in range(o // 128, nsub):
                    bk, col = oslot(m, s)
                    first = (kbi == 0)
                    last = (kbi == nsub * qt + s)
                    key = (qt, h, bk)
                    st_flag = first and key not in bank_started
                    if first:
                        bank_started[key] = True
                    c0 = m * QTILE + s * 128 - o
                    kb.op(kb.pe, lambda: nc.tensor.matmul(
                        OB[ob][bk][:, col:col + W], lhsT=PT[pi_][:, c0:c0 + 128],
                        rhs=V[:, kbi, h, :], start=st_flag, stop=last, skip_group_check=True),
                        r=[PT_b[pi_], V_b[kbi // 4]], w=[OB_b[ob][bk]],
                        sig=(m == nmaps - 1 and s == nsub - 1))
            if is_last:
                def getO(m, s, ob=ob):
                    bk, col = oslot(m, s)
                    return OB[ob][bk][:, col:col + W], OB_b[ob][bk]

                finalize(h, qt, getO)

        nb = len(blocks)
        ahead = nst - 1
        for i0 in range(min(ahead, nb)):
            emit_qk_exp(i0)
        for i in range(nb):
            if i + ahead < nb:
                emit_qk_exp(i + ahead)
            emit_pv(i)
        kb.barrier()


def load_hin(kb, d, hin_t, hin_b, tt):
    rr, cc0 = tt // 4, (tt % 4) * 512
    if "hTg" in d:
        for q in range(4):
            kb.dma(kb.sp, hin_t[:, 2 * q:2 * q + 2, :],
                   d["hTg"][q][rr * 256:(rr + 1) * 256, cc0:cc0 + 512].rearrange("(kk p) t -> p kk t", p=128),
                   r=[d["hTg_b"][q]], w=[hin_b])
    else:
        kb.dma(kb.sp, hin_t[:], d["hTf"][rr].rearrange("(k p) t -> p k t", p=128)[:, :, cc0:cc0 + 512],
               r=[d["hTf_b"]], w=[hin_b])


def mix_ap(d, key, r0, r1, t0, n):
    if key + "_chunks" in d:
        j, lo = t0 // 2048, t0 % 2048
        return d[key + "_chunks"][j][r0:r1, lo:lo + n]
    return d[key][r0:r1, t0:t0 + n]


def load_cast(kb, name, shape, src_d, dt=BF16):
    t = kb.sb(name, shape, dt)
    b = kb.buf(name)
    kb.dma(kb.pool, t[:], src_d, w=[b])
    return t, b


def load_plain(kb, name, shape, src_d, dt=F32):
    t = kb.sb(name, shape, dt)
    b = kb.buf(name)
    kb.dma(kb.sp, t[:], src_d, w=[b])
    return t, b


def emit_launchB_mla(kb, c, d):
    nc = kb.nc
    scale = (64 + 32) ** -0.5
    with kb.scope():
        QT = [kb.sb("QT", [128, S], BF16) for _ in range(2)]
        KT = [kb.sb("KT", [128, S], BF16) for _ in range(2)]
        QT_b = [[kb.buf("QT") for _ in range(NQT)] for _ in range(2)]
        KT_b = [[kb.buf("KT") for _ in range(NQT)] for _ in range(2)]
        V = kb.sb("V", [128, S // 128, 2, 65], BF16)
        V_b = [kb.buf("V") for _ in range(NQT)]
        kb.op(kb.pool, lambda: nc.gpsimd.memset(V[:], 1.0), w=V_b)
        with kb.scope():
          if True:
              w_inA, w_inA_b = load_cast(kb, "w_inA", [128, 8, 800], d["w_inA"])
              wkr, wkr_b = kb.sb("wkr", [128, 8, 96], BF16), kb.buf("wkr")
              wkrs, wkrs_b = kb.sb("wkrs", [128, 8, 96], BF16), kb.buf("wkrs")
              kb.op(kb.pool, lambda: nc.gpsimd.memset(wkr[:], 0.0), w=[wkr_b])
              kb.op(kb.pool, lambda: nc.gpsimd.memset(wkrs[:], 0.0), w=[wkrs_b])
              kb.dma(kb.pool, wkr[:, :, 64:96], d["w_kr"], w=[wkr_b])
              kb.dma(kb.pool, wkrs[:, :, 64:96], d["w_krsw"], w=[wkrs_b])
              w_uq, w_uq_b = load_cast(kb, "w_uq", [128, 3, 192], d["w_uq"])
              w_uqs, w_uqs_b = load_cast(kb, "w_uqs", [128, 3, 192], d["w_uq_sw"])
              w_kk, w_kk_b = load_cast(kb, "w_kk", [128, 2, 128], d["w_ukv_k"])
              w_kv, w_kv_b = load_cast(kb, "w_kv", [128, 2, 128], d["w_ukv_v"])
              gq, gq_b = load_plain(kb, "gq", [128, 5], d["lat_g"])
              invc, invc_b = load_plain(kb, "invc", [128, 1], d["rope_inv"])
              sgnc, sgnc_b = load_plain(kb, "sgnc", [128, 1], d["rope_sgn"])
              scr, scr_b = rope_scratch(kb)
              COS = kb.sb("COS", [128, 512], F32)
              SINS = kb.sb("SINS", [128, 512], F32)
              tb = {"cos": kb.buf("COS"), "sin": kb.buf("SINS")}
              hin = [kb.sb("hin", [128, 8, 512], BF16) for _ in range(2)]
              hin_b = [kb.buf("hin") for _ in range(2)]
              csb = kb.sb("csb", [128, 5, 512], F32)
              csb_b = [kb.buf("csb") for _ in range(5)]
              sq = kb.sb("sq", [128, 5, 512], BF16)
              sq_b = [kb.buf("sq") for _ in range(5)]
              cn = kb.sb("cn", [128, 5, 512], BF16)
              cn_b = [kb.buf("cn") for _ in range(5)]
              rs = [kb.sb("rs", [128, 512], F32) for _ in range(2)]
              rs_b = [kb.buf("rs") for _ in range(2)]
              m1 = [kb.sb("m1", [128, 512], F32) for _ in range(2)]
              m1_b = [kb.buf("m1") for _ in range(2)]
              m2 = [kb.sb("m2", [128, 512], F32) for _ in range(2)]
              m2_b = [kb.buf("m2") for _ in range(2)]
              PS = [kb.ps("PS", [128, 512]) for _ in range(6)]
              PS_b = [kb.buf("PS", excl=True) for _ in range(6)]
              pctr = [0]
              mctr = [0]

              def nextps():
                  i = pctr[0] % 6
                  pctr[0] += 1
                  return PS[i], PS_b[i]

              for tt in range(NQT):
                  hi = tt % 2
                  tok = slice(tt * 512, (tt + 1) * 512)
                  rr, cc0 = tt // 4, (tt % 4) * 512
                  load_hin(kb, d, hin[hi], hin_b[hi], tt)
                  emit_rope_tables(kb, c, d["pos"], d["pos_b"], tt, invc, invc_b, sgnc, sgnc_b, COS, SINS, tb, scr, scr_b)
                  if d.get("uT") is not None:
                      ps, psb = nextps()
                      for k in range(8):
                          kb.op(kb.pe, lambda: nc.tensor.matmul(ps[:], lhsT=w_inA[:, k, 0:128], rhs=hin[hi][:, k, :],
                                                                start=(k == 0), stop=(k == 7)),
                                r=[w_inA_b, hin_b[hi]], w=[psb], sig=(k == 7))
                      kb.op(kb.act, lambda: nc.scalar.copy(out=d["uT"][:, tok], in_=ps[:]), r=[psb], w=[d["uT_b"][tt]])
                  if d.get('stop') == 1:
                      break
                  for j in range(5):
                      ps, psb = nextps()
                      for k in range(8):
                          kb.op(kb.pe, lambda: nc.tensor.matmul(ps[:], lhsT=w_inA[:, k, 128 + j * 128:256 + j * 128],
                                                                rhs=hin[hi][:, k, :], start=(k == 0), stop=(k == 7)),
                                r=[w_inA_b, hin_b[hi]], w=[psb], sig=(k == 7))
                      kb.op(kb.act, lambda: nc.scalar.activation(out=sq[:, j, :], in_=ps[:], func=AF.Square),
                            r=[psb], w=[sq_b[j]])
                      kb.op(kb.dve, lambda: nc.vector.tensor_scalar(out=csb[:, j, :], in0=ps[:], scalar1=gq[:, j:j + 1],
                                                                    scalar2=None, op0=ALU.mult),
                            r=[psb, gq_b], w=[csb_b[j]])
                  if d.get('stop') == 2:
                      break
                  for (idx, chunks, n) in [(0, [0, 1, 2], 384), (1, [3, 4], 256)]:
                      ps, psb = nextps()
                      for ii, j in enumerate(chunks):
                          kb.op(kb.pe, lambda: nc.tensor.matmul(ps[:], lhsT=c["ones_bf"][:], rhs=sq[:, j, :],
                                                                start=(ii == 0), stop=(ii == len(chunks) - 1)),
                                r=[sq_b[j], c["ones_b"]], w=[psb], sig=(ii == len(chunks) - 1))
                      kb.op(kb.act, lambda: nc.scalar.activation(out=rs[idx][:], in_=ps[:], func=AF.Sqrt,
                                                                 bias=c["eps"][:], scale=1.0 / n),
                            r=[psb, c["eps_b"]], w=[rs_b[idx]])
                      kb.op(kb.dve, lambda: nc.vector.reciprocal(out=rs[idx][:], in_=rs[idx][:]), r=[rs_b[idx]], w=[rs_b[idx]])
                      for j in chunks:
                          kb.op(kb.dve, lambda: nc.vector.tensor_tensor(out=cn[:, j, :], in0=csb[:, j, :], in1=rs[idx][:],
                                                                        op=ALU.mult),
                                r=[csb_b[j], rs_b[idx]], w=[cn_b[j]])

                  def rope_combine(ps_a, psb_a, ps_s, psb_s, out_ap, out_b, rows):
                      mi = mctr[0] % 2
                      mctr[0] += 1
                      kb.op(kb.dve, lambda: nc.vector.tensor_tensor(out=m1[mi][rows, :], in0=ps_a[rows, :], in1=COS[rows, :],
                                                                    op=ALU.mult), r=[psb_a, tb["cos"]], w=[m1_b[mi]])
                      kb.op(kb.dve, lambda: nc.vector.tensor_tensor(out=m2[mi][rows, :], in0=ps_s[rows, :], in1=SINS[rows, :],
                                                                    op=ALU.mult), r=[psb_s, tb["sin"]], w=[m2_b[mi]])
                      kb.op(kb.pool, lambda: nc.gpsimd.tensor_tensor(out=out_ap, in0=m1[mi][rows, :], in1=m2[mi][rows, :],
                                                                     op=ALU.add), r=[m1_b[mi], m2_b[mi]], w=[out_b])

                  if d.get('stop') == 3:
                      break
                  ps_a, psb_a = nextps()
                  for k in range(8):
                      kb.op(kb.pe, lambda: nc.tensor.matmul(ps_a[0:96, :], lhsT=wkr[:, k, :], rhs=hin[hi][:, k, :],
                                                            start=(k == 0), stop=(k == 7)),
                            r=[wkr_b, hin_b[hi]], w=[psb_a], sig=(k == 7))
                  ps_s, psb_s = nextps()
                  for k in range(8):
                      kb.op(kb.pe, lambda: nc.tensor.matmul(ps_s[0:96, :], lhsT=wkrs[:, k, :], rhs=hin[hi][:, k, :],
                                                            start=(k == 0), stop=(k == 7)),
                            r=[wkrs_b, hin_b[hi]], w=[psb_s], sig=(k == 7))
                  rope_combine(ps_a, psb_a, ps_s, psb_s, KT[0][64:96, tok], KT_b[0][tt], slice(64, 96))
                  kb.op(kb.act, lambda: nc.scalar.copy(out=KT[1][64:96, tok], in_=KT[0][64:96, tok]),
                        r=[KT_b[0][tt]], w=[KT_b[1][tt]])
                  if d.get('stop') == 4:
                      break
                  for h in range(2):
                      ps_a, psb_a = nextps()
                      for j in range(3):
                          kb.op(kb.pe, lambda: nc.tensor.matmul(ps_a[0:96, :], lhsT=w_uq[:, j, h * 96:(h + 1) * 96],
                                                                rhs=cn[:, j, :], start=(j == 0), stop=(j == 2)),
                                r=[w_uq_b, cn_b[j]], w=[psb_a], sig=(j == 2))
                      ps_s, psb_s = nextps()
                      for j in range(3):
                          kb.op(kb.pe, lambda: nc.tensor.matmul(ps_s[0:96, :], lhsT=w_uqs[:, j, h * 96:(h + 1) * 96],
                                                                rhs=cn[:, j, :], start=(j == 0), stop=(j == 2)),
                                r=[w_uqs_b, cn_b[j]], w=[psb_s], sig=(j == 2))
                      rope_combine(ps_a, psb_a, ps_s, psb_s, QT[h][0:96, tok], QT_b[h][tt], slice(0, 96))
                  if d.get('stop') == 5:
                      break
                  for h in range(2):
                      ps, psb = nextps()
                      for j in range(2):
                          kb.op(kb.pe, lambda: nc.tensor.matmul(ps[0:64, :], lhsT=w_kk[:, j, h * 64:(h + 1) * 64],
                                                                rhs=cn[:, 3 + j, :], start=(j == 0), stop=(j == 1)),
                                r=[w_kk_b, cn_b[3 + j]], w=[psb], sig=(j == 1))
                      kb.op(kb.act, lambda: nc.scalar.copy(out=KT[h][0:64, tok], in_=ps[0:64, :]),
                            r=[psb], w=[KT_b[h][tt]])
                  if d.get('stop') == 6:
                      break
                  for s4 in range(4):
                      ps, psb = nextps()
                      for j in range(2):
                          kb.op(kb.pe, lambda: nc.tensor.matmul(ps[:, 0:128], lhsT=cn[:, 3 + j, s4 * 128:(s4 + 1) * 128],
                                                                rhs=w_kv[:, j, :], start=(j == 0), stop=(j == 1)),
                                r=[w_kv_b, cn_b[3 + j]], w=[psb], sig=(j == 1))
                      kb.op(kb.act, lambda: nc.scalar.copy(
                          out=V[:, tt * 4 + s4, :, 0:64], in_=ps[:, 0:128].rearrange("p (h e) -> p h e", h=2)),
                          r=[psb], w=[V_b[tt]])
              kb.barrier()
        if d.get('mid_hook') is not None:
            d['mid_hook']()
        if d.get('skip_attn'):
            return
        with kb.scope():
            on = [kb.sb("on", [128, 4, 128], BF16) for _ in range(2)]
            on_b = [kb.buf("on") for _ in range(2)]
            rec = kb.sb("rec", [128, 8], F32)
            rec_b = kb.buf("rec")
            psT = kb.ps("psT", [128, 1024], BF16)
            psT_b = kb.buf("psT", excl=True)
            oT = [kb.sb("oT", [128, 512], BF16) for _ in range(2)]
            oT_b = [kb.buf("oT") for _ in range(2)]

            def finalize(h, qt, getO):
                i = qt % 2
                for s in range(4):
                    O, Ob = getO(0, s)
                    kb.op(kb.dve, lambda: nc.vector.reciprocal(out=rec[:, s:s + 1], in_=O[:, 64:65]), r=[Ob], w=[rec_b])
                    kb.op(kb.dve, lambda: nc.vector.tensor_scalar(out=on[i][:, s, h * 64:(h + 1) * 64], in0=O[:, 0:64],
                                                                  scalar1=rec[:, s:s + 1], scalar2=None, op0=ALU.mult),
                          r=[Ob, rec_b], w=[on_b[i]])
                if h == 1:
                    for s2 in range(4):
                        kb.op(kb.pe, lambda: nc.tensor.transpose(psT[:, s2 * 128:(s2 + 1) * 128], on[i][:, s2, :],
                                                                 c["ident_bf"][:]),
                              r=[on_b[i], c["ident_b"]], w=[psT_b])
                    kb.op(kb.act, lambda: nc.scalar.copy(out=oT[i][:], in_=psT[:, 0:512]), r=[psT_b], w=[oT_b[i]])
                    kb.dma(kb.sp, mix_ap(d, "mixB", 128, 256, qt * 512, 512), oT[i][:], r=[oT_b[i]], w=[d["mixB_b"]],
                           sembuf=oT_b[i])
                    if d.get("chunk_done") is not None and (qt + 1) % 4 == 0:
                        d["chunk_done"](qt // 4)

            emit_attention(kb, c, 2, 96, 1, 64, QT, QT_b, KT, KT_b, V, V_b, scale, finalize, nst=4, npt=5)


PI = math.pi
T = 512


def bc(ap2, n):
    return ap2.rearrange("p (g o) -> p g o", o=1).broadcast_to([128, 8, n])


def emit_s5(kb, c, d, uT, uT_b):
    nc = kb.nc
    V = nc.vector
    with kb.scope():
        COSt = kb.sb("COSt", [128, 8, T], F32)
        SINt = kb.sb("SINt", [128, 8, T], F32)
        tab_b = kb.buf("tabs")
        LB = kb.sb("LB", [128, 8, 128], BF16)
        LBs = kb.sb("LBs", [128, 8, 128], BF16)
        LC1 = kb.sb("LC1", [128, 8, 128], BF16)
        LC2 = kb.sb("LC2", [128, 8, 128], BF16)
        Rm = kb.sb("Rm", [128, 8, 128], F32)
        Rb = kb.sb("Rb", [128, 8], F32)
        par_b = kb.buf("s5par")
        d_t, d_tb = load_plain(kb, "d_t", [128, 1], d["s5_d"])
        with kb.scope():
            lr, lr_b = load_plain(kb, "lr", [128, 8], d["s5_lr"])
            li, li_b = load_plain(kb, "li", [128, 8], d["s5_li"])
            ldt, ldt_b = load_plain(kb, "ldt", [128, 8], d["s5_logdt"])
            br, br_b = load_plain(kb, "br", [128, 8, 16], d["s5_bre"])
            bi, bi_b = load_plain(kb, "bi", [128, 8, 16], d["s5_bim"])
            c1s, c1s_b = load_plain(kb, "c1s", [128, 128], d["s5_c1src"])
            c2s, c2s_b = load_plain(kb, "c2s", [128, 128], d["s5_c2src"])
            rowm, rowm_b = load_plain(kb, "rowm", [128, 8], d["rowmask"])
            Jm, Jm_b = load_plain(kb, "Jm", [128, 128], d["Jmat"])
            idf, idf_b = load_plain(kb, "idf", [128, 128], d["ident"])
            sb = kb.buf("s5scr")
            names = ["dt", "mag", "th", "cs", "sn", "t1", "t2", "t3", "abr", "abi", "den", "nr", "fre", "fim", "wc", "ws"]
            t = {n: kb.sb("s5_" + n, [128, 8], F32) for n in names}
            RW = [sb, lr_b, li_b, ldt_b]

            def dv(fn):
                kb.op(kb.dve, fn, r=RW, w=[sb])

            def ac(fn):
                kb.op(kb.act, fn, r=RW, w=[sb])

            ac(lambda: nc.scalar.activation(out=t["dt"][:], in_=ldt[:], func=AF.Exp))
            dv(lambda: V.tensor_tensor(out=t["t1"][:], in0=lr[:], in1=t["dt"][:], op=ALU.mult))
            ac(lambda: nc.scalar.activation(out=t["mag"][:], in_=t["t1"][:], func=AF.Exp))
            dv(lambda: V.tensor_copy(out=Rb[:], in_=t["mag"][:]))
            dv(lambda: V.tensor_tensor(out=t["th"][:], in0=li[:], in1=t["dt"][:], op=ALU.mult))
            ac(lambda: nc.scalar.activation(out=t["sn"][:], in_=t["th"][:], func=AF.Sin, scale=1.0 / 16))
            dv(lambda: V.tensor_scalar(out=t["t1"][:], in0=t["th"][:], scalar1=1.0 / 16, scalar2=PI / 2,
                                       op0=ALU.mult, op1=ALU.add))
            ac(lambda: nc.scalar.activation(out=t["cs"][:], in_=t["t1"][:], func=AF.Sin))

            def csq(cn, sn_):
                dv(lambda: V.tensor_tensor(out=t["t1"][:], in0=t[cn][:], in1=t[cn][:], op=ALU.mult))
                dv(lambda: V.tensor_tensor(out=t["t2"][:], in0=t[sn_][:], in1=t[sn_][:], op=ALU.mult))
                dv(lambda: V.tensor_tensor(out=t["t3"][:], in0=t[cn][:], in1=t[sn_][:], op=ALU.mult))
                dv(lambda: V.tensor_tensor(out=t[cn][:], in0=t["t1"][:], in1=t["t2"][:], op=ALU.subtract))
                dv(lambda: V.tensor_scalar(out=t[sn_][:], in0=t["t3"][:], scalar1=2.0, scalar2=None, op0=ALU.mult))

            for _ in range(4):
                csq("cs", "sn")
            dv(lambda: V.tensor_tensor(out=t["abr"][:], in0=t["mag"][:], in1=t["cs"][:], op=ALU.mult))
            dv(lambda: V.tensor_tensor(out=t["abi"][:], in0=t["mag"][:], in1=t["sn"][:], op=ALU.mult))
            dv(lambda: V.tensor_tensor(out=t["t1"][:], in0=lr[:], in1=lr[:], op=ALU.mult))
            dv(lambda: V.tensor_tensor(out=t["t2"][:], in0=li[:], in1=li[:], op=ALU.mult))
            dv(lambda: V.tensor_tensor(out=t["den"][:], in0=t["t1"][:], in1=t["t2"][:], op=ALU.add))
            dv(lambda: V.reciprocal(out=t["den"][:], in_=t["den"][:]))
            dv(lambda: V.tensor_scalar(out=t["nr"][:], in0=t["abr"][:], scalar1=-1.0, scalar2=None, op0=ALU.add))
            dv(lambda: V.tensor_tensor(out=t["t1"][:], in0=t["nr"][:], in1=lr[:], op=ALU.mult))
            dv(lambda: V.tensor_tensor(out=t["t2"][:], in0=t["abi"][:], in1=li[:], op=ALU.mult))
            dv(lambda: V.tensor_tensor(out=t["t1"][:], in0=t["t1"][:], in1=t["t2"][:], op=ALU.add))
            dv(lambda: V.tensor_tensor(out=t["fre"][:], in0=t["t1"][:], in1=t["den"][:], op=ALU.mult))
            dv(lambda: V.tensor_tensor(out=t["t1"][:], in0=t["abi"][:], in1=lr[:], op=ALU.mult))
            dv(lambda: V.tensor_tensor(out=t["t2"][:], in0=t["nr"][:], in1=li[:], op=ALU.mult))
            dv(lambda: V.tensor_tensor(out=t["t1"][:], in0=t["t1"][:], in1=t["t2"][:], op=ALU.subtract))
            dv(lambda: V.tensor_tensor(out=t["fim"][:], in0=t["t1"][:], in1=t["den"][:], op=ALU.mult))
            bbr = kb.sb("bbr", [128, 8, 16], F32)
            bbi = kb.sb("bbi", [128, 8, 16], F32)
            tA = kb.sb("tA", [128, 8, 16], F32)
            RWb = RW + [br_b, bi_b]

            def dvb(fn):
                kb.op(kb.dve, fn, r=RWb, w=[sb])

            dvb(lambda: V.tensor_tensor(out=bbr[:], in0=br[:], in1=bc(t["fre"][:], 16), op=ALU.mult))
            dvb(lambda: V.tensor_tensor(out=tA[:], in0=bi[:], in1=bc(t["fim"][:], 16), op=ALU.mult))
            dvb(lambda: V.tensor_tensor(out=bbr[:], in0=bbr[:], in1=tA[:], op=ALU.subtract))
            dvb(lambda: V.tensor_tensor(out=bbi[:], in0=bi[:], in1=bc(t["fre"][:], 16), op=ALU.mult))
            dvb(lambda: V.tensor_tensor(out=tA[:], in0=br[:], in1=bc(t["fim"][:], 16), op=ALU.mult))
            dvb(lambda: V.tensor_tensor(out=bbi[:], in0=bbi[:], in1=tA[:], op=ALU.add))
            BbA = kb.sb("BbA", [128, 128], F32)
            BswA = kb.sb("BswA", [128, 128], F32)
            fl = lambda x: x.rearrange("p g h -> p (g h)")
            dvb(lambda: V.tensor_copy(out=BbA[0:64, :], in_=fl(bbr[0:64])))
            dvb(lambda: V.tensor_copy(out=BbA[64:128, :], in_=fl(bbi[64:128])))
            dvb(lambda: V.tensor_copy(out=BswA[0:64, :], in_=fl(bbi[0:64])))
            dvb(lambda: V.tensor_scalar(out=BswA[64:128, :], in0=fl(bbr[64:128]), scalar1=-1.0, scalar2=None, op0=ALU.mult))
            kb.op(kb.dve, lambda: V.tensor_scalar(out=c1s[:, 64:128], in0=c1s[:, 64:128], scalar1=-1.0, scalar2=None,
                                                  op0=ALU.mult), r=[c1s_b], w=[c1s_b])
            kb.op(kb.dve, lambda: V.tensor_scalar(out=c2s[:], in0=c2s[:], scalar1=-1.0, scalar2=None, op0=ALU.mult),
                  r=[c2s_b], w=[c2s_b])
            pt = [kb.ps("s5pt", [128, 512]) for _ in range(2)]
            pt_b = [kb.buf("s5pt", excl=True) for _ in range(2)]
            for i, (src, srcb) in enumerate([(BbA, sb), (BswA, sb), (c1s, c1s_b), (c2s, c2s_b)]):
                kb.op(kb.pe, lambda: nc.tensor.transpose(pt[i // 2][:, (i % 2) * 128:(i % 2) * 128 + 128], src[:], idf[:]),
                      r=[srcb, idf_b], w=[pt_b[i // 2]])
            for g in range(8):
                kb.op(kb.dve, lambda: V.tensor_scalar(out=LB[:, g, :], in0=pt[0][:, 0:128], scalar1=rowm[:, g:g + 1],
                                                      scalar2=None, op0=ALU.mult), r=[pt_b[0], rowm_b], w=[par_b])
                kb.op(kb.dve, lambda: V.tensor_scalar(out=LBs[:, g, :], in0=pt[0][:, 128:256], scalar1=rowm[:, g:g + 1],
                                                      scalar2=None, op0=ALU.mult), r=[pt_b[0], rowm_b], w=[par_b])
            kb.op(kb.pool, lambda: nc.gpsimd.memset(LC1[:], 0.0), w=[par_b])
            kb.op(kb.pool, lambda: nc.gpsimd.memset(LC2[:], 0.0), w=[par_b])
            for g in range(8):
                kb.op(kb.dve, lambda: V.tensor_copy(out=LC1[:, g, 16 * g:16 * g + 16], in_=pt[1][:, 16 * g:16 * g + 16]),
                      r=[pt_b[1]], w=[par_b])
                kb.op(kb.dve, lambda: V.tensor_copy(out=LC2[:, g, 16 * g:16 * g + 16],
                                                    in_=pt[1][:, 128 + 16 * g:128 + 16 * g + 16]),
                      r=[pt_b[1]], w=[par_b])
            dv(lambda: V.tensor_copy(out=t["wc"][:], in_=t["cs"][:]))
            dv(lambda: V.tensor_copy(out=t["ws"][:], in_=t["sn"][:]))
            kb.op(kb.pool, lambda: nc.gpsimd.memset(COSt[:, :, 0:1], 1.0), w=[tab_b])
            kb.op(kb.pool, lambda: nc.gpsimd.memset(SINt[:, :, 0:1], 0.0), w=[tab_b])
            x1 = kb.sb("x1", [128, 8, T // 2], F32)
            x2 = kb.sb("x2", [128, 8, T // 2], F32)
            RT = RW + [tab_b]

            def dvt(fn):
                kb.op(kb.dve, fn, r=RT, w=[tab_b, sb])

            n = 1
            while n < T:
                wcb = bc(t["wc"][:], n)
                wsb = bc(t["ws"][:], n)
                dvt(lambda: V.tensor_tensor(out=x1[:, :, 0:n], in0=COSt[:, :, 0:n], in1=wcb, op=ALU.mult))
                dvt(lambda: V.tensor_tensor(out=x2[:, :, 0:n], in0=SINt[:, :, 0:n], in1=wsb, op=ALU.mult))
                dvt(lambda: V.tensor_tensor(out=COSt[:, :, n:2 * n], in0=x1[:, :, 0:n], in1=x2[:, :, 0:n], op=ALU.subtract))
                dvt(lambda: V.tensor_tensor(out=x1[:, :, 0:n], in0=SINt[:, :, 0:n], in1=wcb, op=ALU.mult))
                dvt(lambda: V.tensor_tensor(out=x2[:, :, 0:n], in0=COSt[:, :, 0:n], in1=wsb, op=ALU.mult))
                dvt(lambda: V.tensor_tensor(out=SINt[:, :, n:2 * n], in0=x1[:, :, 0:n], in1=x2[:, :, 0:n], op=ALU.add))
                csq("wc", "ws")
                n *= 2
            tmpR = kb.sb("tmpR", [128, 128], F32)
            for g in range(8):
                kb.op(kb.dve, lambda: V.tensor_scalar(out=tmpR[:], in0=Jm[:], scalar1=t["ws"][:, g:g + 1], scalar2=None,
                                                      op0=ALU.mult), r=[Jm_b, sb], w=[sb])
                kb.op(kb.dve, lambda: V.scalar_tensor_tensor(out=Rm[:, g, :], in0=idf[:], scalar=t["wc"][:, g:g + 1],
                                                             in1=tmpR[:], op0=ALU.mult, op1=ALU.add),
                      r=[idf_b, sb], w=[par_b])
            kb.barrier()
        with kb.scope():
            psA = [kb.ps("psA", [128, 512]) for _ in range(2)]
            psB = [kb.ps("psB", [128, 512]) for _ in range(2)]
            psA_b = [kb.buf("psA", excl=True) for _ in range(2)]
            psB_b = [kb.buf("psB", excl=True) for _ in range(2)]
            psY = [kb.ps("psY", [128, 512]) for _ in range(2)]
            psY_b = [kb.buf("psY", excl=True) for _ in range(2)]
            psI = [kb.ps("psI", [128, 512]) for _ in range(2)]
            psI_b = [kb.buf("psI", excl=True) for _ in range(2)]
            NB = 2
            m1 = [kb.sb("m1", [128, T], F32) for _ in range(NB)]
            m2 = [kb.sb("m2", [128, T], F32) for _ in range(NB)]
            Wt = [kb.sb("Wt", [128, T], F32) for _ in range(NB)]
            NZ = 3
            Z = [kb.sb("Z", [128, T], F32) for _ in range(NZ)]
            ZC = [kb.sb("ZC", [128, T], BF16) for _ in range(NZ)]
            ZS = [kb.sb("ZS", [128, T], BF16) for _ in range(NZ)]
            m1_b = [kb.buf("m1") for _ in range(NB)]
            m2_b = [kb.buf("m2") for _ in range(NB)]
            Wt_b = [kb.buf("Wt") for _ in range(NB)]
            Z_b = [kb.buf("Z") for _ in range(3)]
            ZC_b = [kb.buf("ZC") for _ in range(3)]
            ZS_b = [kb.buf("ZS") for _ in range(3)]
            init = kb.sb("init", [128, 8], F32)
            init_b = [kb.buf("init") for _ in range(8)]
            kb.op(kb.dve, lambda: V.memset(init[:], 0.0), w=init_b)
            ysb = [kb.sb("ysb", [128, T], F32) for _ in range(2)]
            ysb_b = [kb.buf("ysb") for _ in range(2)]
            g1 = [kb.sb("g1", [128, T], F32) for _ in range(2)]
            g1_b = [kb.buf("g1") for _ in range(2)]
            gout = [kb.sb("gout", [128, T], BF16) for _ in range(2)]
            gout_b = [kb.buf("gout") for _ in range(2)]
            items = [(tt, g) for tt in range(NQT) for g in range(8)]

            def stage1(j):
                tt, g = items[j]
                i = j % 2
                tok = slice(tt * T, (tt + 1) * T)
                kb.op(kb.pe, lambda: nc.tensor.matmul(psA[i][:], lhsT=LB[:, g, :], rhs=uT[:, tok], start=True, stop=True),
                      r=[par_b, uT_b[tt]], w=[psA_b[i]])
                kb.op(kb.pe, lambda: nc.tensor.matmul(psB[i][:], lhsT=LBs[:, g, :], rhs=uT[:, tok], start=True, stop=True),
                      r=[par_b, uT_b[tt]], w=[psB_b[i]])
                kb.op(kb.dve, lambda: V.tensor_tensor(out=m1[i][:], in0=psA[i][:], in1=COSt[:, g, :], op=ALU.mult),
                      r=[psA_b[i], tab_b], w=[m1_b[i]])
                kb.op(kb.dve, lambda: V.tensor_tensor(out=m2[i][:], in0=psB[i][:], in1=SINt[:, g, :], op=ALU.mult),
                      r=[psB_b[i], tab_b], w=[m2_b[i]])
                kb.op(kb.pool, lambda: nc.gpsimd.tensor_tensor(out=Wt[i][:], in0=m1[i][:], in1=m2[i][:], op=ALU.add),
                      r=[m1_b[i], m2_b[i]], w=[Wt_b[i]])

            def stage2(j):
                tt, g = items[j]
                i = j % 2
                z = j % 3
                yi = tt % 2
                tok = slice(tt * T, (tt + 1) * T)
                kb.op(kb.dve, lambda: V.tensor_tensor_scan(out=Z[z][:], data0=Rb[:, g:g + 1].broadcast_to([128, T]),
                                                          data1=Wt[i][:], initial=init[:, g:g + 1],
                                                          op0=ALU.mult, op1=ALU.add),
                      r=[Wt_b[i], init_b[g], par_b], w=[Z_b[z]])
                kb.op(kb.pool, lambda: nc.gpsimd.tensor_tensor(out=ZC[z][:], in0=Z[z][:], in1=COSt[:, g, :], op=ALU.mult),
                      r=[Z_b[z], tab_b], w=[ZC_b[z]])
                kb.op(kb.dve, lambda: V.tensor_tensor(out=ZS[z][:], in0=Z[z][:], in1=SINt[:, g, :], op=ALU.mult),
                      r=[Z_b[z], tab_b], w=[ZS_b[z]])

            def stage3(j):
                tt, g = items[j]
                z = j % 3
                yi = tt % 2
                tok = slice(tt * T, (tt + 1) * T)
                if tt < NQT - 1:
                    ii = g % 2
                    kb.op(kb.pe, lambda: nc.tensor.matmul(psI[ii][:, g:g + 1], lhsT=Rm[:, g, :], rhs=Z[z][:, T - 1:T],
                                                          start=True, stop=True),
                          r=[par_b, Z_b[z]], w=[psI_b[ii]])
                    kb.op(kb.act, lambda: nc.scalar.copy(out=init[:, g:g + 1], in_=psI[ii][:, g:g + 1]),
                          r=[psI_b[ii]], w=[init_b[g]])
                kb.op(kb.pe, lambda: nc.tensor.matmul(psY[yi][:], lhsT=LC1[:, g, :], rhs=ZC[z][:], start=(g == 0), stop=False),
                      r=[par_b, ZC_b[z]], w=[psY_b[yi]], sig=False)
                kb.op(kb.pe, lambda: nc.tensor.matmul(psY[yi][:], lhsT=LC2[:, g, :], rhs=ZS[z][:], start=False, stop=(g == 7)),
                      r=[par_b, ZS_b[z]], w=[psY_b[yi]], sig=True)
                if g == 7:
                    kb.op(kb.dve, lambda: V.scalar_tensor_tensor(out=ysb[yi][:], in0=uT[:, tok], scalar=d_t[:, 0:1], in1=psY[yi][:],
                                                                 op0=ALU.mult, op1=ALU.add),
                          r=[uT_b[tt], d_tb, psY_b[yi]], w=[ysb_b[yi]])
                    kb.op(kb.act, lambda: nc.scalar.activation(out=g1[yi][:], in_=ysb[yi][:], func=AF.Square),
                          r=[ysb_b[yi]], w=[g1_b[yi]])
                    kb.op(kb.act, lambda: nc.scalar.activation(out=g1[yi][:], in_=g1[yi][:], func=AF.Identity, scale=0.044715,
                                                               bias=c["one"][:]), r=[g1_b[yi], c["one_b"]], w=[g1_b[yi]])
                    kb.op(kb.pool, lambda: nc.gpsimd.tensor_tensor(out=g1[yi][:], in0=g1[yi][:], in1=ysb[yi][:], op=ALU.mult),
                          r=[ysb_b[yi]], w=[g1_b[yi]])
                    kb.op(kb.act, lambda: nc.scalar.activation(out=g1[yi][:], in_=g1[yi][:], func=AF.Sigmoid,
                                                               scale=2.0 * math.sqrt(2.0 / PI)), r=[g1_b[yi]], w=[g1_b[yi]])
                    kb.op(kb.pool, lambda: nc.gpsimd.tensor_tensor(out=gout[yi][:], in0=g1[yi][:], in1=ysb[yi][:], op=ALU.mult),
                          r=[g1_b[yi], ysb_b[yi]], w=[gout_b[yi]])
                    kb.dma(kb.sp, mix_ap(d, "mixB", 0, 128, tt * T, T), gout[yi][:], r=[gout_b[yi]], w=[d["mixB_b"]],
                           sembuf=gout_b[yi])

            n_it = len(items)
            stage1(0)
            if n_it > 1:
                stage1(1)
            stage2(0)
            for j in range(n_it):
                if j + 2 < n_it:
                    stage1(j + 2)
                if j + 1 < n_it:
                    stage2(j + 1)
                stage3(j)
            kb.barrier()


NORM_EPS = 1e-6


def emit_mix_out(kb, c, x_d, x_db, xo_d, xo_db, mixF_d, mixF_db, col0, chunk_rows, wout_d, NT, glu=None, idx_d=None):
    nc = kb.nc
    ntt = NT // 512
    with kb.scope():
        wout, wout_b = load_cast(kb, "wout", [128, 8, 1024], wout_d)
        if glu is not None:
            wglu, wglu_b = load_cast(kb, "wglu", [128, 4, 512], glu["w"])
            bglu, bglu_b = load_plain(kb, "bglu", [128, 4], glu["b"])
        mt = [kb.sb("mt", [128, 8, 512], BF16) for _ in range(2)]
        mt_b = [kb.buf("mt") for _ in range(2)]
        if idx_d is not None:
            U32 = mybir.dt.uint32
            idx_t, idx_b = load_plain(kb, "idx", [128, 8], idx_d, U32)
            mtall = kb.sb("mtall", [128, 8, NT], BF16)
            mtall_b = kb.buf("mtall")
            m4 = mixF_d
            for k in range(8):
                kb.gather_rows(mtall[:, k, :], mtall_b, m4, mixF_db, idx_t[:, k:k + 1], idx_b)
        so = [kb.sb("so", [128, 4, 512], BF16) for _ in range(2)]
        so_b = [kb.buf("so") for _ in range(2)]
        sg = [kb.sb("sgm", [128, 512], F32) for _ in range(2)]
        sg_b = [kb.buf("sgm") for _ in range(2)]
        xt = [kb.sb("xt", [128, 8, 512], F32) for _ in range(2)]
        xt_b = [kb.buf("xt") for _ in range(2)]
        ps = [kb.ps("mo", [128, 512]) for _ in range(4)]
        ps_b = [kb.buf("mo", excl=True) for _ in range(4)]
        pc = 0
        xv = x_d.rearrange("(k p) t -> p k t", p=128)
        xov = xo_d.rearrange("(k p) t -> p k t", p=128)
        for tt in range(ntt):
            i = tt % 2
            c0 = col0 + tt * 512
            if idx_d is not None:
                kb.op(kb.pool, lambda: nc.gpsimd.tensor_copy(out=mt[i][:], in_=mtall[:, :, tt * 512:(tt + 1) * 512]),
                      r=[mtall_b], w=[mt_b[i]])
            else:
                for k in range(8):
                    r0 = chunk_rows[k]
                    kb.dma(kb.sp, mt[i][:, k, :], mixF_d[r0:r0 + 128, c0:c0 + 512], r=[mixF_db], w=[mt_b[i]])
            kb.dma(kb.sp, xt[i][:], xv[:, :, tt * 512:(tt + 1) * 512], r=[x_db], w=[xt_b[i]])
            if glu is not None:
                for j in range(4):
                    p_, pb = ps[pc % 4], ps_b[pc % 4]
                    si = pc % 2
                    pc += 1
                    for k in range(4):
                        kb.op(kb.pe, lambda: nc.tensor.matmul(p_[:], lhsT=wglu[:, k, j * 128:(j + 1) * 128], rhs=mt[i][:, k, :],
                                                              start=(k == 0), stop=(k == 3)),
                              r=[wglu_b, mt_b[i]], w=[pb], sig=(k == 3))
                    kb.op(kb.act, lambda: nc.scalar.activation(out=sg[si][:], in_=p_[:], func=AF.Sigmoid, bias=bglu[:, j:j + 1]),
                          r=[pb, bglu_b], w=[sg_b[si]])
                    kb.op(kb.dve, lambda: nc.vector.tensor_tensor(out=so[i][:, j, :], in0=mt[i][:, j, :], in1=sg[si][:], op=ALU.mult),
                          r=[sg_b[si], mt_b[i]], w=[so_b[i]])
            for m in range(8):
                p_, pb = ps[pc % 4], ps_b[pc % 4]
                pc += 1
                for k in range(8):
                    if glu is not None and k < 4:
                        rhs, rb = so[i][:, k, :], so_b[i]
                    else:
                        rhs, rb = mt[i][:, k, :], mt_b[i]
                    kb.op(kb.pe, lambda: nc.tensor.matmul(p_[:], lhsT=wout[:, k, m * 128:(m + 1) * 128], rhs=rhs,
                                                          start=(k == 0), stop=(k == 7)),
                          r=[wout_b, rb], w=[pb], sig=(k == 7))
                kb.op(kb.dve, lambda: nc.vector.tensor_tensor(out=xt[i][:, m, :], in0=p_[:], in1=xt[i][:, m, :], op=ALU.add),
                      r=[pb], w=[xt_b[i]])
            kb.dma(kb.sp, xov[:, :, tt * 512:(tt + 1) * 512], xt[i][:], r=[xt_b[i]], w=[xo_db], sembuf=xt_b[i])
        kb.barrier()


def emit_launchD(kb, c, d, lam_init):
    nc = kb.nc
    V_ = nc.vector
    scale = 64 ** -0.5
    with kb.scope():
        QT = [kb.sb("QT", [128, 2 * S], BF16) for _ in range(2)]
        KT = [kb.sb("KT", [128, S], BF16) for _ in range(2)]
        QT_b = [[kb.buf("QT") for _ in range(NQT)] for _ in range(2)]
        KT_b = [[kb.buf("KT") for _ in range(NQT)] for _ in range(2)]
        for h_ in range(2):
            kb.op(kb.pool, lambda: nc.gpsimd.memset(QT[h_][:], 0.0), w=QT_b[h_])
        Vt = kb.sb("V", [128, S // 128, 2, 129], BF16)
        V_b = [kb.buf("V") for _ in range(NQT)]
        kb.op(kb.pool, lambda: nc.gpsimd.memset(Vt[:], 1.0), w=V_b)
        neglam = kb.sb("neglam", [128, 1], F32)
        gsub = kb.sb("gsub", [128, 128], F32)
        lam_b = kb.buf("lam")
        with kb.scope():
            lq, lq_b = load_plain(kb, "lq", [128, 4, 64], d["lam_in"])
            pr = kb.sb("pr", [128, 2, 64], F32)
            sm = kb.sb("sm", [128, 2], F32)
            kb.op(kb.dve, lambda: V_.tensor_tensor(out=pr[:, 0, :], in0=lq[:, 0, :], in1=lq[:, 1, :], op=ALU.mult), r=[lq_b], w=[lam_b])
            kb.op(kb.dve, lambda: V_.tensor_tensor(out=pr[:, 1, :], in0=lq[:, 2, :], in1=lq[:, 3, :], op=ALU.mult), r=[lq_b], w=[lam_b])
            kb.op(kb.dve, lambda: V_.tensor_reduce(out=sm[:], in_=pr[:], axis=AX.X, op=ALU.add), r=[lam_b], w=[lam_b])
            kb.op(kb.act, lambda: nc.scalar.activation(out=sm[:], in_=sm[:], func=AF.Exp), r=[lam_b], w=[lam_b])
            kb.op(kb.dve, lambda: V_.tensor_tensor(out=neglam[:], in0=sm[:, 1:2], in1=sm[:, 0:1], op=ALU.subtract), r=[lam_b], w=[lam_b])
            kb.op(kb.dve, lambda: V_.tensor_scalar(out=neglam[:], in0=neglam[:], scalar1=-lam_init, scalar2=None, op0=ALU.add),
                  r=[lam_b], w=[lam_b])
            kb.dma(kb.sp, gsub[:], d["subln"], w=[lam_b])
            kb.op(kb.dve, lambda: V_.tensor_scalar(out=gsub[:], in0=gsub[:], scalar1=1.0 - lam_init, scalar2=None, op0=ALU.mult),
                  r=[lam_b], w=[lam_b])
            kb.barrier()
        with kb.scope():
            wq, wq_b = load_cast(kb, "wq", [128, 8, 256], d["w_q"])
            wqs, wqs_b = load_cast(kb, "wqs", [128, 8, 256], d["w_q_sw"])
            wk, wk_b = load_cast(kb, "wk", [128, 8, 256], d["w_k"])
            wks, wks_b = load_cast(kb, "wks", [128, 8, 256], d["w_k_sw"])
            wv, wv_b = load_cast(kb, "wv", [128, 8, 256], d["w_v"])
            invc, invc_b = load_plain(kb, "invc", [128, 1], d["rope_inv"])
            sgnc, sgnc_b = load_plain(kb, "sgnc", [128, 1], d["rope_sgn"])
            scr, scr_b = rope_scratch(kb)
            COS = kb.sb("COS", [128, 512], F32)
            SINS = kb.sb("SINS", [128, 512], F32)
            tb = {"cos": kb.buf("COS"), "sin": kb.buf("SINS")}
            hin = [kb.sb("hin", [128, 8, 512], BF16) for _ in range(2)]
            hin_b = [kb.buf("hin") for _ in range(2)]
            m1 = [kb.sb("m1", [128, 512], F32) for _ in range(2)]
            m1_b = [kb.buf("m1") for _ in range(2)]
            m2 = [kb.sb("m2", [128, 512], F32) for _ in range(2)]
            m2_b = [kb.buf("m2") for _ in range(2)]
            PS = [kb.ps("PS", [128, 512]) for _ in range(6)]
            PS_b = [kb.buf("PS", excl=True) for _ in range(6)]
            pctr = [0]
            mctr = [0]

            def nextps():
                i = pctr[0] % 6
                pctr[0] += 1
                return PS[i], PS_b[i]

            for tt in range(NQT):
                hi = tt % 2
                tok = slice(tt * 512, (tt + 1) * 512)
                rr, cc0 = tt // 4, (tt % 4) * 512
                load_hin(kb, d, hin[hi], hin_b[hi], tt)
                emit_rope_tables(kb, c, d["pos"], d["pos_b"], tt, invc, invc_b, sgnc, sgnc_b, COS, SINS, tb, scr, scr_b)
                for h in range(2):
                    for (w_, w_b, ws_, ws_b, dst, dst_b) in [(wq, wq_b, wqs, wqs_b, None, QT_b), (wk, wk_b, wks, wks_b, KT, KT_b)]:
                        ps_a, psb_a = nextps()
                        for k in range(8):
                            kb.op(kb.pe, lambda: nc.tensor.matmul(ps_a[:], lhsT=w_[:, k, h * 128:(h + 1) * 128], rhs=hin[hi][:, k, :],
                                                                  start=(k == 0), stop=(k == 7)),
                                  r=[w_b, hin_b[hi]], w=[psb_a], sig=(k == 7))
                        ps_s, psb_s = nextps()
                        for k in range(8):
                            kb.op(kb.pe, lambda: nc.tensor.matmul(ps_s[:], lhsT=ws_[:, k, h * 128:(h + 1) * 128], rhs=hin[hi][:, k, :],
                                                                  start=(k == 0), stop=(k == 7)),
                                  r=[ws_b, hin_b[hi]], w=[psb_s], sig=(k == 7))
                        mi = mctr[0] % 2
                        mctr[0] += 1
                        kb.op(kb.dve, lambda: V_.tensor_tensor(out=m1[mi][:], in0=ps_a[:], in1=COS[:], op=ALU.mult),
                              r=[psb_a, tb["cos"]], w=[m1_b[mi]])
                        kb.op(kb.dve, lambda: V_.tensor_tensor(out=m2[mi][:], in0=ps_s[:], in1=SINS[:], op=ALU.mult),
                              r=[psb_s, tb["sin"]], w=[m2_b[mi]])
                        if dst is None:
                            for mm in range(2):
                                rws = slice(mm * 64, (mm + 1) * 64)
                                qv = QT[h][rws, tt * 1024:(tt + 1) * 1024].rearrange("p (t m q) -> p t m q", t=2, m=2)[:, :, mm, :]
                                kb.op(kb.pool, lambda: nc.gpsimd.tensor_tensor(
                                    out=qv, in0=m1[mi][rws, :].rearrange("p (t q) -> p t q", t=2),
                                    in1=m2[mi][rws, :].rearrange("p (t q) -> p t q", t=2), op=ALU.add),
                                    r=[m1_b[mi], m2_b[mi]], w=[dst_b[h][tt]])
                        else:
                            kb.op(kb.pool, lambda: nc.gpsimd.tensor_tensor(out=dst[h][:, tok], in0=m1[mi][:], in1=m2[mi][:], op=ALU.add),
                                  r=[m1_b[mi], m2_b[mi]], w=[dst_b[h][tt]])
                for s4 in range(4):
                    ps, psb = nextps()
                    for k in range(8):
                        kb.op(kb.pe, lambda: nc.tensor.matmul(ps[:, 0:256], lhsT=hin[hi][:, k, s4 * 128:(s4 + 1) * 128], rhs=wv[:, k, :],
                                                              start=(k == 0), stop=(k == 7)),
                              r=[wv_b, hin_b[hi]], w=[psb], sig=(k == 7))
                    kb.op(kb.act, lambda: nc.scalar.copy(out=Vt[:, tt * 4 + s4, :, 0:128],
                                                         in_=ps[:, 0:256].rearrange("p (h e) -> p h e", h=2)),
                          r=[psb], w=[V_b[tt]])
            kb.barrier()
        with kb.scope():
            on = [kb.sb("on", [128, 2, 256], BF16) for _ in range(2)]
            on_b = [kb.buf("on") for _ in range(2)]
            rec = kb.sb("rec", [128, 4], F32)
            t1 = kb.sb("t1", [128, 128], F32)
            o_ = kb.sb("o_", [128, 128], F32)
            junk = kb.sb("junk", [128, 128], F32)
            ssq = kb.sb("ssq", [128, 1], F32)
            fb = kb.buf("fin")
            psT = kb.ps("psT", [128, 1024], BF16)
            psT_b = kb.buf("psT", excl=True)
            oT = [kb.sb("oT", [128, 2, 256], BF16) for _ in range(2)]
            oT_b = [kb.buf("oT") for _ in range(2)]

            def finalize(h, qt, getO):
                i = qt % 2
                for s in range(2):
                    O1, O1b = getO(0, s)
                    O2, O2b = getO(1, s)
                    kb.op(kb.dve, lambda: V_.reciprocal(out=rec[:, 0:1], in_=O1[:, 128:129]), r=[O1b], w=[fb])
                    kb.op(kb.dve, lambda: V_.reciprocal(out=rec[:, 1:2], in_=O2[:, 128:129]), r=[O2b], w=[fb])
                    kb.op(kb.dve, lambda: V_.tensor_tensor(out=rec[:, 1:2], in0=rec[:, 1:2], in1=neglam[:], op=ALU.mult),
                          r=[fb, lam_b], w=[fb])
                    kb.op(kb.dve, lambda: V_.tensor_scalar(out=t1[:], in0=O1[:, 0:128], scalar1=rec[:, 0:1], scalar2=None,
                                                           op0=ALU.mult), r=[O1b, fb], w=[fb])
                    kb.op(kb.dve, lambda: V_.scalar_tensor_tensor(out=o_[:], in0=O2[:, 0:128], scalar=rec[:, 1:2], in1=t1[:],
                                                                  op0=ALU.mult, op1=ALU.add), r=[O2b, fb], w=[fb])
                    kb.op(kb.dve, lambda: V_.scalar_tensor_tensor(out=junk[:], in0=o_[:], scalar=1.0, in1=o_[:],
                                                                  op0=ALU.mult, op1=ALU.mult, accum_out=ssq[:]),
                          r=[fb], w=[fb])
                    kb.op(kb.act, lambda: nc.scalar.activation(out=ssq[:], in_=ssq[:], func=AF.Ln, bias=c["eps"][:],
                                                               scale=1.0 / 128), r=[fb, c["eps_b"]], w=[fb])
                    kb.op(kb.act, lambda: nc.scalar.activation(out=ssq[:], in_=ssq[:], func=AF.Exp, scale=-0.5),
                          r=[fb], w=[fb])
                    kb.op(kb.dve, lambda: V_.scalar_tensor_tensor(out=on[i][:, s, h * 128:(h + 1) * 128], in0=o_[:],
                                                                  scalar=ssq[:, 0:1], in1=gsub[:], op0=ALU.mult, op1=ALU.mult),
                          r=[fb, lam_b], w=[on_b[i]])
                if h == 1:
                    for h2 in range(2):
                        for s in range(2):
                            kb.op(kb.pe, lambda: nc.tensor.transpose(psT[:, (h2 * 2 + s) * 128:(h2 * 2 + s + 1) * 128],
                                                                     on[i][:, s, h2 * 128:(h2 + 1) * 128], c["ident_bf"][:]),
                                  r=[on_b[i], c["ident_b"]], w=[psT_b])
                    kb.op(kb.act, lambda: nc.scalar.copy(out=oT[i][:].rearrange("p h q -> p (h q)"), in_=psT[:, 0:512]),
                          r=[psT_b], w=[oT_b[i]])
                    for h2 in range(2):
                        kb.dma(kb.sp, mix_ap(d, "mixD", h2 * 128, (h2 + 1) * 128, qt * 256, 256), oT[i][:, h2, :],
                               r=[oT_b[i]], w=[d["mixD_b"]], sembuf=oT_b[i])
                    if d.get("chunk_done") is not None and (qt + 1) % 8 == 0:
                        d["chunk_done"](qt // 8)

            emit_attention(kb, c, 2, 128, 2, 128, QT, QT_b, KT, KT_b, Vt, V_b, scale, finalize, QTILE=256, nob=2, nst=3, npt=4)


def kp(w, k):
    n = w.shape[1]
    return np.ascontiguousarray(w.reshape(k, 128, n).transpose(1, 0, 2))

def swap_halves(w, start, half):
    w = w.copy()
    a = w[:, start:start + half].copy()
    w[:, start:start + half] = w[:, start + half:start + 2 * half]
    w[:, start + half:start + 2 * half] = a
    return w

def mla_rope_consts():
    half = 16
    inv = (10000.0 ** (-np.arange(half, dtype=np.float32) * 2.0 / 32)).astype(np.float32)
    invc = np.zeros((128, 1), np.float32); sgn = np.zeros((128, 1), np.float32)
    invc[64:80, 0] = inv; invc[80:96, 0] = inv
    sgn[64:80, 0] = -1.0; sgn[80:96, 0] = 1.0
    return invc, sgn

def prep_launchB(inp, hp):
    w_in = inp["ev_w_in"][0]
    u = w_in[:, 128 * hp:128 * hp + 128]
    cq = w_in[:, 512:896]; ckv = w_in[:, 896:1152]; kr = w_in[:, 1152:1184]
    d = {}
    d["w_inA"] = kp(np.concatenate([u, cq, ckv, kr], 1), 8)
    d["w_kr"] = kp(kr, 8)
    d["w_krsw"] = kp(swap_halves(kr, 0, 16), 8)
    wuq = inp["mla_w_uq"][0]
    my = wuq[:, 192 * hp:192 * hp + 192]
    mys = swap_halves(swap_halves(my, 64, 16), 96 + 64, 16)
    d["w_uq"] = kp(my, 3); d["w_uq_sw"] = kp(mys, 3)
    wukv = inp["mla_w_ukv"][0]
    h0, h1 = 2 * hp, 2 * hp + 1
    d["w_ukv_k"] = kp(np.concatenate([wukv[:, 128 * h0:128 * h0 + 64], wukv[:, 128 * h1:128 * h1 + 64]], 1), 2)
    d["w_ukv_v"] = kp(np.concatenate([wukv[:, 128 * h0 + 64:128 * h0 + 128], wukv[:, 128 * h1 + 64:128 * h1 + 128]], 1), 2)
    g = np.concatenate([inp["mla_q_norm"][0], inp["mla_kv_norm"][0]])
    d["lat_g"] = np.ascontiguousarray(g.reshape(5, 128).T)
    d["rope_inv"], d["rope_sgn"] = mla_rope_consts()
    return d

def consts():
    tri = (np.arange(128)[:, None] <= np.arange(128)[None, :]).astype(np.float32)
    return {"tri": tri, "ident": np.eye(128, dtype=np.float32)}

def prep_s5(inp, hp):
    gs = slice(8 * hp, 8 * hp + 8)
    d = {}
    def dup(a):
        return np.ascontiguousarray(np.concatenate([a.T, a.T], 0))
    d["s5_lr"] = dup(inp["s5_lambda_re"][0, gs])
    d["s5_li"] = dup(inp["s5_lambda_im"][0, gs])
    d["s5_logdt"] = np.ascontiguousarray(np.broadcast_to(inp["s5_log_dt"][0, gs][None, :], (128, 8))).astype(np.float32)
    def dupb(b):
        x = b.transpose(1, 0, 2)
        return np.ascontiguousarray(np.concatenate([x, x], 0))
    d["s5_bre"] = dupb(inp["s5_b_re"][0, gs]); d["s5_bim"] = dupb(inp["s5_b_im"][0, gs])
    cre = inp["s5_c_re"][0, gs].reshape(128, 64); cim = inp["s5_c_im"][0, gs].reshape(128, 64)
    d["s5_c1src"] = np.ascontiguousarray(np.concatenate([cre, cim], 1))
    d["s5_c2src"] = np.ascontiguousarray(np.concatenate([cim, cre], 1))
    d["s5_d"] = np.ascontiguousarray(inp["s5_d"][0, gs].reshape(128, 1))
    rm = np.zeros((128, 8), np.float32)
    for g in range(8): rm[16 * g:16 * g + 16, g] = 1
    d["rowmask"] = rm
    J = np.zeros((128, 128), np.float32)
    for p in range(64):
        J[p, 64 + p] = 1.0; J[64 + p, p] = -1.0
    d["Jmat"] = J
    return d

def diff_rope_consts():
    half = 8
    inv = (500000.0 ** (-np.arange(half, dtype=np.float32) * 2.0 / 16)).astype(np.float32)
    invc = np.zeros((128, 1), np.float32); sgn = np.zeros((128, 1), np.float32)
    for m in range(2):
        invc[m * 64:m * 64 + 8, 0] = inv; invc[m * 64 + 8:m * 64 + 16, 0] = inv
        sgn[m * 64:m * 64 + 8, 0] = -1.0; sgn[m * 64 + 8:m * 64 + 16, 0] = 1.0
    return invc, sgn

def prep_launchD(inp, hp):
    w = inp["od_w_in"][0]
    q = w[:, 256 * hp:256 * hp + 256]; k = w[:, 1024 + 256 * hp:1024 + 256 * hp + 256]; v = w[:, 2048 + 256 * hp:2048 + 256 * hp + 256]
    def sw(a):
        a = a.copy()
        for b0 in range(0, 256, 64):
            a = swap_halves(a, b0, 8)
        return a
    d = {"w_q": kp(q, 8), "w_q_sw": kp(sw(q), 8), "w_k": kp(k, 8), "w_k_sw": kp(sw(k), 8), "w_v": kp(v, 8)}
    d["rope_inv"], d["rope_sgn"] = diff_rope_consts()
    lam = np.stack([inp["diff_lq1"][0], inp["diff_lk1"][0], inp["diff_lq2"][0], inp["diff_lk2"][0]])
    d["lam_in"] = np.ascontiguousarray(np.broadcast_to(lam[None], (128, 4, 64))).astype(np.float32)
    d["subln"] = np.ascontiguousarray(np.broadcast_to(inp["diff_subln"][0][None], (128, 128))).astype(np.float32)
    return d


def arrange_wgu(w):
    g = w[:, :2816].reshape(8, 128, 11, 256)
    u = w[:, 2816:].reshape(8, 128, 11, 256)
    gu = np.concatenate([g, u], axis=-1)
    return np.ascontiguousarray(gu.transpose(2, 1, 0, 3))

def arrange_wd(w):
    return np.ascontiguousarray(w.reshape(22, 128, 4, 256).transpose(2, 1, 0, 3))


import ml_dtypes as _mld
_BF = _mld.bfloat16
NT = 2048
LAM_INIT = 0.8 - 0.6 * math.exp(-0.3 * 1)


def _common_consts(kb, d):
    nc = kb.nc
    c = emit_consts(kb)
    c["eps"] = kb.sb("eps", [128, 1], F32)
    c["eps_b"] = kb.buf("eps")
    kb.op(kb.dve, lambda: nc.vector.memset(c["eps"][:], NORM_EPS), w=[c["eps_b"]])
    if "tri" in d:
        c["tri"], c["tri_b"] = load_cast(kb, "tri", [128, 128], d["tri"])
        c["ident_bf"], c["ident_b"] = load_cast(kb, "ident", [128, 128], d["ident"])
    return c


def _din(kb, d, name, shape, dt=F32):
    d[name] = kb.nc.dram_tensor(name, shape, dt, kind="ExternalInput").ap()
    d[name + "_b"] = kb.buf(name)


def _dout(kb, d, name, shape, dt=F32):
    d[name] = kb.nc.dram_tensor(name, shape, dt, kind="ExternalOutput").ap()
    d[name + "_b"] = kb.buf(name)


def _ffn_inputs(kb, d, sfx):
    _din(kb, d, "g" + sfx, [128, 8])
    _din(kb, d, "wgu" + sfx, [11, 128, 8, 512])
    _din(kb, d, "wd" + sfx, [4, 128, 22, 256])


def emit_norm_out(kb, c, x_d, x_db, g_d, out_d, out_db, dt):
    with kb.scope():
        hT = kb.sb("hTo", [128, 8, NT], dt)
        hT_b = [kb.buf("hTo") for _ in range(NT // 512)]
        emit_norm(kb, c, x_d, x_db, g_d, hT, hT_b, NT)
        ov = out_d.rearrange("(k p) t -> p k t", p=128)
        for tt in range(NT // 512):
            kb.dma(kb.sp, ov[:, :, tt * 512:(tt + 1) * 512], hT[:, :, tt * 512:(tt + 1) * 512], r=[hT_b[tt]], w=[out_db],
                   sembuf=hT_b[tt])
        kb.barrier()


_B_IN = [("w_inA", [128, 8, 800]), ("w_kr", [128, 8, 32]), ("w_krsw", [128, 8, 32]), ("w_uq", [128, 3, 192]),
         ("w_uq_sw", [128, 3, 192]), ("w_ukv_k", [128, 2, 128]), ("w_ukv_v", [128, 2, 128]), ("lat_g", [128, 5]),
         ("rope_inv", [128, 1]), ("rope_sgn", [128, 1]), ("tri", [128, 128]), ("ident", [128, 128]),
         ("s5_lr", [128, 8]), ("s5_li", [128, 8]), ("s5_logdt", [128, 8]), ("s5_bre", [128, 8, 16]), ("s5_bim", [128, 8, 16]),
         ("s5_c1src", [128, 128]), ("s5_c2src", [128, 128]), ("s5_d", [128, 1]), ("rowmask", [128, 8]), ("Jmat", [128, 128])]


_D_IN = [("w_q", [128, 8, 256]), ("w_q_sw", [128, 8, 256]), ("w_k", [128, 8, 256]), ("w_k_sw", [128, 8, 256]),
         ("w_v", [128, 8, 256]), ("rope_inv", [128, 1]), ("rope_sgn", [128, 1]), ("lam_in", [128, 4, 64]), ("subln", [128, 128])]
GROUPS = [[0, 1, 2, 3], [4, 5, 6, 7]]


def _dint(kb, d, name, shape, dt=F32):
    d[name] = kb.nc.dram_tensor(name, shape, dt, kind="Internal").ap()
    d[name + "_b"] = kb.buf(name)


def build_fused():
    kb = KB(); d = {}
    U32 = mybir.dt.uint32
    _din(kb, d, "xT", [1024, NT]); _din(kb, d, "pos", [S], I32)
    for sfx in "0123":
        _ffn_inputs(kb, d, sfx)
    for n in ["g_ev", "g_od", "g_fin"]:
        _din(kb, d, n, [128, 8])
    for n, sh in _B_IN:
        _din(kb, d, n, sh)
    dD = {}
    for n, sh in _D_IN:
        _din(kb, d, "D_" + n, sh)
        dD[n] = d["D_" + n]
    _din(kb, d, "wout0", [128, 8, 1024]); _din(kb, d, "wout1", [128, 8, 1024])
    _din(kb, d, "wglu", [128, 4, 512]); _din(kb, d, "bglu", [128, 4])
    _din(kb, d, "idxC", [128, 8], U32); _din(kb, d, "idxE", [128, 8], U32)
    for n in ["x1T", "x2T", "x3T", "x4T", "x5T", "x6T"]:
        _dint(kb, d, n, [1024, NT])
    _dint(kb, d, "hT", [1024, NT], BF16); _dint(kb, d, "h2T", [1024, NT], BF16)
    for q in range(4):
        _dint(kb, d, f"hTg{q}", [1024, NT], BF16); _dint(kb, d, f"h2Tg{q}", [1024, NT], BF16)
        _dint(kb, d, f"mixB{q}", [256, NT], BF16); _dint(kb, d, f"mixD{q}", [256, NT], BF16)
    _dint(kb, d, "mixF", [4096, NT], BF16); _dint(kb, d, "mixF2", [4096, NT], BF16)
    d["mixB_b"] = kb.buf("mixB"); d["mixD_b"] = kb.buf("mixD")
    d["mixB_chunks"] = [d[f"mixB{q}"] for q in range(4)]
    _dout(kb, d, "outT", [1024, NT])
    c = _common_consts(kb, d)
    emit_ffn(kb, c, d["xT"], d["xT_b"], d["x1T"], d["x1T_b"], d["g0"], d["wgu0"], d["wd0"], NT)
    emit_norm_out(kb, c, d["x1T"], d["x1T_b"], d["g_ev"], d["hT"], d["hT_b"], BF16)
    for q in range(4):
        kb.allgather(d["hT"][256 * q:256 * (q + 1), :], d["hT_b"], d[f"hTg{q}"], d[f"hTg{q}_b"], GROUPS)
    d["hTg"] = [d[f"hTg{q}"] for q in range(4)]
    d["hTg_b"] = [d[f"hTg{q}_b"] for q in range(4)]
    with kb.scope():
        uT = kb.sb("uT", [128, S], BF16)
        d["uT"] = uT
        d["uT_b"] = [kb.buf("uT") for _ in range(NQT)]
        d["mid_hook"] = lambda: emit_s5(kb, c, d, uT, d["uT_b"])
        d["chunk_done"] = lambda j: kb.allgather(d[f"mixB{j}"], d["mixB_b"], d["mixF"][1024 * j:1024 * (j + 1), :],
                                                 d["mixF_b"], GROUPS)
        emit_launchB_mla(kb, c, d)
    emit_mix_out(kb, c, d["x1T"], d["x1T_b"], d["x2T"], d["x2T_b"], d["mixF"], d["mixF_b"], 0, None, d["wout0"], NT,
                 glu={"w": d["wglu"], "b": d["bglu"]}, idx_d=d["idxC"])
    emit_ffn(kb, c, d["x2T"], d["x2T_b"], d["x3T"], d["x3T_b"], d["g1"], d["wgu1"], d["wd1"], NT)
    emit_ffn(kb, c, d["x3T"], d["x3T_b"], d["x4T"], d["x4T_b"], d["g2"], d["wgu2"], d["wd2"], NT)
    emit_norm_out(kb, c, d["x4T"], d["x4T_b"], d["g_od"], d["h2T"], d["h2T_b"], BF16)
    for q in range(4):
        kb.allgather(d["h2T"][256 * q:256 * (q + 1), :], d["h2T_b"], d[f"h2Tg{q}"], d[f"h2Tg{q}_b"], GROUPS)
    dD.update({"hTg": [d[f"h2Tg{q}"] for q in range(4)], "hTg_b": [d[f"h2Tg{q}_b"] for q in range(4)],
               "pos": d["pos"], "pos_b": d["pos_b"], "mixD_chunks": [d[f"mixD{q}"] for q in range(4)], "mixD_b": d["mixD_b"]})
    dD["chunk_done"] = lambda j: kb.allgather(d[f"mixD{j}"], d["mixD_b"], d["mixF2"][1024 * j:1024 * (j + 1), :],
                                              d["mixF2_b"], GROUPS)
    emit_launchD(kb, c, dD, LAM_INIT)
    emit_mix_out(kb, c, d["x4T"], d["x4T_b"], d["x5T"], d["x5T_b"], d["mixF2"], d["mixF2_b"], 0, None, d["wout1"], NT,
                 idx_d=d["idxE"])
    emit_ffn(kb, c, d["x5T"], d["x5T_b"], d["x6T"], d["x6T_b"], d["g3"], d["wgu3"], d["wd3"], NT)
    emit_norm_out(kb, c, d["x6T"], d["x6T_b"], d["g_fin"], d["outT"], d["outT_b"], F32)
    kb.finish([])
    return kb.nc


def _gT(g):
    return np.ascontiguousarray(np.asarray(g, np.float32).reshape(8, 128).T)


def _ffn_host(inp, l, j, sfx):
    return {"g" + sfx: _gT(inp["ffn_norm"][l, j]), "wgu" + sfx: arrange_wgu(inp["ffn_w_gu"][l, j]),
            "wd" + sfx: arrange_wd(inp["ffn_w_down"][l, j])}


def _run(nc, in_maps):
    res = run_bass_kernel_spmd(nc, in_maps, core_ids=list(range(8)))
    return [{k: np.asarray(v) for k, v in r.items()} for r in res.results]


def _idx(rows, r):
    a = np.zeros((128, 8), np.uint32)
    for k in range(8):
        a[:, k] = r * 1024 + rows[k] + np.arange(128)
    return a


def kernel(**inputs):
    inp = {k: np.asarray(v) for k, v in inputs.items()}
    x = inp["x"].astype(np.float32)
    pos = inp["positions"].astype(np.int32)
    cs = consts()
    shared = {}
    for (l, j, sfx) in [(0, 0, "0"), (0, 1, "1"), (1, 0, "2"), (1, 1, "3")]:
        shared.update(_ffn_host(inp, l, j, sfx))
    shared["g_ev"] = _gT(inp["ev_norm"][0]); shared["g_od"] = _gT(inp["od_norm"][0]); shared["g_fin"] = _gT(inp["final_norm"])
    shared["wout0"] = kp(inp["ev_w_out"][0], 8); shared["wout1"] = kp(inp["od_w_out"][0], 8)
    shared["wglu"] = kp(inp["s5_w_glu"][0], 4)
    shared["bglu"] = np.ascontiguousarray(inp["s5_b_glu"][0].reshape(4, 128).T)
    shared.update(cs)
    rows0 = [256 * k for k in range(4)] + [256 * k + 128 for k in range(4)]
    rows1 = [128 * k for k in range(8)]
    maps = []
    for ci in range(8):
        b, r = ci // 4, ci % 4
        m = dict(shared)
        m["xT"] = np.ascontiguousarray(x[b, NT * r:NT * (r + 1)].T)
        m["pos"] = np.ascontiguousarray(pos[b])
        m.update(prep_launchB(inp, r))
        m.update(prep_s5(inp, r))
        for k_, v_ in prep_launchD(inp, r).items():
            m["D_" + k_] = v_
        m["idxC"] = _idx(rows0, r); m["idxE"] = _idx(rows1, r)
        maps.append(m)
    res = _run(build_fused(), maps)
    out = np.empty((2, S, 1024), np.float32)
    for ci in range(8):
        b, r = ci // 4, ci % 4
        out[b, NT * r:NT * (r + 1)] = res[ci]["outT"].T
    return out
```
